# Optimizing a Trainium2 kernel written in Bass

```python
import math
import jax
import jax.numpy as jnp
from jax import lax
import numpy as np

D_MODEL = 1024
BATCH = 1
SEQ = 16384
DEPTH = 4
DEC_BATCH = 16
DEC_SEQ = 4096
PAST_LEN = 128

HC = D_MODEL // 2
HYENA_ORDER = 2
FILTER_EMB = 33
FILTER_BANDS = (FILTER_EMB - 1) // 2
FILTER_HIDDEN = 64
DECAY_TARGET = 1e-2
FAST_DECAY_PCT = 0.3
SLOW_DECAY_PCT = 1.5
DECAY_SHIFT = 0.05
SHORT_CONV = 3
B_HEAD_DIM = 64
B_HEADS = (D_MODEL // 2) // B_HEAD_DIM
B_KV_HEADS = 2
B_GROUP = B_HEADS // B_KV_HEADS
B_WIDTH = B_HEADS * B_HEAD_DIM
B_KV_WIDTH = B_KV_HEADS * B_HEAD_DIM
B_WINDOW = 128
B_BLOCK = 128
C_PATTERNS = ((128, 1), (512, 4), (2048, 16))
N_PAT = len(C_PATTERNS)
C_HEADS = 8
C_HEAD_DIM = D_MODEL // C_HEADS
C_WIDTH = C_HEADS * C_HEAD_DIM
C_BLOCK = 64
ROPE_THETA = 500000.0
ROPE_FRACTION = 4
NORM_EPS = 1e-6
NEG_INF = -1e30
EVEN_IN = 4 * HC + 2 * B_WIDTH + 2 * B_KV_WIDTH
EVEN_INNER = HC + B_WIDTH
ODD_IN = N_PAT * 3 * C_WIDTH + C_WIDTH
N_EVEN = (DEPTH + 1) // 2
N_ODD = DEPTH // 2

kernel_name = "hybrid_hyena_swa_dilated_encoder"


def rmsnorm(x, g):
    xf = x.astype(jnp.float32)
    y = xf * jax.lax.rsqrt(jnp.mean(xf * xf, axis=-1, keepdims=True) + NORM_EPS)
    return (y * g.astype(jnp.float32)).astype(x.dtype)


def rope(x, pos):
    hd = x.shape[-1]
    rot = hd // ROPE_FRACTION
    half = rot // 2
    inv = jnp.power(ROPE_THETA, -2.0 * jnp.arange(half, dtype=jnp.float32) / rot)
    ang = pos.astype(jnp.float32)[:, None] * inv[None, :]
    cos = jnp.cos(ang)[:, None, :]
    sin = jnp.sin(ang)[:, None, :]
    xf = x.astype(jnp.float32)
    x1, x2 = xf[..., :half], xf[..., half:rot]
    out = jnp.concatenate([x1 * cos - x2 * sin, x2 * cos + x1 * sin, xf[..., rot:]], axis=-1)
    return out.astype(x.dtype)


def short_conv(u, w, b):
    up = jnp.pad(u, ((0, 0), (1, 1), (0, 0)))
    return up[:, :-2] * w[0] + up[:, 1:-1] * w[1] + up[:, 2:] * w[2] + b


def hyena_filter_spectrum(L, w1, b1, f1, w2, b2, f2, w3):
    f32 = jnp.float32
    t = jnp.linspace(0.0, 1.0, L, dtype=f32)[:, None]
    wpos = 2.0 * math.pi * jnp.arange(L, dtype=f32)[:, None] / L
    bands = jnp.linspace(1e-4, FILTER_BANDS - 1, FILTER_BANDS, dtype=f32)[None, :]
    feats = jnp.concatenate([t, jnp.cos(bands * wpos), -jnp.sin(bands * wpos)], axis=-1)
    h = jnp.sin(f1.astype(f32) * (feats @ w1.astype(f32) + b1.astype(f32)))
    h = jnp.sin(f2.astype(f32) * (h @ w2.astype(f32) + b2.astype(f32)))
    h = (h @ w3.astype(f32)).reshape(L, HYENA_ORDER, 2, HC)
    max_decay = math.log(DECAY_TARGET) / FAST_DECAY_PCT
    min_decay = math.log(DECAY_TARGET) / SLOW_DECAY_PCT
    deltas = jnp.abs(jnp.linspace(min_decay, max_decay, HC, dtype=f32))
    window = jnp.exp(-t * deltas[None, :]) + DECAY_SHIFT
    h = h * window[:, None, None, :]
    fwd, bwd = h[:, :, 0], h[:, :, 1]
    filt = jnp.concatenate([fwd[:1] + bwd[:1], fwd[1:], jnp.zeros_like(fwd[:1]), bwd[1:][::-1]], axis=0)
    return jnp.fft.rfft(filt, axis=0)


def hyena_mix(u, spec, hyena_d):
    L = u.shape[1]
    x1, x2, z = jnp.split(u.astype(jnp.float32), 3, axis=-1)
    gates = (x1, x2)
    for o in range(HYENA_ORDER):
        zf = jnp.fft.rfft(z, n=2 * L, axis=1)
        conv = jnp.fft.irfft(zf * spec[None, :, o, :], n=2 * L, axis=1)[:, :L]
        z = gates[o] * (conv + hyena_d[o].astype(jnp.float32) * z)
    return z.astype(u.dtype)


def band_attention(q, k, v, radius, block, n_valid, sink=None):
    B, L, Hk, G, hd = q.shape
    nb = L // block
    qb = q.reshape(B, nb, block, Hk, G, hd)
    pad = ((0, 0), (block, block), (0, 0), (0, 0))
    kp = jnp.pad(k, pad).reshape(B, nb + 2, block, Hk, hd)
    vp = jnp.pad(v, pad).reshape(B, nb + 2, block, Hk, hd)
    kw = jnp.concatenate([kp[:, :-2], kp[:, 1:-1], kp[:, 2:]], axis=2)
    vw = jnp.concatenate([vp[:, :-2], vp[:, 1:-1], vp[:, 2:]], axis=2).astype(jnp.float32)
    qpos = jnp.arange(L).reshape(nb, block)
    kpos = jnp.arange(nb)[:, None] * block - block + jnp.arange(3 * block)[None, :]
    rel = kpos[:, None, :] - qpos[:, :, None]
    valid = (jnp.abs(rel) <= radius) & (kpos[:, None, :] >= 0) & (kpos[:, None, :] < n_valid)
    s = jnp.einsum('bnqhgd,bnkhd->bnhgqk', qb, kw, preferred_element_type=jnp.float32) * (hd ** -0.5)
    s = jnp.where(valid[None, :, None, None], s, NEG_INF)
    m = jnp.max(s, axis=-1)
    if sink is not None:
        sink_b = sink.astype(jnp.float32).reshape(1, 1, Hk, G, 1)
        m = jnp.maximum(m, sink_b)
    p = jnp.exp(s - m[..., None])
    denom = jnp.sum(p, axis=-1)
    if sink is not None:
        denom = denom + jnp.exp(sink_b - m)
    o = jnp.einsum('bnhgqk,bnkhd->bnqhgd', p, vw)
    o = o / jnp.transpose(denom, (0, 1, 4, 2, 3))[..., None]
    lse = jnp.transpose(m + jnp.log(denom), (0, 1, 4, 2, 3)).reshape(B, L, Hk, G)
    return o.reshape(B, L, Hk, G, hd).astype(q.dtype), lse


def dilated_attention(q, k, v, window, dilation):
    B, L, H, hd = q.shape
    radius = window // (2 * dilation)
    Ls = L // dilation
    Lp = -(-Ls // C_BLOCK) * C_BLOCK

    def strided(t):
        t = t.reshape(B, Ls, dilation, H, hd).swapaxes(1, 2).reshape(B * dilation, Ls, H, hd)
        return jnp.pad(t, ((0, 0), (0, Lp - Ls), (0, 0), (0, 0)))

    o, lse = band_attention(strided(q)[:, :, :, None], strided(k), strided(v), radius, C_BLOCK, Ls)
    o = o[:, :Ls, :, 0].reshape(B, dilation, Ls, H, hd).swapaxes(1, 2).reshape(B, L, H, hd)
    lse = lse[:, :Ls, :, 0].reshape(B, dilation, Ls, H).swapaxes(1, 2).reshape(B, L, H)
    return o, lse


def hybrid_layer(x, norm_g, w_in, conv_w, conv_b, fw1, fb1, ff1, fw2, fb2, ff2, fw3, hyena_d, sink, w_out):
    B, L, _ = x.shape
    pos = jnp.arange(L)
    proj = rmsnorm(x, norm_g) @ w_in
    cuts = [3 * HC, 4 * HC, 4 * HC + B_WIDTH, 4 * HC + B_WIDTH + B_KV_WIDTH,
            4 * HC + B_WIDTH + 2 * B_KV_WIDTH]
    hy_in, hy_gate, q, k, v, at_gate = jnp.split(proj, cuts, axis=-1)
    spec = hyena_filter_spectrum(L, fw1, fb1, ff1, fw2, fb2, ff2, fw3)
    hy = hyena_mix(short_conv(hy_in, conv_w, conv_b), spec, hyena_d)
    q = rope(q.reshape(B, L, B_HEADS, B_HEAD_DIM), pos).reshape(B, L, B_KV_HEADS, B_GROUP, B_HEAD_DIM)
    k = rope(k.reshape(B, L, B_KV_HEADS, B_HEAD_DIM), pos)
    v = v.reshape(B, L, B_KV_HEADS, B_HEAD_DIM)
    at, _ = band_attention(q, k, v, B_WINDOW, B_BLOCK, L, sink)
    mixed = jnp.concatenate([hy * jax.nn.silu(hy_gate),
                             at.reshape(B, L, B_WIDTH) * jax.nn.silu(at_gate)], axis=-1)
    return x + mixed @ w_out


def dilated_layer(x, norm_g, w_in, w_out):
    B, L, _ = x.shape
    pos = jnp.arange(L)
    proj = rmsnorm(x, norm_g) @ w_in
    qkv = proj[..., :N_PAT * 3 * C_WIDTH].reshape(B, L, N_PAT, 3, C_HEADS, C_HEAD_DIM)
    gate = proj[..., N_PAT * 3 * C_WIDTH:]
    outs, lses = [], []
    for g, (window, dilation) in enumerate(C_PATTERNS):
        q = rope(qkv[:, :, g, 0], pos)
        k = rope(qkv[:, :, g, 1], pos)
        o, lse = dilated_attention(q, k, qkv[:, :, g, 2], window, dilation)
        outs.append(o)
        lses.append(lse)
    alpha = jax.nn.softmax(jnp.stack(lses), axis=0)
    o = jnp.einsum('pblh,pblhd->blhd', alpha, jnp.stack(outs).astype(jnp.float32))
    y = o.reshape(B, L, C_WIDTH).astype(x.dtype) * jax.nn.silu(gate)
    return x + y @ w_out


def trunk(x, a_norm, a_w_in, a_conv_w, a_conv_b, a_filt_w1, a_filt_b1, a_filt_f1, a_filt_w2,
          a_filt_b2, a_filt_f2, a_filt_w3, a_hyena_d, a_sink, a_w_out, c_norm, c_w_in, c_w_out,
          final_norm):
    for layer in range(DEPTH):
        i = layer // 2
        if layer % 2 == 0:
            x = hybrid_layer(x, a_norm[i], a_w_in[i], a_conv_w[i], a_conv_b[i], a_filt_w1[i],
                             a_filt_b1[i], a_filt_f1[i], a_filt_w2[i], a_filt_b2[i], a_filt_f2[i],
                             a_filt_w3[i], a_hyena_d[i], a_sink[i], a_w_out[i])
        else:
            x = dilated_layer(x, c_norm[i], c_w_in[i], c_w_out[i])
    return rmsnorm(x, final_norm)


def setup_inputs(seed: int = 0) -> dict:
    key = jax.random.key(seed)
    ks = jax.random.split(key, 20)
    f32 = jnp.float32

    def nrm(k, shape, scale):
        return scale * jax.random.normal(k, shape, f32)

    return {
        "x_prompt": nrm(ks[0], (BATCH, SEQ, D_MODEL), 1.0),
        "x_sample": nrm(ks[1], (DEC_BATCH, DEC_SEQ, D_MODEL), 1.0),
        "a_norm": 1.0 + nrm(ks[2], (N_EVEN, D_MODEL), 0.05),
        "a_w_in": nrm(ks[3], (N_EVEN, D_MODEL, EVEN_IN), D_MODEL ** -0.5),
        "a_conv_w": nrm(ks[4], (N_EVEN, SHORT_CONV, 3 * HC), SHORT_CONV ** -0.5),
        "a_conv_b": nrm(ks[5], (N_EVEN, 3 * HC), 0.02),
        "a_filt_w1": nrm(ks[6], (N_EVEN, FILTER_EMB, FILTER_HIDDEN), FILTER_EMB ** -0.5),
        "a_filt_b1": nrm(ks[7], (N_EVEN, FILTER_HIDDEN), 0.1),
        "a_filt_f1": 1.0 + nrm(ks[8], (N_EVEN, FILTER_HIDDEN), 0.01),
        "a_filt_w2": nrm(ks[9], (N_EVEN, FILTER_HIDDEN, FILTER_HIDDEN), FILTER_HIDDEN ** -0.5),
        "a_filt_b2": nrm(ks[10], (N_EVEN, FILTER_HIDDEN), 0.1),
        "a_filt_f2": 1.0 + nrm(ks[11], (N_EVEN, FILTER_HIDDEN), 0.01),
        "a_filt_w3": nrm(ks[12], (N_EVEN, FILTER_HIDDEN, HYENA_ORDER * 2 * HC), 0.03 * FILTER_HIDDEN ** -0.5),
        "a_hyena_d": nrm(ks[13], (N_EVEN, HYENA_ORDER, HC), 0.5),
        "a_sink": nrm(ks[14], (N_EVEN, B_HEADS), 0.5),
        "a_w_out": nrm(ks[15], (N_EVEN, EVEN_INNER, D_MODEL), 0.5 * EVEN_INNER ** -0.5),
        "c_norm": 1.0 + nrm(ks[16], (N_ODD, D_MODEL), 0.05),
        "c_w_in": nrm(ks[17], (N_ODD, D_MODEL, ODD_IN), D_MODEL ** -0.5),
        "c_w_out": nrm(ks[18], (N_ODD, C_WIDTH, D_MODEL), 0.5 * C_WIDTH ** -0.5),
        "final_norm": 1.0 + nrm(ks[19], (D_MODEL,), 0.05),
    }


def reference(x_prompt, x_sample, a_norm, a_w_in, a_conv_w, a_conv_b, a_filt_w1, a_filt_b1, a_filt_f1,
              a_filt_w2, a_filt_b2, a_filt_f2, a_filt_w3, a_hyena_d, a_sink, a_w_out, c_norm, c_w_in,
              c_w_out, final_norm):
    y_prompt = trunk(x_prompt, a_norm, a_w_in, a_conv_w, a_conv_b, a_filt_w1, a_filt_b1, a_filt_f1,
                     a_filt_w2, a_filt_b2, a_filt_f2, a_filt_w3, a_hyena_d, a_sink, a_w_out, c_norm,
                     c_w_in, c_w_out, final_norm)
    y_sample = trunk(x_sample, a_norm, a_w_in, a_conv_w, a_conv_b, a_filt_w1, a_filt_b1, a_filt_f1,
                     a_filt_w2, a_filt_b2, a_filt_f2, a_filt_w3, a_hyena_d, a_sink, a_w_out, c_norm,
                     c_w_in, c_w_out, final_norm)
    return (y_prompt, y_sample)
```

```python
import contextlib
import math
import numpy as np
import ml_dtypes
import concourse.bass as bass
import concourse.mybir as mybir
from concourse.bass_utils import run_bass_kernel_spmd

F32, BF16 = mybir.dt.float32, mybir.dt.bfloat16
AF = mybir.ActivationFunctionType
ALU = mybir.AluOpType
AX = mybir.AxisListType

D = 1024
HC = 512
NCORES = 8
EVEN_IN = 3328
ODD_IN = 10240
EPS = 1e-6
SAFE_SAME_ENGINE = True


class Res:
    __slots__ = ("name", "t", "writers", "readers", "dsem", "is_dram", "is_psum")

    def __init__(self, name, t, is_dram=False):
        self.name, self.t = name, t
        self.writers = {}
        self.readers = {}
        self.dsem = {}
        self.is_dram = is_dram
        self.is_psum = False

    def __getitem__(self, idx):
        return self.t[idx]


class DSem:
    __slots__ = ("h", "cnt")

    def __init__(self, h):
        self.h, self.cnt = h, 0


class EngQ:
    def __init__(self, nc, eng, name, is_pe=False):
        self.eng, self.name, self.is_pe = eng, name, is_pe
        self.sem = nc.alloc_semaphore("sem_" + name)
        self.n = 0
        self.seen = {}


class K:
    def __init__(self, nc):
        self.nc = nc
        self.pe = EngQ(nc, nc.tensor, "pe", True)
        self.act = EngQ(nc, nc.scalar, "act")
        self.dve = EngQ(nc, nc.vector, "dve")
        self.pool = EngQ(nc, nc.gpsimd, "pool")
        self.sp = EngQ(nc, nc.sync, "sp")
        self.engs = [self.pe, self.act, self.dve, self.pool, self.sp]
        self.free_dsems = {}
        self.all_dsems = []
        self.dsem_of = {}
        self.live = []
        self.ninst = 0
        self.uid = 0

    def sb(self, es, name, shape, dt):
        self.uid += 1
        name = "%s_%d" % (name, self.uid)
        t = es.enter_context(self.nc.sbuf_tensor(name, list(shape), dt))
        r = Res(name, t)
        self.live.append(r)
        return r

    def ps(self, es, name, shape, dt=F32):
        self.uid += 1
        name = "%s_%d" % (name, self.uid)
        t = es.enter_context(self.nc.psum_tensor(name, list(shape), dt))
        r = Res(name, t)
        r.is_psum = True
        self.live.append(r)
        return r

    def dram(self, name, shape, dt, kind="Internal"):
        t = self.nc.dram_tensor(name, list(shape), dt, kind=kind)
        return Res(name, t.ap(), is_dram=True)

    def _get_dsem(self, r, q):
        d = r.dsem.get(q.name)
        if d is None:
            fl = self.free_dsems.setdefault(q.name, [])
            if fl:
                d = fl.pop()
            else:
                h = self.nc.alloc_semaphore("dsem%d" % len(self.all_dsems))
                d = DSem(h)
                self.all_dsems.append(d)
                self.dsem_of[h] = d
            r.dsem[q.name] = d
        return d

    def _wait(self, q, sem, val):
        if sem is q.sem and (q.is_pe or not SAFE_SAME_ENGINE):
            return
        ds = self.dsem_of.get(sem)
        if ds is not None:
            val = ds.cnt
        if q.seen.get(sem, 0) >= val:
            return
        q.eng.wait_ge(sem, val)
        q.seen[sem] = val
        self.ninst += 1

    @staticmethod
    def _rw(R, W):
        R2 = [r for r in R if not r.is_psum]
        W2 = list(W) + [r for r in R if r.is_psum and r not in W]
        return R2, W2

    def _deps(self, q, R, W, dma_sem=None):
        for r in R:
            for s, v in r.writers.items():
                self._wait(q, s, v)
        for w in W:
            for s, v in w.writers.items():
                if dma_sem is not None and s is dma_sem and not w.readers:
                    continue
                self._wait(q, s, v)
            for s, v in w.readers.items():
                self._wait(q, s, v)

    def _mark(self, tok, R, W):
        s, v = tok
        for r in R:
            if r.readers.get(s, 0) < v:
                r.readers[s] = v
        for w in W:
            if w.is_dram:
                w.writers[s] = v
            else:
                w.writers = {s: v}
                w.readers = {}

    def op(self, q, fn, R=(), W=()):
        R, W = self._rw(R, W)
        self._deps(q, R, W)
        ins = fn()
        q.n += 1
        ins.then_inc(q.sem, 1)
        self.ninst += 1
        self._mark((q.sem, q.n), R, W)

    def mm(self, mms, R=(), W=()):
        q = self.pe
        self._deps(q, R, W)
        ins = None
        for kw in mms:
            ins = self.nc.tensor.matmul(**kw)
        self.ninst += len(mms)
        q.n += 1
        ins.then_inc(q.sem, 1)
        self._mark((q.sem, q.n), R, W)

    def tr(self, trs, R=(), W=()):
        q = self.pe
        self._deps(q, R, W)
        ins = None
        for (o, i, ident) in trs:
            ins = self.nc.tensor.transpose(out=o, in_=i, identity=ident)
        self.ninst += len(trs)
        q.n += 1
        ins.then_inc(q.sem, 1)
        self._mark((q.sem, q.n), R, W)

    def dma(self, q, out, in_, R, W, slow=False):
        assert len(W) == 1
        owner = W[0] if not W[0].is_dram else R[0]
        assert not owner.is_dram
        ds = self._get_dsem(owner, q)
        self._deps(q, R, W, dma_sem=ds.h)
        ds.cnt += 16
        q.eng.dma_start(out=out, in_=in_, allow_slow_non_contiguous=slow).then_inc(ds.h, 16)
        self.ninst += 1
        self._mark((ds.h, ds.cnt), R, W)

    def barrier(self):
        toks = [(e.sem, e.n) for e in self.engs if e.n > 0]
        toks += [(d.h, d.cnt) for d in self.all_dsems if d.cnt > 0]
        for q in self.engs:
            for tok in toks:
                if tok[0] is q.sem:
                    continue
                if q.seen.get(tok[0], 0) >= tok[1]:
                    continue
                q.eng.wait_ge(tok[0], tok[1])
                q.seen[tok[0]] = tok[1]
        for r in self.live:
            for qn, d in r.dsem.items():
                self.free_dsems[qn].append(d)
            r.dsem = {}
        self.live = []


def dap(res, offset, pairs):
    return bass.AP(res.t.tensor, offset, [list(p) for p in pairs])


def rope_table(L, rot):
    half = rot // 2
    inv = np.power(np.float32(500000.0), -2.0 * np.arange(half, dtype=np.float32) / np.float32(rot)).astype(np.float32)
    ang = (np.arange(L, dtype=np.float32)[:, None] * inv[None, :]).astype(np.float32)
    co, si = np.cos(ang), np.sin(ang)
    return np.concatenate([co, co, -si, si], axis=1).astype(np.float32)


def filter_feats(L):
    t = np.linspace(0.0, 1.0, L, dtype=np.float32)[:, None]
    wpos = (2.0 * math.pi * np.arange(L, dtype=np.float32)[:, None] / L).astype(np.float32)
    bands = np.linspace(1e-4, 15, 16, dtype=np.float32)[None, :]
    feats = np.concatenate([t, np.cos(bands * wpos), -np.sin(bands * wpos)], axis=-1).astype(np.float32)
    return feats


def decay_deltas():
    max_decay = math.log(1e-2) / 0.3
    min_decay = math.log(1e-2) / 1.5
    return np.abs(np.linspace(min_decay, max_decay, HC, dtype=np.float32)).astype(np.float32)


def const_tables(Ls_list):
    c = {}
    ident = np.eye(128, dtype=np.float32)
    c["ident"] = ident.astype(ml_dtypes.bfloat16)
    c["antiid"] = ident[::-1].copy().astype(ml_dtypes.bfloat16)
    j = np.arange(128)[:, None]
    i = np.arange(128)[None, :]
    c["maskA"] = (j >= i).astype(np.float32).astype(ml_dtypes.bfloat16)
    c["maskB"] = (j <= i).astype(np.float32).astype(ml_dtypes.bfloat16)
    lo = np.zeros((128, 128), np.float32)
    lo[64:, :] = 1.0
    hi = np.zeros((128, 128), np.float32)
    hi[:64, :] = 1.0
    c["ones"] = np.ones((128, 128), np.float32).astype(ml_dtypes.bfloat16)
    c["vlo"] = lo.astype(ml_dtypes.bfloat16)
    c["vhi"] = hi.astype(ml_dtypes.bfloat16)
    Lmax = max(Ls_list)
    c["rope8"] = rope_table(Lmax, 16)
    c["rope16"] = rope_table(Lmax, 32)
    for L in sorted(set(Ls_list)):
        f = filter_feats(L)
        c["featsT_%d" % L] = np.ascontiguousarray(f.T)
        c["featsTr_%d" % L] = np.ascontiguousarray(f[::-1].T)
        t = np.linspace(0.0, 1.0, L, dtype=np.float32)
        c["trow_%d" % L] = np.stack([t, t[::-1]]).astype(np.float32)
    c["negdelta"] = (-decay_deltas()).reshape(4, 128).T.copy()
    return c


def pstep(res):
    return res.t[:].ap[0][0]


def sview(res, off, dims, nparts=128, p0=0):
    ps_ = pstep(res)
    return bass.AP(res.t[:].tensor, p0 * ps_ + off, [[ps_, nparts]] + [list(d) for d in dims])


class Prog:
    GP = 8192

    def __init__(self, Lp, Ls, ns, depth=4, debug=()):
        self.Lp, self.Ls, self.ns, self.depth = Lp, Ls, ns, depth
        self.debug = set(debug)
        self.nc = bass.Bass("TRN2", target_bir_lowering=False)
        self.k = K(self.nc)
        self.seqs = [("p", Lp)] + [("s%d" % i, Ls) for i in range(ns)]
        self.inputs = {}
        self.outputs = {}
        self.consts = const_tables([Lp, Ls])
        self._declare()

    def inp(self, name, shape, dt):
        r = self.k.dram(name, shape, dt, kind="ExternalInput")
        self.inputs[name] = r
        return r

    def scratch(self, name, shape, dt):
        kind = "ExternalOutput" if name in self.debug else "Internal"
        r = self.k.dram(name, shape, dt, kind=kind)
        if kind == "ExternalOutput":
            self.outputs[name] = r
        return r

    def _declare(self):
        ne, no = (self.depth + 1) // 2, self.depth // 2
        self.ne, self.no = ne, no
        i = self.inp
        self.x_in = {"p": i("x_p", [self.Lp, D], F32)}
        xs = i("x_s", [self.ns, self.Ls, D], F32)
        for s in range(self.ns):
            r = Res("x_s%d" % s, xs.t[s], is_dram=True)
            self.x_in["s%d" % s] = r
        self.w = {}
        for name, shape in [
            ("a_norm", [ne, D]), ("a_w_in", [ne, D, EVEN_IN]), ("a_conv_w", [ne, 3, 3 * HC]),
            ("a_conv_b", [ne, 3 * HC]), ("a_filt_w1", [ne, 33, 64]), ("a_filt_b1", [ne, 64]),
            ("a_filt_f1", [ne, 64]), ("a_filt_w2", [ne, 64, 64]), ("a_filt_b2", [ne, 64]),
            ("a_filt_f2", [ne, 64]), ("a_filt_w3", [ne, 64, 4 * HC]), ("a_hyena_d", [ne, 2, HC]),
            ("a_sink", [ne, 8]), ("a_w_out", [ne, D, D]), ("c_norm", [max(no, 1), D]),
            ("c_w_in", [max(no, 1), D, ODD_IN]), ("c_w_out", [max(no, 1), D, D]), ("final_norm", [1, D]),
        ]:
            self.w[name] = i(name, shape, F32)
        self.c = {}
        for name, arr in self.consts.items():
            dt = BF16 if arr.dtype == ml_dtypes.bfloat16 else F32
            self.c[name] = i("c_" + name, list(arr.shape), dt)
        self.y = {}
        for (sn, L) in self.seqs:
            r = self.k.dram("y_" + sn, [L, D], F32, kind="ExternalOutput")
            self.outputs["y_" + sn] = r
            self.y[sn] = r
        sc = self.scratch
        self.xa, self.xb, self.xnT = {}, {}, {}
        self.HY, self.HG, self.QT, self.KT, self.VB, self.AGT, self.MT, self.HYO = {}, {}, {}, {}, {}, {}, {}, {}
        for (sn, L) in self.seqs:
            self.xa[sn] = sc("xa_" + sn, [L, D], F32)
            self.xb[sn] = sc("xb_" + sn, [L, D], F32)
            self.xnT[sn] = sc("xnT_" + sn, [8, 128, L + 4], BF16)
            self.HY[sn] = sc("HY_" + sn, [L, 3 * HC], F32)
            self.HG[sn] = sc("HG_" + sn, [L, HC], F32)
            self.QT[sn] = sc("QT_" + sn, [512, L], BF16)
            self.KT[sn] = sc("KT_" + sn, [128, L], BF16)
            self.VB[sn] = sc("VB_" + sn, [L, 128], BF16)
            self.AGT[sn] = sc("AGT_" + sn, [512, L], BF16)
            self.MT[sn] = sc("MT_" + sn, [D, L], BF16)
            self.HYO[sn] = sc("HYO_" + sn, [L, HC], BF16)
        self.QTc, self.KTc, self.Vc, self.CGT = {}, {}, {}, {}
        for (sn, L) in self.seqs:
            for g, dl in enumerate((1, 4, 16)):
                pad = 64 * dl
                self.QTc[sn, g] = sc("QTc%d_%s" % (g, sn), [D, L], BF16)
                self.KTc[sn, g] = sc("KTc%d_%s" % (g, sn), [D, L + 2 * pad], BF16)
                self.Vc[sn, g] = sc("Vc%d_%s" % (g, sn), [L + 2 * pad, D], BF16)
            self.CGT[sn] = sc("CGT_" + sn, [D, L], BF16)
        self.A, self.Dx = {}, {}
        for L in sorted(set([self.Lp, self.Ls])):
            self.A[L] = sc("A_%d" % L, [2, HC, 2 * L], BF16)
            self.Dx[L] = sc("Dx_%d" % L, [2, HC], F32)

    def phase_norm(self, src, gname=None, final=False):
        k, nc = self.k, self.nc
        with contextlib.ExitStack() as es:
            ident = k.sb(es, "ident", [128, 128], BF16)
            k.dma(k.sp, ident[:], self.c["ident"].t[:, :], [self.c["ident"]], [ident])
            zc = k.sb(es, "zc", [128, 8, 2], BF16)
            k.op(k.dve, lambda: nc.vector.memset(zc[:], 0.0), W=[zc])
            xin = [k.sb(es, "xin%d" % i, [128, 4, D], F32) for i in range(2)]
            sq = [k.sb(es, "sq%d" % i, [128, D], F32) for i in range(2)]
            ss = [k.sb(es, "ss%d" % i, [128, 4], F32) for i in range(2)]
            rs = [k.sb(es, "rs%d" % i, [128, 4], F32) for i in range(2)]
            if final:
                gb = k.sb(es, "gb", [128, D], F32)
                k.dma(k.sp, gb[:], dap(self.w["final_norm"], 0, [[0, 128], [1, D]]), [self.w["final_norm"]], [gb])
                yo = [k.sb(es, "yo%d" % i, [128, 4, D], F32) for i in range(2)]
            else:
                xs = [k.sb(es, "xs%d" % i, [128, 4, D], BF16) for i in range(2)]
                xo = [k.sb(es, "xo%d" % i, [128, 8, 512], BF16) for i in range(2)]
                pst = [k.ps(es, "pt%d" % i, [128, 8, 128], BF16) for i in range(4)]
            it = 0
            for (sn, L) in self.seqs:
                x = src[sn]
                if not final:
                    xt = self.xnT[sn]
                    xt_v = xt.t.rearrange("k p t -> p k t")
                    k.dma(k.pool, xt_v[:, :, 0:2], zc[:], [zc], [xt], slow=True)
                    k.dma(k.pool, xt_v[:, :, L + 2:L + 4], zc[:], [zc], [xt], slow=True)
                xv = x.t.rearrange("(c j p) d -> c p j d", j=4, p=128)
                for c in range(L // 512):
                    b = it % 2
                    it += 1
                    xi, s_, r_, q_ = xin[b], ss[b], rs[b], sq[b]
                    k.dma(k.sp, xi[:], xv[c], [x], [xi])
                    k.op(k.dve, lambda: nc.vector.memset(s_[:], 0.0), W=[s_])
                    for j in range(4):
                        k.op(k.act, lambda j=j: nc.scalar.activation(out=q_[:], in_=xi[:, j, :], func=AF.Square,
                                                                     accum_out=s_[:, j:j + 1]), R=[xi], W=[q_, s_])
                    k.op(k.act, lambda: nc.scalar.activation(out=r_[:], in_=s_[:], func=AF.Sqrt, bias=EPS, scale=1.0 / D),
                         R=[s_], W=[r_])
                    k.op(k.dve, lambda: nc.vector.reciprocal(out=r_[:], in_=r_[:]), R=[r_], W=[r_])
                    if final:
                        y_ = yo[b]
                        for j in range(4):
                            k.op(k.dve, lambda j=j: nc.vector.scalar_tensor_tensor(
                                out=y_[:, j, :], in0=xi[:, j, :], scalar=r_[:, j:j + 1], in1=gb[:],
                                op0=ALU.mult, op1=ALU.mult), R=[xi, r_, gb], W=[y_])
                        yv = self.y[sn].t.rearrange("(c j p) d -> c p j d", j=4, p=128)
                        k.dma(k.pool, yv[c], y_[:], [y_], [self.y[sn]])
                        continue
                    xs_, xo_ = xs[b], xo[b]
                    for j in range(4):
                        if j % 2 == 0:
                            k.op(k.act, lambda j=j: nc.scalar.activation(out=xs_[:, j, :], in_=xi[:, j, :], func=AF.Copy,
                                                                         scale=r_[:, j:j + 1]), R=[xi, r_], W=[xs_])
                        else:
                            k.op(k.dve, lambda j=j: nc.vector.tensor_scalar(out=xs_[:, j, :], in0=xi[:, j, :],
                                                                            scalar1=r_[:, j:j + 1], scalar2=None,
                                                                            op0=ALU.mult), R=[xi, r_], W=[xs_])
                    for j in range(4):
                        pt = pst[(it * 4 + j) % 4]
                        k.tr([(pt[:, kk, :], xs_[:, j, kk * 128:(kk + 1) * 128], ident[:]) for kk in range(8)],
                             R=[xs_, ident], W=[pt])
                        if j % 2 == 0:
                            k.op(k.dve, lambda j=j, pt=pt: nc.vector.tensor_copy(out=xo_[:, :, j * 128:(j + 1) * 128], in_=pt[:]),
                                 R=[pt], W=[xo_])
                        else:
                            k.op(k.act, lambda j=j, pt=pt: nc.scalar.copy(out=xo_[:, :, j * 128:(j + 1) * 128], in_=pt[:]),
                                 R=[pt], W=[xo_])
                    k.dma(k.pool, xt_v[:, :, 2 + c * 512:2 + (c + 1) * 512], xo_[:], [xo_], [xt])
            k.barrier()

    def rope_tm(self, ps, nh, hd, rot, cs_ap_cc, cs_ap_ss, dst, dst_off, tmp_a, tmp_b, extra_R=()):
        k, nc = self.k, self.nc
        half = rot // 2
        x_all = sview(ps, 0, [(1, nh * hd)])
        xr = sview(ps, 0, [(hd, nh), (1, rot)])
        x1 = sview(ps, 0, [(hd, nh), (1, half)])
        x2 = sview(ps, half, [(hd, nh), (1, half)])
        ta = sview(tmp_a, 0, [(rot, nh), (1, rot)])
        tb1 = sview(tmp_b, 0, [(rot, nh), (1, half)])
        tb2 = sview(tmp_b, half, [(rot, nh), (1, half)])
        tb = sview(tmp_b, 0, [(rot, nh), (1, rot)])
        ss1 = cs_ap_ss[0]
        ss2 = cs_ap_ss[1]
        R0 = [ps] + list(extra_R)
        k.op(k.act, lambda: nc.scalar.copy(out=sview(dst, dst_off, [(1, nh * hd)]), in_=x_all), R=[ps], W=[dst])
        k.op(k.dve, lambda: nc.vector.tensor_tensor(out=ta, in0=xr, in1=cs_ap_cc, op=ALU.mult), R=R0, W=[tmp_a])
        k.op(k.dve, lambda: nc.vector.tensor_tensor(out=tb1, in0=x2, in1=ss1, op=ALU.mult), R=R0, W=[tmp_b])
        k.op(k.dve, lambda: nc.vector.tensor_tensor(out=tb2, in0=x1, in1=ss2, op=ALU.mult), R=R0, W=[tmp_b])
        k.op(k.dve, lambda: nc.vector.tensor_tensor(out=sview(dst, dst_off, [(hd, nh), (1, rot)]), in0=ta, in1=tb,
                                                    op=ALU.add), R=[tmp_a, tmp_b], W=[dst])

    def phase_even_proj(self, li, parts=("hy", "hg", "q", "kv", "ag")):
        k, nc = self.k, self.nc
        W = self.w["a_w_in"]
        with contextlib.ExitStack() as es:
            ident = k.sb(es, "ident", [128, 128], BF16)
            k.dma(k.sp, ident[:], self.c["ident"].t[:, :], [self.c["ident"]], [ident])
            Wsb = k.sb(es, "Wsb", [128, 8, 1792], BF16)
            Wtap = [k.sb(es, "Wtap%d" % t, [128, 8, 1536], BF16) for t in range(3)]
            biasb = k.sb(es, "biasb", [128, 1536], F32)
            k.dma(k.sp, biasb[:], dap(self.w["a_conv_b"], li * 1536, [[0, 128], [1, 1536]]), [self.w["a_conv_b"]], [biasb])
            with contextlib.ExitStack() as es2:
                gsb = k.sb(es2, "gsb", [128, 8, 1], F32)
                k.dma(k.sp, gsb[:], dap(self.w["a_norm"], li * D, [[1, 128], [128, 8], [1, 1]]), [self.w["a_norm"]], [gsb], slow=True)
                tapb = [k.sb(es2, "tapb%d" % t, [128, 1536], F32) for t in range(3)]
                for t in range(3):
                    k.dma(k.sp, tapb[t][:], dap(self.w["a_conv_w"], (li * 3 + t) * 1536, [[0, 128], [1, 1536]]),
                          [self.w["a_conv_w"]], [tapb[t]])
                stage = [k.sb(es2, "stage%d" % i, [128, 8, 256], F32) for i in range(2)]
                wv = W.t[li].rearrange("(k p) c -> p k c", p=128)
                for cb in range(13):
                    c0 = cb * 256
                    st = stage[cb % 2]
                    k.dma(k.sp, st[:], wv[:, :, c0:c0 + 256], [W], [st])
                    k.op(k.dve, lambda st=st: nc.vector.tensor_tensor(
                        out=st[:], in0=st[:], in1=sview(gsb, 0, [(1, 8), (0, 256)]), op=ALU.mult), R=[st, gsb], W=[st])
                    if c0 < 1536:
                        for t in range(3):
                            k.op(k.dve, lambda st=st, t=t, c0=c0: nc.vector.tensor_tensor(
                                out=Wtap[t][:, :, c0:c0 + 256], in0=st[:],
                                in1=sview(tapb[t], c0, [(0, 8), (1, 256)]), op=ALU.mult), R=[st, tapb[t]], W=[Wtap[t]])
                    else:
                        k.op(k.act, lambda st=st, c0=c0: nc.scalar.copy(out=Wsb[:, :, c0 - 1536:c0 - 1536 + 256], in_=st[:]),
                             R=[st], W=[Wsb])
                k.barrier()
            xc = [k.sb(es, "xc%d" % i, [128, 8, 516], BF16) for i in range(2)]
            xcB = [k.sb(es, "xcB%d" % i, [128, 8, 516], BF16) for i in range(2)]
            hyo = [k.sb(es, "hyo%d" % i, [128, 1536], F32) for i in range(2)]
            hgo = [k.sb(es, "hgo%d" % i, [128, 512], F32) for i in range(2)]
            qbf = [k.sb(es, "qbf%d" % i, [128, 512], BF16) for i in range(2)]
            kvbf = [k.sb(es, "kvbf%d" % i, [128, 256], BF16) for i in range(2)]
            qTs = [k.sb(es, "qTs%d" % i, [128, 4, 512], BF16) for i in range(2)]
            kTs = [k.sb(es, "kTs%d" % i, [128, 512], BF16) for i in range(2)]
            vo = [k.sb(es, "vo%d" % i, [128, 4, 128], BF16) for i in range(2)]
            agT = [k.sb(es, "agT%d" % i, [128, 4, 512], BF16) for i in range(2)]
            cs = [k.sb(es, "cs%d" % i, [128, 4, 32], F32) for i in range(2)]
            tmpa = [k.sb(es, "tmpa%d" % i, [128, 128], F32) for i in range(2)]
            tmpb = [k.sb(es, "tmpb%d" % i, [128, 128], F32) for i in range(2)]
            psA = [k.ps(es, "psA%d" % i, [128, 512], F32) for i in range(6)]
            psT = [k.ps(es, "psT%d" % i, [128, 8, 128], BF16) for i in range(2)]
            pa = [0]

            def nps():
                pa[0] += 1
                return psA[pa[0] % 6]
            it = 0
            tt = 0
            for (sn, L) in self.seqs:
                xt = self.xnT[sn]
                xt_v = xt.t.rearrange("k p t -> p k t")
                rope_v = self.c["rope8"].t.rearrange("(c j p) r -> c p j r", j=4, p=128)
                for c in range(L // 512):
                    b = it % 2
                    it += 1
                    x_ = xc[b]
                    k.dma(k.sp, x_[:], xt_v[:, :, c * 512:c * 512 + 516], [xt], [x_])
                    xB_ = xcB[b]
                    k.dma(k.sp, xB_[:, :, 0:514], xt_v[:, :, c * 512 + 1:c * 512 + 515], [xt], [xB_])
                    cs_ = cs[b]
                    k.dma(k.sp, cs_[:], rope_v[c], [self.c["rope8"]], [cs_])
                    qT_, kT_, vo_, ag_ = qTs[b], kTs[b], vo[b], agT[b]
                    for j in range(4):
                        tb_ = tt % 2
                        tt += 1
                        t0 = 2 + j * 128
                        hy_ = hyo[tb_]
                        tok0 = c * 512 + j * 128
                        for g in range(3 if "hy" in parts else 0):
                            p_ = nps()
                            mms = []
                            for t in range(3):
                                for kk in range(8):
                                    src_ = x_ if t == 1 else xB_
                                    o_ = t0 if t == 1 else (j * 128 + t)
                                    mms.append(dict(out=p_[:], lhsT=src_[:, kk, o_:o_ + 128],
                                                    rhs=Wtap[t][:, kk, g * 512:(g + 1) * 512],
                                                    start=(t == 0 and kk == 0), stop=(t == 2 and kk == 7)))
                            k.mm(mms, R=[x_, xB_] + Wtap, W=[p_])
                            k.op(k.dve, lambda p_=p_, g=g, hy_=hy_: nc.vector.tensor_tensor(
                                out=hy_[:, g * 512:(g + 1) * 512], in0=p_[:], in1=biasb[:, g * 512:(g + 1) * 512], op=ALU.add),
                                R=[p_, biasb], W=[hy_])
                        if "hy" in parts:
                            k.dma(k.pool, self.HY[sn].t[tok0:tok0 + 128, :], hy_[:], [hy_], [self.HY[sn]])
                        if "hg" not in parts:
                            continue
                        p_ = nps()
                        k.mm([dict(out=p_[:], lhsT=x_[:, kk, t0:t0 + 128], rhs=Wsb[:, kk, 0:512], start=(kk == 0), stop=(kk == 7))
                              for kk in range(8)], R=[x_, Wsb], W=[p_])
                        hg_ = hgo[tb_]
                        k.op(k.act, lambda p_=p_, hg_=hg_: nc.scalar.activation(out=hg_[:], in_=p_[:], func=AF.Silu), R=[p_], W=[hg_])
                        k.dma(k.pool, self.HG[sn].t[tok0:tok0 + 128, :], hg_[:], [hg_], [self.HG[sn]])
                        if "q" not in parts:
                            continue
                        p_ = nps()
                        k.mm([dict(out=p_[:], lhsT=x_[:, kk, t0:t0 + 128], rhs=Wsb[:, kk, 512:1024], start=(kk == 0), stop=(kk == 7))
                              for kk in range(8)], R=[x_, Wsb], W=[p_])
                        q_ = qbf[tb_]
                        cc = sview(cs_, j * 32, [(0, 8), (1, 16)])
                        s1 = sview(cs_, j * 32 + 16, [(0, 8), (1, 8)])
                        s2 = sview(cs_, j * 32 + 24, [(0, 8), (1, 8)])
                        self.rope_tm(p_, 8, 64, 16, cc, (s1, s2), q_, 0, tmpa[tb_], tmpb[tb_], extra_R=[cs_])
                        pt = psT[tb_]
                        k.tr([(pt[:, blk, :], q_[:, blk * 128:(blk + 1) * 128], ident[:]) for blk in range(4)], R=[q_, ident], W=[pt])
                        k.op(k.act, lambda pt=pt, j=j, qT_=qT_: nc.scalar.copy(out=qT_[:, :, j * 128:(j + 1) * 128], in_=pt[:, 0:4, :]),
                             R=[pt], W=[qT_])
                        if "kv" not in parts:
                            continue
                        p_ = nps()
                        k.mm([dict(out=p_[:, 0:256], lhsT=x_[:, kk, t0:t0 + 128], rhs=Wsb[:, kk, 1024:1280], start=(kk == 0), stop=(kk == 7))
                              for kk in range(8)], R=[x_, Wsb], W=[p_])
                        kv_ = kvbf[tb_]
                        cc2 = sview(cs_, j * 32, [(0, 2), (1, 16)])
                        s12 = sview(cs_, j * 32 + 16, [(0, 2), (1, 8)])
                        s22 = sview(cs_, j * 32 + 24, [(0, 2), (1, 8)])
                        self.rope_tm(p_, 2, 64, 16, cc2, (s12, s22), kv_, 0, tmpa[tb_], tmpb[tb_], extra_R=[cs_])
                        k.op(k.dve, lambda p_=p_, vo_=vo_, j=j: nc.vector.tensor_copy(out=vo_[:, j, :], in_=p_[:, 128:256]), R=[p_], W=[vo_])
                        k.tr([(pt[:, 4, :], kv_[:, 0:128], ident[:])], R=[kv_, ident], W=[pt])
                        k.op(k.dve, lambda pt=pt, j=j, kT_=kT_: nc.vector.tensor_copy(out=kT_[:, j * 128:(j + 1) * 128], in_=pt[:, 4, :]),
                             R=[pt], W=[kT_])
                    for cb in range(4 if "ag" in parts else 0):
                        p_ = nps()
                        k.mm([dict(out=p_[:], lhsT=Wsb[:, kk, 1280 + cb * 128:1280 + (cb + 1) * 128], rhs=x_[:, kk, 2:514],
                                   start=(kk == 0), stop=(kk == 7)) for kk in range(8)], R=[x_, Wsb], W=[p_])
                        k.op(k.act, lambda p_=p_, cb=cb, ag_=ag_: nc.scalar.activation(out=ag_[:, cb, :], in_=p_[:], func=AF.Silu),
                             R=[p_], W=[ag_])
                    sl = slice(c * 512, (c + 1) * 512)
                    if "q" in parts:
                        k.dma(k.pool, self.QT[sn].t.rearrange("(b p) t -> p b t", p=128)[:, :, sl], qT_[:], [qT_], [self.QT[sn]])
                    if "kv" in parts:
                        k.dma(k.pool, self.KT[sn].t[:, sl], kT_[:], [kT_], [self.KT[sn]])
                        k.dma(k.pool, self.VB[sn].t.rearrange("(c j p) d -> c p j d", j=4, p=128)[c], vo_[:], [vo_], [self.VB[sn]])
                    if "ag" in parts:
                        k.dma(k.pool, self.AGT[sn].t.rearrange("(b p) t -> p b t", p=128)[:, :, sl], ag_[:], [ag_], [self.AGT[sn]])
            k.barrier()

    def finish(self):
        k = self.k
        k.barrier()

    def in_map(self, x_p, x_s, weights):
        m = {"x_p": np.ascontiguousarray(x_p, dtype=np.float32), "x_s": np.ascontiguousarray(x_s, dtype=np.float32)}
        for name in self.w:
            a = np.asarray(weights[name], dtype=np.float32)
            if name == "final_norm":
                a = a.reshape(1, D)
            m[name] = np.ascontiguousarray(a)
        for name, arr in self.consts.items():
            m["c_" + name] = arr
        return m

    def phase_outproj(self, wname, li, src, dst):
        k, nc = self.k, self.nc
        Wd = self.w[wname]
        with contextlib.ExitStack() as es:
            Wo = k.sb(es, "Wo", [128, 8, D], BF16)
            with contextlib.ExitStack() as es2:
                stage = [k.sb(es2, "stg%d" % i, [128, 8, 256], F32) for i in range(2)]
                wv = Wd.t[li].rearrange("(k p) c -> p k c", p=128)
                for cb in range(4):
                    st = stage[cb % 2]
                    k.dma(k.sp, st[:], wv[:, :, cb * 256:(cb + 1) * 256], [Wd], [st])
                    if cb % 2:
                        k.op(k.act, lambda st=st, cb=cb: nc.scalar.copy(out=Wo[:, :, cb * 256:(cb + 1) * 256], in_=st[:]), R=[st], W=[Wo])
                    else:
                        k.op(k.dve, lambda st=st, cb=cb: nc.vector.tensor_copy(out=Wo[:, :, cb * 256:(cb + 1) * 256], in_=st[:]), R=[st], W=[Wo])
                k.barrier()
            mt = [k.sb(es, "mt%d" % i, [128, 8, 512], BF16) for i in range(2)]
            xi = [k.sb(es, "xi%d" % i, [128, 4, D], F32) for i in range(2)]
            xo = [k.sb(es, "xo%d" % i, [128, 4, D], F32) for i in range(2)]
            ps = [k.ps(es, "po%d" % i, [128, 512], F32) for i in range(4)]
            it = 0
            pi = 0
            for (sn, L) in self.seqs:
                mtv = self.MT[sn].t.rearrange("(k p) t -> p k t", p=128)
                xv = src[sn].t.rearrange("(c j p) d -> c p j d", j=4, p=128)
                dv = dst[sn].t.rearrange("(c j p) d -> c p j d", j=4, p=128)
                for c in range(L // 512):
                    b = it % 2
                    it += 1
                    mt_, xi_, xo_ = mt[b], xi[b], xo[b]
                    k.dma(k.sp, mt_[:], mtv[:, :, c * 512:(c + 1) * 512], [self.MT[sn]], [mt_])
                    k.dma(k.sp, xi_[:], xv[c], [src[sn]], [xi_])
                    for j in range(4):
                        for g in range(2):
                            p_ = ps[pi % 4]
                            pi += 1
                            k.mm([dict(out=p_[:], lhsT=mt_[:, kk, j * 128:(j + 1) * 128], rhs=Wo[:, kk, g * 512:(g + 1) * 512],
                                       start=(kk == 0), stop=(kk == 7)) for kk in range(8)], R=[mt_, Wo], W=[p_])
                            k.op(k.dve, lambda p_=p_, j=j, g=g: nc.vector.tensor_tensor(
                                out=xo_[:, j, g * 512:(g + 1) * 512], in0=p_[:], in1=xi_[:, j, g * 512:(g + 1) * 512], op=ALU.add),
                                R=[p_, xi_], W=[xo_])
                    k.dma(k.pool, dv[c], xo_[:], [xo_], [dst[sn]])
            k.barrier()

    def load_const(self, es, name, shape, dt):
        k = self.k
        r = k.sb(es, name, shape, dt)
        k.dma(k.sp, r[:], self.c[name].t[:, :], [self.c[name]], [r])
        return r

    def phase_band_attn(self, li):
        k, nc = self.k, self.nc
        with contextlib.ExitStack() as es:
            maskA = self.load_const(es, "maskA", [128, 128], BF16)
            maskB = self.load_const(es, "maskB", [128, 128], BF16)
            ones = self.load_const(es, "ones", [128, 128], BF16)
            snk = k.sb(es, "snk", [64, 8], F32)
            k.dma(k.sp, snk[:], dap(self.w["a_sink"], li * 8, [[0, 64], [1, 8]]), [self.w["a_sink"]], [snk])
            k.op(k.act, lambda: nc.scalar.activation(out=snk[:], in_=snk[:], func=AF.Exp), R=[snk], W=[snk])
            q_sb = [k.sb(es, "q_sb%d" % i, [64, 8, 512], BF16) for i in range(2)]
            ag_sb = [k.sb(es, "ag_sb%d" % i, [64, 8, 512], BF16) for i in range(2)]
            k_sb = [k.sb(es, "k_sb%d" % i, [64, 2, 768], BF16) for i in range(2)]
            v_sb = [k.sb(es, "v_sb%d" % i, [128, 6, 128], BF16) for i in range(2)]
            mt_sb = [k.sb(es, "mt_sb%d" % i, [64, 8, 512], BF16) for i in range(2)]
            pT = [k.sb(es, "pT%d" % i, [128, 512], BF16) for i in range(3)]
            den = [k.sb(es, "den%d" % i, [64, 512], F32) for i in range(2)]
            o_sb = [k.sb(es, "o_sb%d" % i, [64, 512], F32) for i in range(2)]
            psS = [k.ps(es, "psS%d" % i, [128, 512], F32) for i in range(3)]
            psO = [k.ps(es, "psO%d" % i, [128, 512], F32) for i in range(2)]
            psD = [k.ps(es, "psD%d" % i, [128, 512], F32) for i in range(2)]
            it = 0
            u = 0
            si = 0
            for (sn, L) in self.seqs:
                nq = L // 128
                qv = self.QT[sn].t.rearrange("(h d) t -> d h t", d=64)
                agv = self.AGT[sn].t.rearrange("(h d) t -> d h t", d=64)
                kv = self.KT[sn].t.rearrange("(h d) t -> d h t", d=64)
                vv = self.VB[sn].t.rearrange("(b p) d -> p b d", p=128)
                mv = self.MT[sn].t[512:1024, :].rearrange("(h d) t -> d h t", d=64)
                for c in range(L // 512):
                    b = it % 2
                    it += 1
                    t0 = c * 512
                    q_, ag_, k_, v_, mt_ = q_sb[b], ag_sb[b], k_sb[b], v_sb[b], mt_sb[b]
                    k.dma(k.sp, q_[:], qv[:, :, t0:t0 + 512], [self.QT[sn]], [q_])
                    k.dma(k.sp, ag_[:], agv[:, :, t0:t0 + 512], [self.AGT[sn]], [ag_])
                    ks, ke = max(0, t0 - 128), min(L, t0 + 640)
                    k.dma(k.sp, k_[:, :, ks - (t0 - 128):ke - (t0 - 128)], kv[:, :, ks:ke], [self.KT[sn]], [k_])
                    kb0, kb1 = max(0, c * 4 - 1), min(nq, c * 4 + 5)
                    k.dma(k.sp, v_[:, kb0 - (c * 4 - 1):kb1 - (c * 4 - 1), :], vv[:, kb0:kb1, :], [self.VB[sn]], [v_])
                    for jq in range(4):
                        qb = c * 4 + jq
                        for kvh in range(2):
                            kbs = [kb for kb in (qb - 1, qb, qb + 1) if 0 <= kb < nq]
                            pO, pD = psO[u % 2], psD[u % 2]
                            den_, o_ = den[u % 2], o_sb[u % 2]
                            u += 1
                            for i, kb in enumerate(kbs):
                                bl = kb - (c * 4 - 1)
                                pS, pT_ = psS[si % 3], pT[si % 3]
                                si += 1
                                k.mm([dict(out=pS[:], lhsT=k_[:, kvh, bl * 128:(bl + 1) * 128],
                                           rhs=q_[:, kvh * 4:(kvh + 1) * 4, jq * 128:(jq + 1) * 128], start=True, stop=True)],
                                     R=[k_, q_], W=[pS])
                                k.op(k.act, lambda pS=pS, pT_=pT_: nc.scalar.activation(out=pT_[:], in_=pS[:], func=AF.Exp, scale=0.125),
                                     R=[pS], W=[pT_])
                                if kb != qb:
                                    mk = maskA if kb < qb else maskB
                                    k.op(k.dve, lambda pT_=pT_, mk=mk: nc.vector.tensor_tensor(
                                        out=sview(pT_, 0, [(128, 4), (1, 128)]), in0=sview(pT_, 0, [(128, 4), (1, 128)]),
                                        in1=sview(mk, 0, [(0, 4), (1, 128)]), op=ALU.mult), R=[pT_, mk], W=[pT_])
                                st, sp_ = (i == 0), (i == len(kbs) - 1)
                                k.mm([dict(out=pO[0:64, :], lhsT=v_[:, bl, kvh * 64:(kvh + 1) * 64], rhs=pT_[:], start=st, stop=sp_)],
                                     R=[v_, pT_], W=[pO])
                                k.mm([dict(out=pD[0:64, :], lhsT=ones[:, 0:64], rhs=pT_[:], start=st, stop=sp_)], R=[ones, pT_], W=[pD])
                            k.op(k.dve, lambda pD=pD, den_=den_, kvh=kvh: nc.vector.tensor_tensor(
                                out=sview(den_, 0, [(128, 4), (1, 128)], nparts=64), in0=sview(pD, 0, [(128, 4), (1, 128)], nparts=64),
                                in1=sview(snk, kvh * 4, [(1, 4), (0, 128)], nparts=64), op=ALU.add), R=[pD, snk], W=[den_])
                            k.op(k.dve, lambda den_=den_: nc.vector.reciprocal(out=den_[:], in_=den_[:]), R=[den_], W=[den_])
                            k.op(k.dve, lambda pO=pO, den_=den_, o_=o_: nc.vector.tensor_tensor(
                                out=o_[:], in0=pO[0:64, :], in1=den_[:], op=ALU.mult), R=[pO, den_], W=[o_])
                            k.op(k.pool, lambda o_=o_, kvh=kvh, jq=jq: nc.gpsimd.tensor_tensor(
                                out=mt_[:, kvh * 4:(kvh + 1) * 4, jq * 128:(jq + 1) * 128], in0=sview(o_, 0, [(128, 4), (1, 128)], nparts=64),
                                in1=ag_[:, kvh * 4:(kvh + 1) * 4, jq * 128:(jq + 1) * 128], op=ALU.mult), R=[o_, ag_], W=[mt_])
                    k.dma(k.pool, mv[:, :, t0:t0 + 512], mt_[:], [mt_], [self.MT[sn]])
            k.barrier()

    def phase_filter(self, li):
        k, nc = self.k, self.nc
        PI = float(np.pi)
        with contextlib.ExitStack() as es:
            def ldw(name, shape, src_ap):
                st = k.sb(es, name + "_f", shape, F32)
                k.dma(k.sp, st[:], src_ap, [self.w[name]], [st], slow=True)
                return st
            w1f = ldw("a_filt_w1", [33, 64], self.w["a_filt_w1"].t[li])
            w2f = ldw("a_filt_w2", [64, 64], self.w["a_filt_w2"].t[li])
            w3f = ldw("a_filt_w3", [64, 4 * HC], self.w["a_filt_w3"].t[li])
            w1 = k.sb(es, "w1b", [33, 64], BF16)
            w2 = k.sb(es, "w2b", [64, 64], BF16)
            w3 = k.sb(es, "w3b", [64, 4 * HC], BF16)
            k.op(k.dve, lambda: nc.vector.tensor_copy(out=w1[:], in_=w1f[:]), R=[w1f], W=[w1])
            k.op(k.dve, lambda: nc.vector.tensor_copy(out=w2[:], in_=w2f[:]), R=[w2f], W=[w2])
            k.op(k.act, lambda: nc.scalar.copy(out=w3[:], in_=w3f[:]), R=[w3f], W=[w3])
            fb = []
            for (fn, bn) in (("a_filt_f1", "a_filt_b1"), ("a_filt_f2", "a_filt_b2")):
                f_ = k.sb(es, fn, [64, 1], F32)
                b_ = k.sb(es, bn, [64, 1], F32)
                k.dma(k.sp, f_[:], dap(self.w[fn], li * 64, [[1, 64], [1, 1]]), [self.w[fn]], [f_], slow=True)
                k.dma(k.sp, b_[:], dap(self.w[bn], li * 64, [[1, 64], [1, 1]]), [self.w[bn]], [b_], slow=True)
                k.op(k.dve, lambda f_=f_, b_=b_: nc.vector.tensor_tensor(out=b_[:], in0=b_[:], in1=f_[:], op=ALU.mult), R=[f_, b_], W=[b_])
                fb.append((f_, b_))
            ndl = self.load_const(es, "negdelta", [128, 4], F32)
            fs = [k.sb(es, "fs%d" % i, [33, 512], F32) for i in range(2)]
            fsb = [k.sb(es, "fsb%d" % i, [33, 512], BF16) for i in range(2)]
            tr_ = [k.sb(es, "trw%d" % i, [128, 512], F32) for i in range(2)]
            pre = k.sb(es, "pre", [64, 512], F32)
            m1 = k.sb(es, "m1", [64, 512], F32)
            hb = [k.sb(es, "hb%d" % i, [64, 512], BF16) for i in range(2)]
            win = [k.sb(es, "win%d" % i, [128, 512], F32) for i in range(2)]
            ao = [k.sb(es, "ao%d" % i, [128, 512], BF16) for i in range(4)]
            dx = k.sb(es, "dxs", [128, 1], F32)
            psh = [k.ps(es, "psh%d" % i, [128, 512], F32) for i in range(2)]
            ps3 = [k.ps(es, "ps3%d" % i, [128, 512], F32) for i in range(4)]
            it = 0
            a3 = 0

            def sin_layer(p_, f_, b_, out_bf):
                k.op(k.dve, lambda: nc.vector.tensor_scalar(out=pre[:], in0=p_[0:64, :], scalar1=f_[:, 0:1], scalar2=b_[:, 0:1],
                                                            op0=ALU.mult, op1=ALU.add), R=[p_, f_, b_], W=[pre])
                k.op(k.dve, lambda: nc.vector.tensor_scalar(out=m1[:], in0=pre[:], scalar1=PI, scalar2=-2 * PI, op0=ALU.is_gt, op1=ALU.mult),
                     R=[pre], W=[m1])
                k.op(k.dve, lambda: nc.vector.tensor_tensor(out=pre[:], in0=pre[:], in1=m1[:], op=ALU.add), R=[pre, m1], W=[pre])
                k.op(k.dve, lambda: nc.vector.tensor_scalar(out=m1[:], in0=pre[:], scalar1=-PI, scalar2=2 * PI, op0=ALU.is_lt, op1=ALU.mult),
                     R=[pre], W=[m1])
                k.op(k.dve, lambda: nc.vector.tensor_tensor(out=pre[:], in0=pre[:], in1=m1[:], op=ALU.add), R=[pre, m1], W=[pre])
                k.op(k.act, lambda: nc.scalar.activation(out=out_bf[:], in_=pre[:], func=AF.Sin), R=[pre], W=[out_bf])

            for L in sorted(set([self.Lp, self.Ls])):
                A, Dx = self.A[L], self.Dx[L]
                for dr in range(2):
                    fT = self.c["featsT_%d" % L] if dr == 0 else self.c["featsTr_%d" % L]
                    for c in range(L // 512):
                        b = it % 2
                        it += 1
                        f_s, f_b, t_ = fs[b], fsb[b], tr_[b]
                        k.dma(k.sp, f_s[:], fT.t[:, c * 512:(c + 1) * 512], [fT], [f_s])
                        k.dma(k.sp, t_[:], dap(self.c["trow_%d" % L], dr * L + c * 512, [[0, 128], [1, 512]]), [self.c["trow_%d" % L]], [t_])
                        k.op(k.act, lambda: nc.scalar.copy(out=f_b[:], in_=f_s[:]), R=[f_s], W=[f_b])
                        p_ = psh[0]
                        k.mm([dict(out=p_[0:64, :], lhsT=w1[:], rhs=f_b[:], start=True, stop=True)], R=[w1, f_b], W=[p_])
                        sin_layer(p_, fb[0][0], fb[0][1], hb[0])
                        p_ = psh[1]
                        k.mm([dict(out=p_[0:64, :], lhsT=w2[:], rhs=hb[0][:], start=True, stop=True)], R=[w2, hb[0]], W=[p_])
                        sin_layer(p_, fb[1][0], fb[1][1], hb[1])
                        for cb in range(4):
                            w_ = win[cb % 2]
                            k.op(k.act, lambda w_=w_, cb=cb: nc.scalar.activation(out=w_[:], in_=t_[:], func=AF.Exp, scale=ndl[:, cb:cb + 1]),
                                 R=[t_, ndl], W=[w_])
                            for o in range(2):
                                p3 = ps3[a3 % 4]
                                a_ = ao[a3 % 4]
                                a3 += 1
                                col = o * 1024 + dr * 512 + cb * 128
                                k.mm([dict(out=p3[:], lhsT=w3[:, col:col + 128], rhs=hb[1][:], start=True, stop=True)], R=[w3, hb[1]], W=[p3])
                                k.op(k.dve, lambda p3=p3, a_=a_, w_=w_: nc.vector.scalar_tensor_tensor(
                                    out=a_[:], in0=w_[:], scalar=0.05, in1=p3[:], op0=ALU.add, op1=ALU.mult), R=[p3, w_], W=[a_])
                                row0 = o * HC + cb * 128
                                if dr == 0:
                                    k.dma(k.pool, dap(A, row0 * 2 * L + (L - 1) + c * 512, [[2 * L, 128], [1, 512]]), a_[:], [a_], [A])
                                else:
                                    last = (c == L // 512 - 1)
                                    n = 511 if last else 512
                                    k.dma(k.pool, dap(A, row0 * 2 * L + c * 512, [[2 * L, 128], [1, n]]), a_[:, 0:n], [a_], [A])
                                    if last:
                                        k.op(k.dve, lambda p3=p3, w_=w_: nc.vector.scalar_tensor_tensor(
                                            out=dx[:], in0=w_[:, 511:512], scalar=0.05, in1=p3[:, 511:512], op0=ALU.add, op1=ALU.mult),
                                            R=[p3, w_], W=[dx])
                                        k.dma(k.pool, dap(Dx, row0, [[1, 128], [1, 1]]), dx[:], [dx], [Dx], slow=True)
            k.barrier()

    def phase_hyena(self, li):
        k, nc = self.k, self.nc
        CH = 32
        groups = {}
        for (sn, L) in self.seqs:
            groups.setdefault(L, []).append(sn)
        NBM = max(len(v) * (L // 128) for L, v in groups.items())
        GP = self.GP
        with contextlib.ExitStack() as es:
            anti = self.load_const(es, "antiid", [128, 128], BF16)
            zero = k.sb(es, "zero", [128, 128], BF16)
            k.op(k.dve, lambda: nc.vector.memset(zero[:], 0.0), W=[zero])
            x1 = k.sb(es, "x1s", [128, NBM * CH], F32)
            x2 = k.sb(es, "x2s", [128, NBM * CH], F32)
            z0 = k.sb(es, "z0s", [128, NBM * CH], F32)
            z1 = k.sb(es, "z1s", [128, NBM * CH], F32)
            hg = k.sb(es, "hgs", [128, NBM * CH], F32)
            zb = k.sb(es, "zbs", [128, NBM * CH], BF16)
            zr = k.sb(es, "zrs", [128, NBM * CH], BF16)
            ho = k.sb(es, "hos", [128, NBM * CH], BF16)
            tmp = [k.sb(es, "tmpc%d" % i, [128, NBM], F32) for i in range(2)]
            G = [k.sb(es, "G%d" % i, [128, GP], BF16) for i in range(3)]
            deff = k.sb(es, "deff", [128, 2, HC], F32)
            dxb = k.sb(es, "dxb", [128, 2, HC], F32)
            psr = [k.ps(es, "psr%d" % i, [128, 512], F32) for i in range(2)]
            psc = [k.ps(es, "psc%d" % i, [128, 512], F32) for i in range(4)]
            gi = 0
            pc_i = 0
            for L, sns in groups.items():
                nb = L // 128
                ns_ = len(sns)
                tot = ns_ * nb * CH
                A, Dx = self.A[L], self.Dx[L]
                ncol = 2 * L - 128
                npieces = (ncol + GP - 1) // GP

                def v4(res, cc, j0=0, j1=nb):
                    return sview(res, j0 * ns_ * CH + cc, [(CH, (j1 - j0) * ns_)])
                k.dma(k.sp, deff[:], dap(self.w["a_hyena_d"], li * 2 * HC, [[0, 128], [1, 2 * HC]]), [self.w["a_hyena_d"]], [deff])
                k.dma(k.sp, dxb[:], dap(Dx, 0, [[0, 128], [1, 2 * HC]]), [Dx], [dxb])
                k.op(k.dve, lambda: nc.vector.tensor_tensor(out=deff[:], in0=deff[:], in1=dxb[:], op=ALU.add), R=[deff, dxb], W=[deff])
                for c0 in range(0, HC, CH):
                    for si_, sn in enumerate(sns):
                        hyv = self.HY[sn].t.rearrange("(b p) c -> p b c", p=128)
                        hgv = self.HG[sn].t.rearrange("(b p) c -> p b c", p=128)
                        for b0 in range(0, nb, 16):
                            b1 = min(nb, b0 + 16)
                            o_ = (b0 * ns_ + si_) * CH
                            dst = lambda r: sview(r, o_, [(ns_ * CH, b1 - b0), (1, CH)])
                            k.dma(k.sp, dst(x1), hyv[:, b0:b1, c0:c0 + CH], [self.HY[sn]], [x1])
                            k.dma(k.sp, dst(x2), hyv[:, b0:b1, HC + c0:HC + c0 + CH], [self.HY[sn]], [x2])
                            k.dma(k.sp, dst(z0), hyv[:, b0:b1, 2 * HC + c0:2 * HC + c0 + CH], [self.HY[sn]], [z0])
                            k.dma(k.sp, dst(hg), hgv[:, b0:b1, c0:c0 + CH], [self.HG[sn]], [hg])
                    for o in range(2):
                        zin = z0 if o == 0 else z1
                        k.op(k.act, lambda zin=zin: nc.scalar.copy(out=zb[:, 0:tot], in_=zin[:, 0:tot]), R=[zin], W=[zb])
                        for f0 in range(0, tot, 512):
                            n = min(512, tot - f0)
                            pr = psr[(f0 // 512) % 2]
                            k.mm([dict(out=pr[:, 0:n], lhsT=anti[:], rhs=zb[:, f0:f0 + n], start=True, stop=True)], R=[anti, zb], W=[pr])
                            k.op(k.dve, lambda pr=pr, f0=f0, n=n: nc.vector.tensor_copy(out=zr[:, f0:f0 + n], in_=pr[:, 0:n]),
                                 R=[pr], W=[zr])
                        for cc in range(CH):
                            ch = c0 + cc
                            pcs = psc[pc_i % 4]
                            pc_i += 1

                            def pv(i0=0, n=nb):
                                return pcs[:, i0 * ns_:(i0 + n) * ns_]
                            k.mm([dict(out=pv(), lhsT=zero[:], rhs=v4(zr, cc), start=True, stop=False)], R=[zero, zr], W=[pcs])
                            for pc in range(npieces):
                                g_ = G[gi % 3]
                                gi += 1
                                w_ = min(GP, ncol - pc * GP)
                                base = (o * HC + ch) * 2 * L + pc * GP
                                k.dma(k.sp, g_[:, 0:w_], dap(A, base, [[1, 128], [1, w_]]), [A], [g_])
                                mms = []
                                for col in range(pc * GP, pc * GP + w_, 128):
                                    Dd = (col - (L - 128)) // 128
                                    if Dd >= 0:
                                        j0, j1, i0 = 0, nb - Dd, Dd
                                    else:
                                        j0, j1, i0 = -Dd, nb, 0
                                    mms.append(dict(out=pv(i0, j1 - j0), lhsT=g_[:, col - pc * GP:col - pc * GP + 128],
                                                    rhs=v4(zr, cc, j0, j1), start=False, stop=(col + 128 >= ncol)))
                                k.mm(mms, R=[g_, zr], W=[pcs])
                            t_ = tmp[cc % 2]
                            tv = t_[:, 0:nb * ns_]
                            k.op(k.dve, lambda tv=tv, zin=zin, cc=cc, o=o, ch=ch, pcs=pcs: nc.vector.scalar_tensor_tensor(
                                out=tv, in0=v4(zin, cc), scalar=deff[:, o, ch:ch + 1], in1=pcs[:, 0:nb * ns_],
                                op0=ALU.mult, op1=ALU.add), R=[zin, deff, pcs], W=[t_])
                            if o == 0:
                                k.op(k.dve, lambda tv=tv, cc=cc: nc.vector.tensor_tensor(
                                    out=v4(z1, cc), in0=tv, in1=v4(x1, cc), op=ALU.mult), R=[t_, x1], W=[z1])
                            else:
                                k.op(k.pool, lambda tv=tv, cc=cc: nc.gpsimd.tensor_tensor(
                                    out=tv, in0=tv, in1=v4(x2, cc), op=ALU.mult), R=[t_, x2], W=[t_])
                                k.op(k.pool, lambda tv=tv, cc=cc: nc.gpsimd.tensor_tensor(
                                    out=v4(ho, cc), in0=tv, in1=v4(hg, cc), op=ALU.mult), R=[t_, hg], W=[ho])
                    for si_, sn in enumerate(sns):
                        hov = self.HYO[sn].t.rearrange("(b p) c -> p b c", p=128)
                        for b0 in range(0, nb, 16):
                            b1 = min(nb, b0 + 16)
                            o_ = (b0 * ns_ + si_) * CH
                            k.dma(k.pool, hov[:, b0:b1, c0:c0 + CH], sview(ho, o_, [(ns_ * CH, b1 - b0), (1, CH)]), [ho], [self.HYO[sn]])
            k.barrier()

    def phase_hy_transpose(self):
        k, nc = self.k, self.nc
        with contextlib.ExitStack() as es:
            ident = self.load_const(es, "ident", [128, 128], BF16)
            hi = [k.sb(es, "hi%d" % i, [128, 4, HC], BF16) for i in range(2)]
            mo = [k.sb(es, "mo%d" % i, [128, 4, 512], BF16) for i in range(2)]
            pst = [k.ps(es, "ptt%d" % i, [128, 8, 128], BF16) for i in range(4)]
            it = 0
            for (sn, L) in self.seqs:
                hv = self.HYO[sn].t.rearrange("(c j p) d -> c p j d", j=4, p=128)
                mv = self.MT[sn].t[0:512, :].rearrange("(b p) t -> p b t", p=128)
                for c in range(L // 512):
                    b = it % 2
                    it += 1
                    hi_, mo_ = hi[b], mo[b]
                    k.dma(k.sp, hi_[:], hv[c], [self.HYO[sn]], [hi_])
                    for j in range(4):
                        pt = pst[(it * 4 + j) % 4]
                        k.tr([(pt[:, blk, :], hi_[:, j, blk * 128:(blk + 1) * 128], ident[:]) for blk in range(4)], R=[hi_, ident], W=[pt])
                        if j % 2:
                            k.op(k.act, lambda pt=pt, j=j: nc.scalar.copy(out=mo_[:, :, j * 128:(j + 1) * 128], in_=pt[:, 0:4, :]), R=[pt], W=[mo_])
                        else:
                            k.op(k.dve, lambda pt=pt, j=j: nc.vector.tensor_copy(out=mo_[:, :, j * 128:(j + 1) * 128], in_=pt[:, 0:4, :]), R=[pt], W=[mo_])
                    k.dma(k.pool, mv[:, :, c * 512:(c + 1) * 512], mo_[:], [mo_], [self.MT[sn]])
            k.barrier()

    def phase_init_pads(self):
        k, nc = self.k, self.nc
        with contextlib.ExitStack() as es:
            zt = k.sb(es, "zt", [128, 8, 1024], BF16)
            k.op(k.dve, lambda: nc.vector.memset(zt[:], 0.0), W=[zt])
            for (sn, L) in self.seqs:
                for g, dl in enumerate((1, 4, 16)):
                    pad = 64 * dl
                    kt = self.KTc[sn, g]
                    kv = kt.t.rearrange("(h p) t -> p h t", p=128)
                    k.dma(k.pool, kv[:, :, 0:pad], zt[:, :, 0:pad], [zt], [kt])
                    k.dma(k.pool, kv[:, :, pad + L:pad + L + pad], zt[:, :, 0:pad], [zt], [kt])
                    vt = self.Vc[sn, g]
                    for r0 in (0, pad + L):
                        if pad >= 128:
                            k.dma(k.pool, vt.t[r0:r0 + pad, :].rearrange("(b p) c -> p b c", p=128), zt[:, 0:pad // 128, :], [zt], [vt])
                        else:
                            k.dma(k.pool, vt.t[r0:r0 + pad, :], zt[0:pad, 0, :], [zt], [vt])
            k.barrier()

    def phase_odd_proj(self, li, g):
        k, nc = self.k, self.nc
        W = self.w["c_w_in"]
        dl = (1, 4, 16)[g]
        pad = 64 * dl
        ncol = 3072 + (1024 if g == 0 else 0)
        with contextlib.ExitStack() as es:
            ident = self.load_const(es, "ident", [128, 128], BF16)
            Wsb = k.sb(es, "Wc", [128, 8, ncol], BF16)
            with contextlib.ExitStack() as es2:
                gsb = k.sb(es2, "gsb", [128, 8, 1], F32)
                k.dma(k.sp, gsb[:], dap(self.w["c_norm"], li * D, [[1, 128], [128, 8], [1, 1]]), [self.w["c_norm"]], [gsb], slow=True)
                stage = [k.sb(es2, "stage%d" % i, [128, 8, 256], F32) for i in range(2)]
                wv = W.t[li].rearrange("(k p) c -> p k c", p=128)
                for cb in range(ncol // 256):
                    c0 = cb * 256
                    src0 = g * 3072 + c0 if c0 < 3072 else 9216 + (c0 - 3072)
                    st = stage[cb % 2]
                    k.dma(k.sp, st[:], wv[:, :, src0:src0 + 256], [W], [st])
                    k.op(k.dve, lambda st=st, c0=c0: nc.vector.tensor_tensor(
                        out=Wsb[:, :, c0:c0 + 256], in0=st[:], in1=sview(gsb, 0, [(1, 8), (0, 256)]), op=ALU.mult), R=[st, gsb], W=[Wsb])
                k.barrier()
            xc = [k.sb(es, "xc%d" % i, [128, 8, 512], BF16) for i in range(2)]
            qbf = [k.sb(es, "qbf%d" % i, [128, 512], BF16) for i in range(2)]
            qTs = [k.sb(es, "qTs%d" % i, [128, 8, 512], BF16) for i in range(2)]
            kTs = [k.sb(es, "kTs%d" % i, [128, 8, 512], BF16) for i in range(2)]
            vo = [k.sb(es, "vo%d" % i, [128, 4, 1024], BF16) for i in range(2)]
            agT = [k.sb(es, "agT%d" % i, [128, 8, 512], BF16) for i in range(2)]
            cs = [k.sb(es, "cs%d" % i, [128, 4, 64], F32) for i in range(2)]
            tmpa = [k.sb(es, "tmpa%d" % i, [128, 128], F32) for i in range(2)]
            tmpb = [k.sb(es, "tmpb%d" % i, [128, 128], F32) for i in range(2)]
            psA = [k.ps(es, "psA%d" % i, [128, 512], F32) for i in range(6)]
            psT = [k.ps(es, "psT%d" % i, [128, 8, 128], BF16) for i in range(2)]
            pa = [0]

            def nps():
                pa[0] += 1
                return psA[pa[0] % 6]
            it = 0
            tt = 0
            for (sn, L) in self.seqs:
                xt = self.xnT[sn]
                xt_v = xt.t.rearrange("k p t -> p k t")
                rope_v = self.c["rope16"].t.rearrange("(c j p) r -> c p j r", j=4, p=128)
                for c in range(L // 512):
                    b = it % 2
                    it += 1
                    x_, cs_ = xc[b], cs[b]
                    k.dma(k.sp, x_[:], xt_v[:, :, 2 + c * 512:2 + (c + 1) * 512], [xt], [x_])
                    k.dma(k.sp, cs_[:], rope_v[c], [self.c["rope16"]], [cs_])
                    qT_, kT_, vo_, ag_ = qTs[b], kTs[b], vo[b], agT[b]
                    for j in range(4):
                        for tq, dstT in ((0, qT_), (1, kT_)):
                            for hg_ in range(2):
                                tb_ = tt % 2
                                tt += 1
                                p_ = nps()
                                c0 = tq * 1024 + hg_ * 512
                                k.mm([dict(out=p_[:], lhsT=x_[:, kk, j * 128:(j + 1) * 128], rhs=Wsb[:, kk, c0:c0 + 512],
                                           start=(kk == 0), stop=(kk == 7)) for kk in range(8)], R=[x_, Wsb], W=[p_])
                                q_ = qbf[tb_]
                                cc = sview(cs_, j * 64, [(0, 4), (1, 32)])
                                s1 = sview(cs_, j * 64 + 32, [(0, 4), (1, 16)])
                                s2 = sview(cs_, j * 64 + 48, [(0, 4), (1, 16)])
                                self.rope_tm(p_, 4, 128, 32, cc, (s1, s2), q_, 0, tmpa[tb_], tmpb[tb_], extra_R=[cs_])
                                pt = psT[tb_]
                                k.tr([(pt[:, blk, :], q_[:, blk * 128:(blk + 1) * 128], ident[:]) for blk in range(4)], R=[q_, ident], W=[pt])
                                k.op(k.act if hg_ else k.dve, lambda pt=pt, j=j, hg_=hg_, dstT=dstT: (nc.scalar.copy if hg_ else nc.vector.tensor_copy)(
                                    out=dstT[:, hg_ * 4:(hg_ + 1) * 4, j * 128:(j + 1) * 128], in_=pt[:, 0:4, :]), R=[pt], W=[dstT])
                        for vg in range(2):
                            p_ = nps()
                            c0 = 2048 + vg * 512
                            k.mm([dict(out=p_[:], lhsT=x_[:, kk, j * 128:(j + 1) * 128], rhs=Wsb[:, kk, c0:c0 + 512],
                                       start=(kk == 0), stop=(kk == 7)) for kk in range(8)], R=[x_, Wsb], W=[p_])
                            k.op(k.act, lambda p_=p_, j=j, vg=vg: nc.scalar.copy(out=vo_[:, j, vg * 512:(vg + 1) * 512], in_=p_[:]), R=[p_], W=[vo_])
                    if g == 0:
                        for cb in range(8):
                            p_ = nps()
                            k.mm([dict(out=p_[:], lhsT=Wsb[:, kk, 3072 + cb * 128:3072 + (cb + 1) * 128], rhs=x_[:, kk, :],
                                       start=(kk == 0), stop=(kk == 7)) for kk in range(8)], R=[x_, Wsb], W=[p_])
                            k.op(k.act, lambda p_=p_, cb=cb: nc.scalar.activation(out=ag_[:, cb, :], in_=p_[:], func=AF.Silu), R=[p_], W=[ag_])
                        k.dma(k.pool, self.CGT[sn].t.rearrange("(b p) t -> p b t", p=128)[:, :, c * 512:(c + 1) * 512], ag_[:], [ag_], [self.CGT[sn]])
                    sl = slice(c * 512, (c + 1) * 512)
                    k.dma(k.pool, self.QTc[sn, g].t.rearrange("(b p) t -> p b t", p=128)[:, :, sl], qT_[:], [qT_], [self.QTc[sn, g]])
                    k.dma(k.pool, self.KTc[sn, g].t.rearrange("(b p) t -> p b t", p=128)[:, :, pad + c * 512:pad + (c + 1) * 512], kT_[:], [kT_],
                          [self.KTc[sn, g]])
                    k.dma(k.pool, self.Vc[sn, g].t[pad + c * 512:pad + (c + 1) * 512, :].rearrange("(j p) d -> p j d", p=128), vo_[:], [vo_],
                          [self.Vc[sn, g]])
            k.barrier()

    def phase_dilated_attn(self):
        k, nc = self.k, self.nc
        CT = 2048
        SC = float(128 ** -0.5)
        with contextlib.ExitStack() as es:
            masks = k.sb(es, "masks", [128, 2, 128], BF16)
            k.dma(k.sp, masks[:, 0, :], self.c["maskA"].t[:, :], [self.c["maskA"]], [masks])
            k.dma(k.sp, masks[:, 1, :], self.c["maskB"].t[:, :], [self.c["maskB"]], [masks])
            ones = self.load_const(es, "ones", [128, 128], BF16)
            vlo = self.load_const(es, "vlo", [128, 128], BF16)
            vhi = self.load_const(es, "vhi", [128, 128], BF16)
            q_sb = [k.sb(es, "dq%d" % i, [128, CT], BF16) for i in range(2)]
            k_sb = [k.sb(es, "dk%d" % i, [128, CT + 2048], BF16) for i in range(2)]
            v_sb = [k.sb(es, "dv%d" % i, [128, 32, 128], BF16) for i in range(2)]
            g_sb = [k.sb(es, "dg%d" % i, [128, CT], BF16) for i in range(2)]
            accn = [k.sb(es, "accn%d" % i, [128, CT], F32) for i in range(2)]
            accd = [k.sb(es, "accd%d" % i, [128, CT], F32) for i in range(2)]
            mo = [k.sb(es, "dmo%d" % i, [128, CT], BF16) for i in range(2)]
            pT = [k.sb(es, "dpT%d" % i, [128, 512], BF16) for i in range(3)]
            psS = [k.ps(es, "dpsS%d" % i, [128, 512], F32) for i in range(3)]
            psO = [k.ps(es, "dpsO%d" % i, [128, 512], F32) for i in range(3)]
            hi = 0
            li_ = 0
            si = 0
            for (sn, L) in self.seqs:
                for t0 in range(0, L, CT):
                    for h in range(8):
                        an, ad, g_, mo_ = accn[hi % 2], accd[hi % 2], g_sb[hi % 2], mo[hi % 2]
                        hi += 1
                        k.dma(k.sp, g_[:], self.CGT[sn].t[h * 128:(h + 1) * 128, t0:t0 + CT], [self.CGT[sn]], [g_])
                        for g, dl in enumerate((1, 4, 16)):
                            pad = 64 * dl
                            Ls_ = L // dl
                            q_, k_, v_ = q_sb[li_ % 2], k_sb[li_ % 2], v_sb[li_ % 2]
                            li_ += 1
                            k.dma(k.sp, q_[:], self.QTc[sn, g].t[h * 128:(h + 1) * 128, t0:t0 + CT], [self.QTc[sn, g]], [q_])
                            k.dma(k.sp, k_[:, 0:CT + 2 * pad], self.KTc[sn, g].t[h * 128:(h + 1) * 128, t0:t0 + CT + 2 * pad],
                                  [self.KTc[sn, g]], [k_])
                            nblk = CT // (128 * dl)
                            nm = nblk + 1
                            V = self.Vc[sn, g]
                            for r in range(dl):
                                k.dma(k.sp, v_[:, r * nm:(r + 1) * nm, :],
                                      dap(V, (t0 + r) * D + h * 128, [[dl * D, 128], [128 * dl * D, nm], [1, 128]]), [V], [v_])
                            if dl == 16:
                                pairs = [((r, 0), (r + 1, 0)) for r in range(0, 16, 2)]
                            else:
                                pairs = [((r, b), (r, b + 1)) for r in range(dl) for b in range(0, nblk, 2)]
                            for pr in pairs:
                                pS, pT_ = psS[si % 3], pT[si % 3]
                                pO = psO[si % 3]
                                si += 1
                                mms = []
                                for ti, (r, b) in enumerate(pr):
                                    qcol = 128 * b * dl + r
                                    for kt in range(2):
                                        kcol = 128 * (b + kt) * dl + r
                                        mms.append(dict(out=pS[:, (ti * 2 + kt) * 128:(ti * 2 + kt + 1) * 128],
                                                        lhsT=sview(k_, kcol, [(dl, 128)]), rhs=sview(q_, qcol, [(dl, 128)]),
                                                        start=True, stop=True))
                                k.mm(mms, R=[k_, q_], W=[pS])
                                k.op(k.act, lambda pS=pS, pT_=pT_: nc.scalar.activation(out=pT_[:], in_=pS[:], func=AF.Exp, scale=SC),
                                     R=[pS], W=[pT_])
                                k.op(k.dve, lambda pT_=pT_: nc.vector.tensor_tensor(
                                    out=sview(pT_, 0, [(256, 2), (1, 256)]), in0=sview(pT_, 0, [(256, 2), (1, 256)]),
                                    in1=sview(masks, 0, [(0, 2), (1, 256)]), op=ALU.mult), R=[pT_, masks], W=[pT_])
                                mms = []
                                for ti, (r, b) in enumerate(pr):
                                    for kt in range(2):
                                        mms.append(dict(out=pO[:, ti * 128:(ti + 1) * 128], lhsT=v_[:, r * nm + b + kt, :],
                                                        rhs=pT_[:, (ti * 2 + kt) * 128:(ti * 2 + kt + 1) * 128], start=(kt == 0), stop=(kt == 1)))
                                for ti, (r, b) in enumerate(pr):
                                    for kt in range(2):
                                        s_lo = (t0 // dl) + 128 * (b + kt) - 64
                                        vl = vlo if s_lo < 0 else (vhi if s_lo + 128 > Ls_ else ones)
                                        mms.append(dict(out=pO[:, 256 + ti * 128:256 + (ti + 1) * 128], lhsT=vl[:],
                                                        rhs=pT_[:, (ti * 2 + kt) * 128:(ti * 2 + kt + 1) * 128], start=(kt == 0), stop=(kt == 1)))
                                k.mm(mms, R=[v_, pT_, ones, vlo, vhi], W=[pO])
                                (r0, b0), (r1, b1) = pr
                                c0 = 128 * b0 * dl + r0
                                step = (128 * b1 * dl + r1) - c0
                                av_n = sview(an, c0, [(step, 2), (dl, 128)])
                                av_d = sview(ad, c0, [(step, 2), (dl, 128)])
                                pn = sview(pO, 0, [(128, 2), (1, 128)])
                                pd = sview(pO, 256, [(128, 2), (1, 128)])
                                if g == 0:
                                    k.op(k.dve, lambda av_n=av_n, pn=pn: nc.vector.tensor_copy(out=av_n, in_=pn), R=[pO], W=[an])
                                    k.op(k.act, lambda av_d=av_d, pd=pd: nc.scalar.copy(out=av_d, in_=pd), R=[pO], W=[ad])
                                else:
                                    k.op(k.dve, lambda av_n=av_n, pn=pn: nc.vector.tensor_tensor(out=av_n, in0=av_n, in1=pn, op=ALU.add),
                                         R=[pO, an], W=[an])
                                    k.op(k.dve, lambda av_d=av_d, pd=pd: nc.vector.tensor_tensor(out=av_d, in0=av_d, in1=pd, op=ALU.add),
                                         R=[pO, ad], W=[ad])
                        k.op(k.dve, lambda ad=ad: nc.vector.reciprocal(out=ad[:], in_=ad[:]), R=[ad], W=[ad])
                        k.op(k.pool, lambda an=an, ad=ad: nc.gpsimd.tensor_tensor(out=an[:], in0=an[:], in1=ad[:], op=ALU.mult), R=[an, ad], W=[an])
                        k.op(k.pool, lambda an=an, g_=g_, mo_=mo_: nc.gpsimd.tensor_tensor(out=mo_[:], in0=an[:], in1=g_[:], op=ALU.mult),
                             R=[an, g_], W=[mo_])
                        k.dma(k.pool, self.MT[sn].t[h * 128:(h + 1) * 128, t0:t0 + CT], mo_[:], [mo_], [self.MT[sn]])
            k.barrier()

    def build_all(self):
        self.phase_init_pads()
        cur = self.x_in
        bufs = [self.xa, self.xb]
        for layer in range(self.depth):
            nxt = bufs[layer % 2]
            li = layer // 2
            self.phase_norm(cur)
            if layer % 2 == 0:
                self.phase_even_proj(li)
                self.phase_filter(li)
                self.phase_hyena(li)
                self.phase_hy_transpose()
                self.phase_band_attn(li)
                self.phase_outproj("a_w_out", li, cur, nxt)
            else:
                for g in range(3):
                    self.phase_odd_proj(li, g)
                self.phase_dilated_attn()
                self.phase_outproj("c_w_out", li, cur, nxt)
            cur = nxt
        self.phase_norm(cur, final=True)
        self.finish()


_PROG_CACHE = {}


def kernel(**inputs):
    x_prompt = np.asarray(inputs["x_prompt"], dtype=np.float32)
    x_sample = np.asarray(inputs["x_sample"], dtype=np.float32)
    Lp, Ls = x_prompt.shape[1], x_sample.shape[1]
    nsamp = x_sample.shape[0]
    ns = nsamp // NCORES
    key = (Lp, Ls, ns)
    if key not in _PROG_CACHE:
        P = Prog(Lp, Ls, ns, depth=4)
        P.build_all()
        _PROG_CACHE[key] = P
    P = _PROG_CACHE[key]
    in_maps = [P.in_map(x_prompt[0], x_sample[c * ns:(c + 1) * ns], inputs) for c in range(NCORES)]
    res = run_bass_kernel_spmd(P.nc, in_maps, core_ids=list(range(NCORES)))
    y_prompt = np.asarray(res.results[0]["y_p"], dtype=np.float32)[None]
    y_sample = np.stack([np.asarray(res.results[c]["y_s%d" % i], dtype=np.float32) for c in range(NCORES) for i in range(ns)])
    return (y_prompt, y_sample)
```

```python
import contextlib
import math
import numpy as np
import ml_dtypes
import concourse.bass as bass
import concourse.mybir as mybir
from concourse.bass_utils import run_bass_kernel_spmd

F32, BF16 = mybir.dt.float32, mybir.dt.bfloat16
AF = mybir.ActivationFunctionType
ALU = mybir.AluOpType
AX = mybir.AxisListType

D = 1024
HC = 512
NCORES = 8
EVEN_IN = 3328
ODD_IN = 10240
EPS = 1e-6
SAFE_SAME_ENGINE = True


class Res:
    __slots__ = ("name", "t", "writers", "readers", "dsem", "is_dram", "is_psum")

    def __init__(self, name, t, is_dram=False):
        self.name, self.t = name, t
        self.writers = {}
        self.readers = {}
        self.dsem = {}
        self.is_dram = is_dram
        self.is_psum = False

    def __getitem__(self, idx):
        return self.t[idx]


class DSem:
    __slots__ = ("h", "cnt")

    def __init__(self, h):
        self.h, self.cnt = h, 0


class EngQ:
    def __init__(self, nc, eng, name, is_pe=False):
        self.eng, self.name, self.is_pe = eng, name, is_pe
        self.sem = nc.alloc_semaphore("sem_" + name)
        self.n = 0
        self.seen = {}


class K:
    def __init__(self, nc):
        self.nc = nc
        self.pe = EngQ(nc, nc.tensor, "pe", True)
        self.act = EngQ(nc, nc.scalar, "act")
        self.dve = EngQ(nc, nc.vector, "dve")
        self.pool = EngQ(nc, nc.gpsimd, "pool")
        self.sp = EngQ(nc, nc.sync, "sp")
        self.engs = [self.pe, self.act, self.dve, self.pool, self.sp]
        self.free_dsems = {}
        self.all_dsems = []
        self.dsem_of = {}
        self.live = []
        self.ninst = 0
        self.uid = 0

    def sb(self, es, name, shape, dt):
        self.uid += 1
        name = "%s_%d" % (name, self.uid)
        t = es.enter_context(self.nc.sbuf_tensor(name, list(shape), dt))
        r = Res(name, t)
        self.live.append(r)
        return r

    def ps(self, es, name, shape, dt=F32):
        self.uid += 1
        name = "%s_%d" % (name, self.uid)
        t = es.enter_context(self.nc.psum_tensor(name, list(shape), dt))
        r = Res(name, t)
        r.is_psum = True
        self.live.append(r)
        return r

    def dram(self, name, shape, dt, kind="Internal"):
        t = self.nc.dram_tensor(name, list(shape), dt, kind=kind)
        return Res(name, t.ap(), is_dram=True)

    def _get_dsem(self, r, q):
        d = r.dsem.get(q.name)
        if d is None:
            fl = self.free_dsems.setdefault(q.name, [])
            if fl:
                d = fl.pop()
            else:
                h = self.nc.alloc_semaphore("dsem%d" % len(self.all_dsems))
                d = DSem(h)
                self.all_dsems.append(d)
                self.dsem_of[h] = d
            r.dsem[q.name] = d
        return d

    def _wait(self, q, sem, val):
        if sem is q.sem and (q.is_pe or not SAFE_SAME_ENGINE):
            return
        ds = self.dsem_of.get(sem)
        if ds is not None:
            val = ds.cnt
        if q.seen.get(sem, 0) >= val:
            return
        q.eng.wait_ge(sem, val)
        q.seen[sem] = val
        self.ninst += 1

    @staticmethod
    def _rw(R, W):
        R2 = [r for r in R if not r.is_psum]
        W2 = list(W) + [r for r in R if r.is_psum and r not in W]
        return R2, W2

    def _deps(self, q, R, W, dma_sem=None):
        for r in R:
            for s, v in r.writers.items():
                self._wait(q, s, v)
        for w in W:
            for s, v in w.writers.items():
                if dma_sem is not None and s is dma_sem and not w.readers:
                    continue
                self._wait(q, s, v)
            for s, v in w.readers.items():
                self._wait(q, s, v)

    def _mark(self, tok, R, W):
        s, v = tok
        for r in R:
            if r.readers.get(s, 0) < v:
                r.readers[s] = v
        for w in W:
            if w.is_dram:
                w.writers[s] = v
            else:
                w.writers = {s: v}
                w.readers = {}

    def op(self, q, fn, R=(), W=()):
        R, W = self._rw(R, W)
        self._deps(q, R, W)
        ins = fn()
        q.n += 1
        ins.then_inc(q.sem, 1)
        self.ninst += 1
        self._mark((q.sem, q.n), R, W)

    def mm(self, mms, R=(), W=()):
        q = self.pe
        self._deps(q, R, W)
        ins = None
        for kw in mms:
            ins = self.nc.tensor.matmul(**kw)
        self.ninst += len(mms)
        q.n += 1
        ins.then_inc(q.sem, 1)
        self._mark((q.sem, q.n), R, W)

    def tr(self, trs, R=(), W=()):
        q = self.pe
        self._deps(q, R, W)
        ins = None
        for (o, i, ident) in trs:
            ins = self.nc.tensor.transpose(out=o, in_=i, identity=ident)
        self.ninst += len(trs)
        q.n += 1
        ins.then_inc(q.sem, 1)
        self._mark((q.sem, q.n), R, W)

    def dma(self, q, out, in_, R, W, slow=False):
        assert len(W) == 1
        owner = W[0] if not W[0].is_dram else R[0]
        assert not owner.is_dram
        ds = self._get_dsem(owner, q)
        self._deps(q, R, W, dma_sem=ds.h)
        ds.cnt += 16
        q.eng.dma_start(out=out, in_=in_, allow_slow_non_contiguous=slow).then_inc(ds.h, 16)
        self.ninst += 1
        self._mark((ds.h, ds.cnt), R, W)

    def barrier(self):
        toks = [(e.sem, e.n) for e in self.engs if e.n > 0]
        toks += [(d.h, d.cnt) for d in self.all_dsems if d.cnt > 0]
        for q in self.engs:
            for tok in toks:
                if tok[0] is q.sem:
                    continue
                if q.seen.get(tok[0], 0) >= tok[1]:
                    continue
                q.eng.wait_ge(tok[0], tok[1])
                q.seen[tok[0]] = tok[1]
        for r in self.live:
            for qn, d in r.dsem.items():
                self.free_dsems[qn].append(d)
            r.dsem = {}
        self.live = []


class Pipe:
    def __init__(self, depth=1):
        self.q, self.depth = [], depth

    def push(self, fn):
        self.q.append(fn)
        while len(self.q) > self.depth:
            self.q.pop(0)()

    def flush(self):
        while self.q:
            self.q.pop(0)()


def dap(res, offset, pairs):
    return bass.AP(res.t.tensor, offset, [list(p) for p in pairs])


def rope_table(L, rot):
    half = rot // 2
    inv = np.power(np.float32(500000.0), -2.0 * np.arange(half, dtype=np.float32) / np.float32(rot)).astype(np.float32)
    ang = (np.arange(L, dtype=np.float32)[:, None] * inv[None, :]).astype(np.float32)
    co, si = np.cos(ang), np.sin(ang)
    return np.concatenate([co, co, -si, si], axis=1).astype(np.float32)


def filter_feats(L):
    t = np.linspace(0.0, 1.0, L, dtype=np.float32)[:, None]
    wpos = (2.0 * math.pi * np.arange(L, dtype=np.float32)[:, None] / L).astype(np.float32)
    bands = np.linspace(1e-4, 15, 16, dtype=np.float32)[None, :]
    feats = np.concatenate([t, np.cos(bands * wpos), -np.sin(bands * wpos)], axis=-1).astype(np.float32)
    return feats


def decay_deltas():
    max_decay = math.log(1e-2) / 0.3
    min_decay = math.log(1e-2) / 1.5
    return np.abs(np.linspace(min_decay, max_decay, HC, dtype=np.float32)).astype(np.float32)


def const_tables(Ls_list):
    c = {}
    ident = np.eye(128, dtype=np.float32)
    c["ident"] = ident.astype(ml_dtypes.bfloat16)
    c["antiid"] = ident[::-1].copy().astype(ml_dtypes.bfloat16)
    j = np.arange(128)[:, None]
    i = np.arange(128)[None, :]
    c["maskA"] = (j >= i).astype(np.float32).astype(ml_dtypes.bfloat16)
    c["maskB"] = (j <= i).astype(np.float32).astype(ml_dtypes.bfloat16)
    lo = np.zeros((128, 128), np.float32)
    lo[64:, :] = 1.0
    hi = np.zeros((128, 128), np.float32)
    hi[:64, :] = 1.0
    c["ones"] = np.ones((128, 128), np.float32).astype(ml_dtypes.bfloat16)
    c["vlo"] = lo.astype(ml_dtypes.bfloat16)
    c["vhi"] = hi.astype(ml_dtypes.bfloat16)
    Lmax = max(Ls_list)
    c["rope8"] = rope_table(Lmax, 16)
    c["rope16"] = rope_table(Lmax, 32)
    for L in sorted(set(Ls_list)):
        f = filter_feats(L)
        c["featsT_%d" % L] = np.ascontiguousarray(f.T)
        c["featsTr_%d" % L] = np.ascontiguousarray(f[::-1].T)
        t = np.linspace(0.0, 1.0, L, dtype=np.float32)
        c["trow_%d" % L] = np.stack([t, t[::-1]]).astype(np.float32)
    c["negdelta"] = (-decay_deltas()).reshape(4, 128).T.copy()
    return c


def pstep(res):
    return res.t[:].ap[0][0]


def sview(res, off, dims, nparts=128, p0=0):
    ps_ = pstep(res)
    return bass.AP(res.t[:].tensor, p0 * ps_ + off, [[ps_, nparts]] + [list(d) for d in dims])


class Prog:
    GP = 8192

    def __init__(self, Lp, Ls, ns, depth=4, debug=()):
        self.Lp, self.Ls, self.ns, self.depth = Lp, Ls, ns, depth
        self.debug = set(debug)
        self.nc = bass.Bass("TRN2", target_bir_lowering=False)
        self.k = K(self.nc)
        self.seqs = [("p", Lp)] + [("s%d" % i, Ls) for i in range(ns)]
        self.inputs = {}
        self.outputs = {}
        self.consts = const_tables([Lp, Ls])
        self._declare()

    def inp(self, name, shape, dt):
        r = self.k.dram(name, shape, dt, kind="ExternalInput")
        self.inputs[name] = r
        return r

    def scratch(self, name, shape, dt):
        kind = "ExternalOutput" if name in self.debug else "Internal"
        r = self.k.dram(name, shape, dt, kind=kind)
        if kind == "ExternalOutput":
            self.outputs[name] = r
        return r

    def _declare(self):
        ne, no = (self.depth + 1) // 2, self.depth // 2
        self.ne, self.no = ne, no
        i = self.inp
        self.x_in = {"p": i("x_p", [self.Lp, D], F32)}
        xs = i("x_s", [self.ns, self.Ls, D], F32)
        for s in range(self.ns):
            r = Res("x_s%d" % s, xs.t[s], is_dram=True)
            self.x_in["s%d" % s] = r
        self.w = {}
        for name, shape in [
            ("a_norm", [ne, D]), ("a_w_in", [ne, D, EVEN_IN]), ("a_conv_w", [ne, 3, 3 * HC]),
            ("a_conv_b", [ne, 3 * HC]), ("a_filt_w1", [ne, 33, 64]), ("a_filt_b1", [ne, 64]),
            ("a_filt_f1", [ne, 64]), ("a_filt_w2", [ne, 64, 64]), ("a_filt_b2", [ne, 64]),
            ("a_filt_f2", [ne, 64]), ("a_filt_w3", [ne, 64, 4 * HC]), ("a_hyena_d", [ne, 2, HC]),
            ("a_sink", [ne, 8]), ("a_w_out", [ne, D, D]), ("c_norm", [max(no, 1), D]),
            ("c_w_in", [max(no, 1), D, ODD_IN]), ("c_w_out", [max(no, 1), D, D]), ("final_norm", [1, D]),
        ]:
            self.w[name] = i(name, shape, F32)
        self.c = {}
        for name, arr in self.consts.items():
            dt = BF16 if arr.dtype == ml_dtypes.bfloat16 else F32
            self.c[name] = i("c_" + name, list(arr.shape), dt)
        self.y = {}
        for (sn, L) in self.seqs:
            r = self.k.dram("y_" + sn, [L, D], F32, kind="ExternalOutput")
            self.outputs["y_" + sn] = r
            self.y[sn] = r
        sc = self.scratch
        self.xa, self.xb, self.xnT = {}, {}, {}
        self.HY, self.HG, self.QT, self.KT, self.VB, self.AGT, self.MT, self.HYO = {}, {}, {}, {}, {}, {}, {}, {}
        for (sn, L) in self.seqs:
            self.xa[sn] = sc("xa_" + sn, [L, D], F32)
            self.xb[sn] = sc("xb_" + sn, [L, D], F32)
            self.xnT[sn] = sc("xnT_" + sn, [8, 128, L + 4], BF16)
            self.HY[sn] = sc("HY_" + sn, [L, 3 * HC], F32)
            self.HG[sn] = sc("HG_" + sn, [L, HC], F32)
            self.QT[sn] = sc("QT_" + sn, [512, L], BF16)
            self.KT[sn] = sc("KT_" + sn, [128, L], BF16)
            self.VB[sn] = sc("VB_" + sn, [L, 128], BF16)
            self.AGT[sn] = sc("AGT_" + sn, [512, L], BF16)
            self.MT[sn] = sc("MT_" + sn, [D, L], BF16)
            self.HYO[sn] = sc("HYO_" + sn, [L, HC], BF16)
        self.QTc, self.KTc, self.Vc, self.CGT = {}, {}, {}, {}
        for (sn, L) in self.seqs:
            for g, dl in enumerate((1, 4, 16)):
                pad = 64 * dl
                self.QTc[sn, g] = sc("QTc%d_%s" % (g, sn), [D, L], BF16)
                self.KTc[sn, g] = sc("KTc%d_%s" % (g, sn), [D, L + 2 * pad], BF16)
                self.Vc[sn, g] = sc("Vc%d_%s" % (g, sn), [L + 2 * pad, D], BF16)
            self.CGT[sn] = sc("CGT_" + sn, [D, L], BF16)
        self.A, self.Dx = {}, {}
        for L in sorted(set([self.Lp, self.Ls])):
            self.A[L] = sc("A_%d" % L, [2, HC, 2 * L], BF16)
            self.Dx[L] = sc("Dx_%d" % L, [2, HC], F32)

    def phase_norm(self, src, gname=None, final=False):
        k, nc = self.k, self.nc
        with contextlib.ExitStack() as es:
            ident = k.sb(es, "ident", [128, 128], BF16)
            k.dma(k.sp, ident[:], self.c["ident"].t[:, :], [self.c["ident"]], [ident])
            zc = k.sb(es, "zc", [128, 8, 2], BF16)
            k.op(k.dve, lambda: nc.vector.memset(zc[:], 0.0), W=[zc])
            xin = [k.sb(es, "xin%d" % i, [128, 4, D], F32) for i in range(2)]
            sq = [k.sb(es, "sq%d" % i, [128, D], F32) for i in range(2)]
            ss = [k.sb(es, "ss%d" % i, [128, 4], F32) for i in range(2)]
            rs = [k.sb(es, "rs%d" % i, [128, 4], F32) for i in range(2)]
            if final:
                gb = k.sb(es, "gb", [128, D], F32)
                k.dma(k.sp, gb[:], dap(self.w["final_norm"], 0, [[0, 128], [1, D]]), [self.w["final_norm"]], [gb])
                yo = [k.sb(es, "yo%d" % i, [128, 4, D], F32) for i in range(2)]
            else:
                xs = [k.sb(es, "xs%d" % i, [128, 4, D], BF16) for i in range(2)]
                xo = [k.sb(es, "xo%d" % i, [128, 8, 512], BF16) for i in range(2)]
                pst = [k.ps(es, "pt%d" % i, [128, 8, 128], BF16) for i in range(4)]
            it = 0
            for (sn, L) in self.seqs:
                x = src[sn]
                if not final:
                    xt = self.xnT[sn]
                    xt_v = xt.t.rearrange("k p t -> p k t")
                    k.dma(k.pool, xt_v[:, :, 0:2], zc[:], [zc], [xt], slow=True)
                    k.dma(k.pool, xt_v[:, :, L + 2:L + 4], zc[:], [zc], [xt], slow=True)
                xv = x.t.rearrange("(c j p) d -> c p j d", j=4, p=128)
                for c in range(L // 512):
                    b = it % 2
                    it += 1
                    xi, s_, r_, q_ = xin[b], ss[b], rs[b], sq[b]
                    k.dma(k.sp, xi[:], xv[c], [x], [xi])
                    k.op(k.dve, lambda: nc.vector.memset(s_[:], 0.0), W=[s_])
                    for j in range(4):
                        k.op(k.act, lambda j=j: nc.scalar.activation(out=q_[:], in_=xi[:, j, :], func=AF.Square,
                                                                     accum_out=s_[:, j:j + 1]), R=[xi], W=[q_, s_])
                    k.op(k.act, lambda: nc.scalar.activation(out=r_[:], in_=s_[:], func=AF.Sqrt, bias=EPS, scale=1.0 / D),
                         R=[s_], W=[r_])
                    k.op(k.dve, lambda: nc.vector.reciprocal(out=r_[:], in_=r_[:]), R=[r_], W=[r_])
                    if final:
                        y_ = yo[b]
                        for j in range(4):
                            k.op(k.dve, lambda j=j: nc.vector.scalar_tensor_tensor(
                                out=y_[:, j, :], in0=xi[:, j, :], scalar=r_[:, j:j + 1], in1=gb[:],
                                op0=ALU.mult, op1=ALU.mult), R=[xi, r_, gb], W=[y_])
                        yv = self.y[sn].t.rearrange("(c j p) d -> c p j d", j=4, p=128)
                        k.dma(k.pool, yv[c], y_[:], [y_], [self.y[sn]])
                        continue
                    xs_, xo_ = xs[b], xo[b]
                    for j in range(4):
                        if j % 2 == 0:
                            k.op(k.act, lambda j=j: nc.scalar.activation(out=xs_[:, j, :], in_=xi[:, j, :], func=AF.Copy,
                                                                         scale=r_[:, j:j + 1]), R=[xi, r_], W=[xs_])
                        else:
                            k.op(k.dve, lambda j=j: nc.vector.tensor_scalar(out=xs_[:, j, :], in0=xi[:, j, :],
                                                                            scalar1=r_[:, j:j + 1], scalar2=None,
                                                                            op0=ALU.mult), R=[xi, r_], W=[xs_])
                    for j in range(4):
                        pt = pst[(it * 4 + j) % 4]
                        k.tr([(pt[:, kk, :], xs_[:, j, kk * 128:(kk + 1) * 128], ident[:]) for kk in range(8)],
                             R=[xs_, ident], W=[pt])
                        if j % 2 == 0:
                            k.op(k.dve, lambda j=j, pt=pt: nc.vector.tensor_copy(out=xo_[:, :, j * 128:(j + 1) * 128], in_=pt[:]),
                                 R=[pt], W=[xo_])
                        else:
                            k.op(k.act, lambda j=j, pt=pt: nc.scalar.copy(out=xo_[:, :, j * 128:(j + 1) * 128], in_=pt[:]),
                                 R=[pt], W=[xo_])
                    k.dma(k.pool, xt_v[:, :, 2 + c * 512:2 + (c + 1) * 512], xo_[:], [xo_], [xt])
            k.barrier()

    def rope_tm(self, ps, nh, hd, rot, cs_ap_cc, cs_ap_ss, dst, dst_off, tmp_a, tmp_b, extra_R=()):
        k, nc = self.k, self.nc
        half = rot // 2
        x_all = sview(ps, 0, [(1, nh * hd)])
        xr = sview(ps, 0, [(hd, nh), (1, rot)])
        x1 = sview(ps, 0, [(hd, nh), (1, half)])
        x2 = sview(ps, half, [(hd, nh), (1, half)])
        ta = sview(tmp_a, 0, [(rot, nh), (1, rot)])
        tb1 = sview(tmp_b, 0, [(rot, nh), (1, half)])
        tb2 = sview(tmp_b, half, [(rot, nh), (1, half)])
        tb = sview(tmp_b, 0, [(rot, nh), (1, rot)])
        ss1 = cs_ap_ss[0]
        ss2 = cs_ap_ss[1]
        R0 = [ps] + list(extra_R)
        k.op(k.act, lambda: nc.scalar.copy(out=sview(dst, dst_off, [(1, nh * hd)]), in_=x_all), R=[ps], W=[dst])
        k.op(k.dve, lambda: nc.vector.tensor_tensor(out=ta, in0=xr, in1=cs_ap_cc, op=ALU.mult), R=R0, W=[tmp_a])
        k.op(k.dve, lambda: nc.vector.tensor_tensor(out=tb1, in0=x2, in1=ss1, op=ALU.mult), R=R0, W=[tmp_b])
        k.op(k.dve, lambda: nc.vector.tensor_tensor(out=tb2, in0=x1, in1=ss2, op=ALU.mult), R=R0, W=[tmp_b])
        k.op(k.dve, lambda: nc.vector.tensor_tensor(out=sview(dst, dst_off, [(hd, nh), (1, rot)]), in0=ta, in1=tb,
                                                    op=ALU.add), R=[tmp_a, tmp_b], W=[dst])

    def phase_even_proj(self, li, parts=("hy", "hg", "q", "kv", "ag")):
        k, nc = self.k, self.nc
        W = self.w["a_w_in"]
        with contextlib.ExitStack() as es:
            ident = k.sb(es, "ident", [128, 128], BF16)
            k.dma(k.sp, ident[:], self.c["ident"].t[:, :], [self.c["ident"]], [ident])
            Wsb = k.sb(es, "Wsb", [128, 8, 1792], BF16)
            Wtap = [k.sb(es, "Wtap%d" % t, [128, 8, 1536], BF16) for t in range(3)]
            biasb = k.sb(es, "biasb", [128, 1536], F32)
            k.dma(k.sp, biasb[:], dap(self.w["a_conv_b"], li * 1536, [[0, 128], [1, 1536]]), [self.w["a_conv_b"]], [biasb])
            with contextlib.ExitStack() as es2:
                gsb = k.sb(es2, "gsb", [128, 8, 1], F32)
                k.dma(k.sp, gsb[:], dap(self.w["a_norm"], li * D, [[1, 128], [128, 8], [1, 1]]), [self.w["a_norm"]], [gsb], slow=True)
                tapb = [k.sb(es2, "tapb%d" % t, [128, 1536], F32) for t in range(3)]
                for t in range(3):
                    k.dma(k.sp, tapb[t][:], dap(self.w["a_conv_w"], (li * 3 + t) * 1536, [[0, 128], [1, 1536]]),
                          [self.w["a_conv_w"]], [tapb[t]])
                stage = [k.sb(es2, "stage%d" % i, [128, 8, 256], F32) for i in range(2)]
                wv = W.t[li].rearrange("(k p) c -> p k c", p=128)
                for cb in range(13):
                    c0 = cb * 256
                    st = stage[cb % 2]
                    k.dma(k.sp, st[:], wv[:, :, c0:c0 + 256], [W], [st])
                    k.op(k.dve, lambda st=st: nc.vector.tensor_tensor(
                        out=st[:], in0=st[:], in1=sview(gsb, 0, [(1, 8), (0, 256)]), op=ALU.mult), R=[st, gsb], W=[st])
                    if c0 < 1536:
                        for t in range(3):
                            k.op(k.dve, lambda st=st, t=t, c0=c0: nc.vector.tensor_tensor(
                                out=Wtap[t][:, :, c0:c0 + 256], in0=st[:],
                                in1=sview(tapb[t], c0, [(0, 8), (1, 256)]), op=ALU.mult), R=[st, tapb[t]], W=[Wtap[t]])
                    else:
                        k.op(k.act, lambda st=st, c0=c0: nc.scalar.copy(out=Wsb[:, :, c0 - 1536:c0 - 1536 + 256], in_=st[:]),
                             R=[st], W=[Wsb])
                k.barrier()
            xc = [k.sb(es, "xc%d" % i, [128, 8, 516], BF16) for i in range(2)]
            xcB = [k.sb(es, "xcB%d" % i, [128, 8, 516], BF16) for i in range(2)]
            hyo = [k.sb(es, "hyo%d" % i, [128, 1536], F32) for i in range(2)]
            hgo = [k.sb(es, "hgo%d" % i, [128, 512], F32) for i in range(2)]
            qbf = [k.sb(es, "qbf%d" % i, [128, 512], BF16) for i in range(4)]
            kvbf = [k.sb(es, "kvbf%d" % i, [128, 256], BF16) for i in range(4)]
            qTs = [k.sb(es, "qTs%d" % i, [128, 4, 512], BF16) for i in range(2)]
            kTs = [k.sb(es, "kTs%d" % i, [128, 512], BF16) for i in range(2)]
            vo = [k.sb(es, "vo%d" % i, [128, 4, 128], BF16) for i in range(2)]
            agT = [k.sb(es, "agT%d" % i, [128, 4, 512], BF16) for i in range(2)]
            cs = [k.sb(es, "cs%d" % i, [128, 4, 32], F32) for i in range(2)]
            tmpa = [k.sb(es, "tmpa%d" % i, [128, 128], F32) for i in range(4)]
            tmpb = [k.sb(es, "tmpb%d" % i, [128, 128], F32) for i in range(4)]
            psA = [k.ps(es, "psA%d" % i, [128, 512], F32) for i in range(5)]
            psT = [k.ps(es, "psT%d" % i, [128, 8, 128], BF16) for i in range(3)]
            pipe = Pipe(1)
            t3 = [0]
            pa = [0]

            def nps():
                pa[0] += 1
                return psA[pa[0] % 5]
            it = 0
            tt = 0
            for (sn, L) in self.seqs:
                xt = self.xnT[sn]
                xt_v = xt.t.rearrange("k p t -> p k t")
                rope_v = self.c["rope8"].t.rearrange("(c j p) r -> c p j r", j=4, p=128)
                for c in range(L // 512):
                    b = it % 2
                    it += 1
                    x_ = xc[b]
                    k.dma(k.sp, x_[:], xt_v[:, :, c * 512:c * 512 + 516], [xt], [x_])
                    xB_ = xcB[b]
                    k.dma(k.sp, xB_[:, :, 0:514], xt_v[:, :, c * 512 + 1:c * 512 + 515], [xt], [xB_])
                    cs_ = cs[b]
                    k.dma(k.sp, cs_[:], rope_v[c], [self.c["rope8"]], [cs_])
                    qT_, kT_, vo_, ag_ = qTs[b], kTs[b], vo[b], agT[b]
                    for j in range(4):
                        tb_ = tt % 2
                        tt += 1
                        t0 = 2 + j * 128
                        hy_ = hyo[tb_]
                        tok0 = c * 512 + j * 128
                        for g in range(3 if "hy" in parts else 0):
                            p_ = nps()
                            mms = []
                            for t in range(3):
                                for kk in range(8):
                                    src_ = x_ if t == 1 else xB_
                                    o_ = t0 if t == 1 else (j * 128 + t)
                                    mms.append(dict(out=p_[:], lhsT=src_[:, kk, o_:o_ + 128],
                                                    rhs=Wtap[t][:, kk, g * 512:(g + 1) * 512],
                                                    start=(t == 0 and kk == 0), stop=(t == 2 and kk == 7)))
                            k.mm(mms, R=[x_, xB_] + Wtap, W=[p_])
                            k.op(k.dve, lambda p_=p_, g=g, hy_=hy_: nc.vector.tensor_tensor(
                                out=hy_[:, g * 512:(g + 1) * 512], in0=p_[:], in1=biasb[:, g * 512:(g + 1) * 512], op=ALU.add),
                                R=[p_, biasb], W=[hy_])
                        if "hy" in parts:
                            k.dma(k.pool, self.HY[sn].t[tok0:tok0 + 128, :], hy_[:], [hy_], [self.HY[sn]])
                        if "hg" not in parts:
                            continue
                        p_ = nps()
                        k.mm([dict(out=p_[:], lhsT=x_[:, kk, t0:t0 + 128], rhs=Wsb[:, kk, 0:512], start=(kk == 0), stop=(kk == 7))
                              for kk in range(8)], R=[x_, Wsb], W=[p_])
                        hg_ = hgo[tb_]
                        k.op(k.act, lambda p_=p_, hg_=hg_: nc.scalar.activation(out=hg_[:], in_=p_[:], func=AF.Silu), R=[p_], W=[hg_])
                        k.dma(k.pool, self.HG[sn].t[tok0:tok0 + 128, :], hg_[:], [hg_], [self.HG[sn]])
                        if "q" not in parts:
                            continue
                        p_ = nps()
                        k.mm([dict(out=p_[:], lhsT=x_[:, kk, t0:t0 + 128], rhs=Wsb[:, kk, 512:1024], start=(kk == 0), stop=(kk == 7))
                              for kk in range(8)], R=[x_, Wsb], W=[p_])
                        q_ = qbf[tt % 4]
                        cc = sview(cs_, j * 32, [(0, 8), (1, 16)])
                        s1 = sview(cs_, j * 32 + 16, [(0, 8), (1, 8)])
                        s2 = sview(cs_, j * 32 + 24, [(0, 8), (1, 8)])
                        self.rope_tm(p_, 8, 64, 16, cc, (s1, s2), q_, 0, tmpa[tt % 4], tmpb[tt % 4], extra_R=[cs_])
                        t3[0] += 1
                        pt = psT[t3[0] % 3]

                        def stageBq(pt=pt, q_=q_, j=j, qT_=qT_):
                            k.tr([(pt[:, blk, :], q_[:, blk * 128:(blk + 1) * 128], ident[:]) for blk in range(4)], R=[q_, ident], W=[pt])
                            k.op(k.act, lambda: nc.scalar.copy(out=qT_[:, :, j * 128:(j + 1) * 128], in_=pt[:, 0:4, :]), R=[pt], W=[qT_])
                        pipe.push(stageBq)
                        if "kv" not in parts:
                            continue
                        p_ = nps()
                        k.mm([dict(out=p_[:, 0:256], lhsT=x_[:, kk, t0:t0 + 128], rhs=Wsb[:, kk, 1024:1280], start=(kk == 0), stop=(kk == 7))
                              for kk in range(8)], R=[x_, Wsb], W=[p_])
                        kv_ = kvbf[tt % 4]
                        cc2 = sview(cs_, j * 32, [(0, 2), (1, 16)])
                        s12 = sview(cs_, j * 32 + 16, [(0, 2), (1, 8)])
                        s22 = sview(cs_, j * 32 + 24, [(0, 2), (1, 8)])
                        self.rope_tm(p_, 2, 64, 16, cc2, (s12, s22), kv_, 0, tmpa[(tt + 2) % 4], tmpb[(tt + 2) % 4], extra_R=[cs_])
                        k.op(k.dve, lambda p_=p_, vo_=vo_, j=j: nc.vector.tensor_copy(out=vo_[:, j, :], in_=p_[:, 128:256]), R=[p_], W=[vo_])
                        t3[0] += 1
                        pt2 = psT[t3[0] % 3]

                        def stageBk(pt2=pt2, kv_=kv_, j=j, kT_=kT_):
                            k.tr([(pt2[:, 4, :], kv_[:, 0:128], ident[:])], R=[kv_, ident], W=[pt2])
                            k.op(k.dve, lambda: nc.vector.tensor_copy(out=kT_[:, j * 128:(j + 1) * 128], in_=pt2[:, 4, :]), R=[pt2], W=[kT_])
                        pipe.push(stageBk)
                    pipe.flush()
                    for cb in range(4 if "ag" in parts else 0):
                        p_ = nps()
                        k.mm([dict(out=p_[:], lhsT=Wsb[:, kk, 1280 + cb * 128:1280 + (cb + 1) * 128], rhs=x_[:, kk, 2:514],
                                   start=(kk == 0), stop=(kk == 7)) for kk in range(8)], R=[x_, Wsb], W=[p_])
                        k.op(k.act, lambda p_=p_, cb=cb, ag_=ag_: nc.scalar.activation(out=ag_[:, cb, :], in_=p_[:], func=AF.Silu),
                             R=[p_], W=[ag_])
                    sl = slice(c * 512, (c + 1) * 512)
                    if "q" in parts:
                        k.dma(k.pool, self.QT[sn].t.rearrange("(b p) t -> p b t", p=128)[:, :, sl], qT_[:], [qT_], [self.QT[sn]])
                    if "kv" in parts:
                        k.dma(k.pool, self.KT[sn].t[:, sl], kT_[:], [kT_], [self.KT[sn]])
                        k.dma(k.pool, self.VB[sn].t.rearrange("(c j p) d -> c p j d", j=4, p=128)[c], vo_[:], [vo_], [self.VB[sn]])
                    if "ag" in parts:
                        k.dma(k.pool, self.AGT[sn].t.rearrange("(b p) t -> p b t", p=128)[:, :, sl], ag_[:], [ag_], [self.AGT[sn]])
            k.barrier()

    def finish(self):
        k = self.k
        k.barrier()

    def in_map(self, x_p, x_s, weights):
        m = {"x_p": np.ascontiguousarray(x_p, dtype=np.float32), "x_s": np.ascontiguousarray(x_s, dtype=np.float32)}
        for name in self.w:
            a = np.asarray(weights[name], dtype=np.float32)
            if name == "final_norm":
                a = a.reshape(1, D)
            m[name] = np.ascontiguousarray(a)
        for name, arr in self.consts.items():
            m["c_" + name] = arr
        return m

    def phase_outproj(self, wname, li, src, dst):
        k, nc = self.k, self.nc
        Wd = self.w[wname]
        with contextlib.ExitStack() as es:
            Wo = k.sb(es, "Wo", [128, 8, D], BF16)
            with contextlib.ExitStack() as es2:
                stage = [k.sb(es2, "stg%d" % i, [128, 8, 256], F32) for i in range(2)]
                wv = Wd.t[li].rearrange("(k p) c -> p k c", p=128)
                for cb in range(4):
                    st = stage[cb % 2]
                    k.dma(k.sp, st[:], wv[:, :, cb * 256:(cb + 1) * 256], [Wd], [st])
                    if cb % 2:
                        k.op(k.act, lambda st=st, cb=cb: nc.scalar.copy(out=Wo[:, :, cb * 256:(cb + 1) * 256], in_=st[:]), R=[st], W=[Wo])
                    else:
                        k.op(k.dve, lambda st=st, cb=cb: nc.vector.tensor_copy(out=Wo[:, :, cb * 256:(cb + 1) * 256], in_=st[:]), R=[st], W=[Wo])
                k.barrier()
            mt = [k.sb(es, "mt%d" % i, [128, 8, 512], BF16) for i in range(2)]
            xi = [k.sb(es, "xi%d" % i, [128, 4, D], F32) for i in range(2)]
            xo = [k.sb(es, "xo%d" % i, [128, 4, D], F32) for i in range(2)]
            ps = [k.ps(es, "po%d" % i, [128, 512], F32) for i in range(4)]
            it = 0
            pi = 0
            for (sn, L) in self.seqs:
                mtv = self.MT[sn].t.rearrange("(k p) t -> p k t", p=128)
                xv = src[sn].t.rearrange("(c j p) d -> c p j d", j=4, p=128)
                dv = dst[sn].t.rearrange("(c j p) d -> c p j d", j=4, p=128)
                for c in range(L // 512):
                    b = it % 2
                    it += 1
                    mt_, xi_, xo_ = mt[b], xi[b], xo[b]
                    k.dma(k.sp, mt_[:], mtv[:, :, c * 512:(c + 1) * 512], [self.MT[sn]], [mt_])
                    k.dma(k.sp, xi_[:], xv[c], [src[sn]], [xi_])
                    for j in range(4):
                        for g in range(2):
                            p_ = ps[pi % 4]
                            pi += 1
                            k.mm([dict(out=p_[:], lhsT=mt_[:, kk, j * 128:(j + 1) * 128], rhs=Wo[:, kk, g * 512:(g + 1) * 512],
                                       start=(kk == 0), stop=(kk == 7)) for kk in range(8)], R=[mt_, Wo], W=[p_])
                            k.op(k.dve, lambda p_=p_, j=j, g=g: nc.vector.tensor_tensor(
                                out=xo_[:, j, g * 512:(g + 1) * 512], in0=p_[:], in1=xi_[:, j, g * 512:(g + 1) * 512], op=ALU.add),
                                R=[p_, xi_], W=[xo_])
                    k.dma(k.pool, dv[c], xo_[:], [xo_], [dst[sn]])
            k.barrier()

    def load_const(self, es, name, shape, dt):
        k = self.k
        r = k.sb(es, name, shape, dt)
        k.dma(k.sp, r[:], self.c[name].t[:, :], [self.c[name]], [r])
        return r

    def phase_band_attn(self, li):
        k, nc = self.k, self.nc
        with contextlib.ExitStack() as es:
            maskA = self.load_const(es, "maskA", [128, 128], BF16)
            maskB = self.load_const(es, "maskB", [128, 128], BF16)
            ones = self.load_const(es, "ones", [128, 128], BF16)
            snk = k.sb(es, "snk", [64, 8], F32)
            k.dma(k.sp, snk[:], dap(self.w["a_sink"], li * 8, [[0, 64], [1, 8]]), [self.w["a_sink"]], [snk])
            k.op(k.act, lambda: nc.scalar.activation(out=snk[:], in_=snk[:], func=AF.Exp), R=[snk], W=[snk])
            q_sb = [k.sb(es, "q_sb%d" % i, [64, 8, 512], BF16) for i in range(2)]
            ag_sb = [k.sb(es, "ag_sb%d" % i, [64, 8, 512], BF16) for i in range(2)]
            k_sb = [k.sb(es, "k_sb%d" % i, [64, 2, 768], BF16) for i in range(2)]
            v_sb = [k.sb(es, "v_sb%d" % i, [128, 6, 128], BF16) for i in range(2)]
            mt_sb = [k.sb(es, "mt_sb%d" % i, [64, 8, 512], BF16) for i in range(2)]
            pT = [k.sb(es, "pT%d" % i, [128, 512], BF16) for i in range(3)]
            den = [k.sb(es, "den%d" % i, [64, 512], F32) for i in range(2)]
            o_sb = [k.sb(es, "o_sb%d" % i, [64, 512], F32) for i in range(2)]
            psS = [k.ps(es, "psS%d" % i, [128, 512], F32) for i in range(3)]
            psO = [k.ps(es, "psO%d" % i, [128, 512], F32) for i in range(2)]
            psD = [k.ps(es, "psD%d" % i, [128, 512], F32) for i in range(2)]
            it = 0
            u = 0
            si = 0
            pipe = Pipe(1)
            for (sn, L) in self.seqs:
                nq = L // 128
                qv = self.QT[sn].t.rearrange("(h d) t -> d h t", d=64)
                agv = self.AGT[sn].t.rearrange("(h d) t -> d h t", d=64)
                kv = self.KT[sn].t.rearrange("(h d) t -> d h t", d=64)
                vv = self.VB[sn].t.rearrange("(b p) d -> p b d", p=128)
                mv = self.MT[sn].t[512:1024, :].rearrange("(h d) t -> d h t", d=64)
                for c in range(L // 512):
                    b = it % 2
                    it += 1
                    t0 = c * 512
                    q_, ag_, k_, v_, mt_ = q_sb[b], ag_sb[b], k_sb[b], v_sb[b], mt_sb[b]
                    k.dma(k.sp, q_[:], qv[:, :, t0:t0 + 512], [self.QT[sn]], [q_])
                    k.dma(k.sp, ag_[:], agv[:, :, t0:t0 + 512], [self.AGT[sn]], [ag_])
                    ks, ke = max(0, t0 - 128), min(L, t0 + 640)
                    k.dma(k.sp, k_[:, :, ks - (t0 - 128):ke - (t0 - 128)], kv[:, :, ks:ke], [self.KT[sn]], [k_])
                    kb0, kb1 = max(0, c * 4 - 1), min(nq, c * 4 + 5)
                    k.dma(k.sp, v_[:, kb0 - (c * 4 - 1):kb1 - (c * 4 - 1), :], vv[:, kb0:kb1, :], [self.VB[sn]], [v_])
                    for jq in range(4):
                        qb = c * 4 + jq
                        for kvh in range(2):
                            kbs = [kb for kb in (qb - 1, qb, qb + 1) if 0 <= kb < nq]
                            pO, pD = psO[u % 2], psD[u % 2]
                            den_, o_ = den[u % 2], o_sb[u % 2]
                            u += 1
                            for i, kb in enumerate(kbs):
                                bl = kb - (c * 4 - 1)
                                pS, pT_ = psS[si % 3], pT[si % 3]
                                si += 1
                                k.mm([dict(out=pS[:], lhsT=k_[:, kvh, bl * 128:(bl + 1) * 128],
                                           rhs=q_[:, kvh * 4:(kvh + 1) * 4, jq * 128:(jq + 1) * 128], start=True, stop=True)],
                                     R=[k_, q_], W=[pS])
                                k.op(k.act, lambda pS=pS, pT_=pT_: nc.scalar.activation(out=pT_[:], in_=pS[:], func=AF.Exp, scale=0.125),
                                     R=[pS], W=[pT_])
                                if kb != qb:
                                    mk = maskA if kb < qb else maskB
                                    k.op(k.dve, lambda pT_=pT_, mk=mk: nc.vector.tensor_tensor(
                                        out=sview(pT_, 0, [(128, 4), (1, 128)]), in0=sview(pT_, 0, [(128, 4), (1, 128)]),
                                        in1=sview(mk, 0, [(0, 4), (1, 128)]), op=ALU.mult), R=[pT_, mk], W=[pT_])
                                st, sp_ = (i == 0), (i == len(kbs) - 1)

                                def stageB(pO=pO, pD=pD, v_=v_, bl=bl, kvh=kvh, pT_=pT_, st=st, sp_=sp_):
                                    k.mm([dict(out=pO[0:64, :], lhsT=v_[:, bl, kvh * 64:(kvh + 1) * 64], rhs=pT_[:], start=st, stop=sp_)],
                                         R=[v_, pT_], W=[pO])
                                    k.mm([dict(out=pD[0:64, :], lhsT=ones[:, 0:64], rhs=pT_[:], start=st, stop=sp_)], R=[ones, pT_], W=[pD])
                                pipe.push(stageB)

                            def stageN(pO=pO, pD=pD, den_=den_, o_=o_, kvh=kvh, jq=jq, mt_=mt_, ag_=ag_):
                                k.op(k.dve, lambda: nc.vector.tensor_tensor(
                                    out=sview(den_, 0, [(128, 4), (1, 128)], nparts=64), in0=sview(pD, 0, [(128, 4), (1, 128)], nparts=64),
                                    in1=sview(snk, kvh * 4, [(1, 4), (0, 128)], nparts=64), op=ALU.add), R=[pD, snk], W=[den_])
                                k.op(k.dve, lambda: nc.vector.reciprocal(out=den_[:], in_=den_[:]), R=[den_], W=[den_])
                                k.op(k.dve, lambda: nc.vector.tensor_tensor(
                                    out=o_[:], in0=pO[0:64, :], in1=den_[:], op=ALU.mult), R=[pO, den_], W=[o_])
                                k.op(k.pool, lambda: nc.gpsimd.tensor_tensor(
                                    out=mt_[:, kvh * 4:(kvh + 1) * 4, jq * 128:(jq + 1) * 128], in0=sview(o_, 0, [(128, 4), (1, 128)], nparts=64),
                                    in1=ag_[:, kvh * 4:(kvh + 1) * 4, jq * 128:(jq + 1) * 128], op=ALU.mult), R=[o_, ag_], W=[mt_])
                            pipe.push(stageN)
                    pipe.flush()
                    k.dma(k.pool, mv[:, :, t0:t0 + 512], mt_[:], [mt_], [self.MT[sn]])
            k.barrier()

    def phase_filter(self, li):
        k, nc = self.k, self.nc
        PI = float(np.pi)
        with contextlib.ExitStack() as es:
            def ldw(name, shape, src_ap):
                st = k.sb(es, name + "_f", shape, F32)
                k.dma(k.sp, st[:], src_ap, [self.w[name]], [st], slow=True)
                return st
            w1f = ldw("a_filt_w1", [33, 64], self.w["a_filt_w1"].t[li])
            w2f = ldw("a_filt_w2", [64, 64], self.w["a_filt_w2"].t[li])
            w3f = ldw("a_filt_w3", [64, 4 * HC], self.w["a_filt_w3"].t[li])
            w1 = k.sb(es, "w1b", [33, 64], BF16)
            w2 = k.sb(es, "w2b", [64, 64], BF16)
            w3 = k.sb(es, "w3b", [64, 4 * HC], BF16)
            k.op(k.dve, lambda: nc.vector.tensor_copy(out=w1[:], in_=w1f[:]), R=[w1f], W=[w1])
            k.op(k.dve, lambda: nc.vector.tensor_copy(out=w2[:], in_=w2f[:]), R=[w2f], W=[w2])
            k.op(k.act, lambda: nc.scalar.copy(out=w3[:], in_=w3f[:]), R=[w3f], W=[w3])
            fb = []
            for (fn, bn) in (("a_filt_f1", "a_filt_b1"), ("a_filt_f2", "a_filt_b2")):
                f_ = k.sb(es, fn, [64, 1], F32)
                b_ = k.sb(es, bn, [64, 1], F32)
                k.dma(k.sp, f_[:], dap(self.w[fn], li * 64, [[1, 64], [1, 1]]), [self.w[fn]], [f_], slow=True)
                k.dma(k.sp, b_[:], dap(self.w[bn], li * 64, [[1, 64], [1, 1]]), [self.w[bn]], [b_], slow=True)
                k.op(k.dve, lambda f_=f_, b_=b_: nc.vector.tensor_tensor(out=b_[:], in0=b_[:], in1=f_[:], op=ALU.mult), R=[f_, b_], W=[b_])
                fb.append((f_, b_))
            ndl = self.load_const(es, "negdelta", [128, 4], F32)
            fs = [k.sb(es, "fs%d" % i, [33, 512], F32) for i in range(2)]
            fsb = [k.sb(es, "fsb%d" % i, [33, 512], BF16) for i in range(2)]
            tr_ = [k.sb(es, "trw%d" % i, [128, 512], F32) for i in range(2)]
            pre = k.sb(es, "pre", [64, 512], F32)
            m1 = k.sb(es, "m1", [64, 512], F32)
            hb = [k.sb(es, "hb%d" % i, [64, 512], BF16) for i in range(2)]
            win = [k.sb(es, "win%d" % i, [128, 512], F32) for i in range(2)]
            ao = [k.sb(es, "ao%d" % i, [128, 512], BF16) for i in range(4)]
            dx = k.sb(es, "dxs", [128, 1], F32)
            psh = [k.ps(es, "psh%d" % i, [128, 512], F32) for i in range(2)]
            ps3 = [k.ps(es, "ps3%d" % i, [128, 512], F32) for i in range(4)]
            it = 0
            a3 = 0

            def sin_layer(p_, f_, b_, out_bf):
                k.op(k.dve, lambda: nc.vector.tensor_scalar(out=pre[:], in0=p_[0:64, :], scalar1=f_[:, 0:1], scalar2=b_[:, 0:1],
                                                            op0=ALU.mult, op1=ALU.add), R=[p_, f_, b_], W=[pre])
                k.op(k.dve, lambda: nc.vector.tensor_scalar(out=m1[:], in0=pre[:], scalar1=PI, scalar2=-2 * PI, op0=ALU.is_gt, op1=ALU.mult),
                     R=[pre], W=[m1])
                k.op(k.dve, lambda: nc.vector.tensor_tensor(out=pre[:], in0=pre[:], in1=m1[:], op=ALU.add), R=[pre, m1], W=[pre])
                k.op(k.dve, lambda: nc.vector.tensor_scalar(out=m1[:], in0=pre[:], scalar1=-PI, scalar2=2 * PI, op0=ALU.is_lt, op1=ALU.mult),
                     R=[pre], W=[m1])
                k.op(k.dve, lambda: nc.vector.tensor_tensor(out=pre[:], in0=pre[:], in1=m1[:], op=ALU.add), R=[pre, m1], W=[pre])
                k.op(k.act, lambda: nc.scalar.activation(out=out_bf[:], in_=pre[:], func=AF.Sin), R=[pre], W=[out_bf])

            for L in sorted(set([self.Lp, self.Ls])):
                A, Dx = self.A[L], self.Dx[L]
                for dr in range(2):
                    fT = self.c["featsT_%d" % L] if dr == 0 else self.c["featsTr_%d" % L]
                    for c in range(L // 512):
                        b = it % 2
                        it += 1
                        f_s, f_b, t_ = fs[b], fsb[b], tr_[b]
                        k.dma(k.sp, f_s[:], fT.t[:, c * 512:(c + 1) * 512], [fT], [f_s])
                        k.dma(k.sp, t_[:], dap(self.c["trow_%d" % L], dr * L + c * 512, [[0, 128], [1, 512]]), [self.c["trow_%d" % L]], [t_])
                        k.op(k.act, lambda: nc.scalar.copy(out=f_b[:], in_=f_s[:]), R=[f_s], W=[f_b])
                        p_ = psh[0]
                        k.mm([dict(out=p_[0:64, :], lhsT=w1[:], rhs=f_b[:], start=True, stop=True)], R=[w1, f_b], W=[p_])
                        sin_layer(p_, fb[0][0], fb[0][1], hb[0])
                        p_ = psh[1]
                        k.mm([dict(out=p_[0:64, :], lhsT=w2[:], rhs=hb[0][:], start=True, stop=True)], R=[w2, hb[0]], W=[p_])
                        sin_layer(p_, fb[1][0], fb[1][1], hb[1])
                        for cb in range(4):
                            w_ = win[cb % 2]
                            k.op(k.act, lambda w_=w_, cb=cb: nc.scalar.activation(out=w_[:], in_=t_[:], func=AF.Exp, scale=ndl[:, cb:cb + 1]),
                                 R=[t_, ndl], W=[w_])
                            for o in range(2):
                                p3 = ps3[a3 % 4]
                                a_ = ao[a3 % 4]
                                a3 += 1
                                col = o * 1024 + dr * 512 + cb * 128
                                k.mm([dict(out=p3[:], lhsT=w3[:, col:col + 128], rhs=hb[1][:], start=True, stop=True)], R=[w3, hb[1]], W=[p3])
                                k.op(k.dve, lambda p3=p3, a_=a_, w_=w_: nc.vector.scalar_tensor_tensor(
                                    out=a_[:], in0=w_[:], scalar=0.05, in1=p3[:], op0=ALU.add, op1=ALU.mult), R=[p3, w_], W=[a_])
                                row0 = o * HC + cb * 128
                                if dr == 0:
                                    k.dma(k.pool, dap(A, row0 * 2 * L + (L - 1) + c * 512, [[2 * L, 128], [1, 512]]), a_[:], [a_], [A])
                                else:
                                    last = (c == L // 512 - 1)
                                    n = 511 if last else 512
                                    k.dma(k.pool, dap(A, row0 * 2 * L + c * 512, [[2 * L, 128], [1, n]]), a_[:, 0:n], [a_], [A])
                                    if last:
                                        k.op(k.dve, lambda p3=p3, w_=w_: nc.vector.scalar_tensor_tensor(
                                            out=dx[:], in0=w_[:, 511:512], scalar=0.05, in1=p3[:, 511:512], op0=ALU.add, op1=ALU.mult),
                                            R=[p3, w_], W=[dx])
                                        k.dma(k.pool, dap(Dx, row0, [[1, 128], [1, 1]]), dx[:], [dx], [Dx], slow=True)
            k.barrier()

    def phase_hyena(self, li):
        k, nc = self.k, self.nc
        CH = 32
        groups = {}
        for (sn, L) in self.seqs:
            groups.setdefault(L, []).append(sn)
        NBM = max(len(v) * (L // 128) for L, v in groups.items())
        GP = self.GP
        with contextlib.ExitStack() as es:
            anti = self.load_const(es, "antiid", [128, 128], BF16)
            zero = k.sb(es, "zero", [128, 128], BF16)
            k.op(k.dve, lambda: nc.vector.memset(zero[:], 0.0), W=[zero])
            x1 = k.sb(es, "x1s", [128, NBM * CH], F32)
            x2 = k.sb(es, "x2s", [128, NBM * CH], F32)
            z0 = k.sb(es, "z0s", [128, NBM * CH], F32)
            z1 = k.sb(es, "z1s", [128, NBM * CH], F32)
            hg = k.sb(es, "hgs", [128, NBM * CH], F32)
            zb = k.sb(es, "zbs", [128, NBM * CH], BF16)
            zr = k.sb(es, "zrs", [128, NBM * CH], BF16)
            ho = k.sb(es, "hos", [128, NBM * CH], BF16)
            tmp = [k.sb(es, "tmpc%d" % i, [128, NBM], F32) for i in range(2)]
            G = [k.sb(es, "G%d" % i, [128, GP], BF16) for i in range(3)]
            deff = k.sb(es, "deff", [128, 2, HC], F32)
            dxb = k.sb(es, "dxb", [128, 2, HC], F32)
            psr = [k.ps(es, "psr%d" % i, [128, 512], F32) for i in range(2)]
            psc = [k.ps(es, "psc%d" % i, [128, 512], F32) for i in range(4)]
            gi = 0
            pc_i = 0
            for L, sns in groups.items():
                nb = L // 128
                ns_ = len(sns)
                tot = ns_ * nb * CH
                A, Dx = self.A[L], self.Dx[L]
                ncol = 2 * L - 128
                npieces = (ncol + GP - 1) // GP

                def v4(res, cc, j0=0, j1=nb):
                    return sview(res, j0 * ns_ * CH + cc, [(CH, (j1 - j0) * ns_)])
                k.dma(k.sp, deff[:], dap(self.w["a_hyena_d"], li * 2 * HC, [[0, 128], [1, 2 * HC]]), [self.w["a_hyena_d"]], [deff])
                k.dma(k.sp, dxb[:], dap(Dx, 0, [[0, 128], [1, 2 * HC]]), [Dx], [dxb])
                k.op(k.dve, lambda: nc.vector.tensor_tensor(out=deff[:], in0=deff[:], in1=dxb[:], op=ALU.add), R=[deff, dxb], W=[deff])
                for c0 in range(0, HC, CH):
                    for si_, sn in enumerate(sns):
                        hyv = self.HY[sn].t.rearrange("(b p) c -> p b c", p=128)
                        hgv = self.HG[sn].t.rearrange("(b p) c -> p b c", p=128)
                        for b0 in range(0, nb, 16):
                            b1 = min(nb, b0 + 16)
                            o_ = (b0 * ns_ + si_) * CH
                            dst = lambda r: sview(r, o_, [(ns_ * CH, b1 - b0), (1, CH)])
                            k.dma(k.sp, dst(x1), hyv[:, b0:b1, c0:c0 + CH], [self.HY[sn]], [x1])
                            k.dma(k.sp, dst(x2), hyv[:, b0:b1, HC + c0:HC + c0 + CH], [self.HY[sn]], [x2])
                            k.dma(k.sp, dst(z0), hyv[:, b0:b1, 2 * HC + c0:2 * HC + c0 + CH], [self.HY[sn]], [z0])
                            k.dma(k.sp, dst(hg), hgv[:, b0:b1, c0:c0 + CH], [self.HG[sn]], [hg])
                    for o in range(2):
                        zin = z0 if o == 0 else z1
                        k.op(k.act, lambda zin=zin: nc.scalar.copy(out=zb[:, 0:tot], in_=zin[:, 0:tot]), R=[zin], W=[zb])
                        for f0 in range(0, tot, 512):
                            n = min(512, tot - f0)
                            pr = psr[(f0 // 512) % 2]
                            k.mm([dict(out=pr[:, 0:n], lhsT=anti[:], rhs=zb[:, f0:f0 + n], start=True, stop=True)], R=[anti, zb], W=[pr])
                            k.op(k.dve, lambda pr=pr, f0=f0, n=n: nc.vector.tensor_copy(out=zr[:, f0:f0 + n], in_=pr[:, 0:n]),
                                 R=[pr], W=[zr])
                        for cc in range(CH):
                            ch = c0 + cc
                            pcs = psc[pc_i % 4]
                            pc_i += 1

                            def pv(i0=0, n=nb):
                                return pcs[:, i0 * ns_:(i0 + n) * ns_]
                            k.mm([dict(out=pv(), lhsT=zero[:], rhs=v4(zr, cc), start=True, stop=False)], R=[zero, zr], W=[pcs])
                            for pc in range(npieces):
                                g_ = G[gi % 3]
                                gi += 1
                                w_ = min(GP, ncol - pc * GP)
                                base = (o * HC + ch) * 2 * L + pc * GP
                                k.dma(k.sp, g_[:, 0:w_], dap(A, base, [[1, 128], [1, w_]]), [A], [g_])
                                mms = []
                                for col in range(pc * GP, pc * GP + w_, 128):
                                    Dd = (col - (L - 128)) // 128
                                    if Dd >= 0:
                                        j0, j1, i0 = 0, nb - Dd, Dd
                                    else:
                                        j0, j1, i0 = -Dd, nb, 0
                                    mms.append(dict(out=pv(i0, j1 - j0), lhsT=g_[:, col - pc * GP:col - pc * GP + 128],
                                                    rhs=v4(zr, cc, j0, j1), start=False, stop=(col + 128 >= ncol)))
                                k.mm(mms, R=[g_, zr], W=[pcs])
                            t_ = tmp[cc % 2]
                            tv = t_[:, 0:nb * ns_]
                            k.op(k.dve, lambda tv=tv, zin=zin, cc=cc, o=o, ch=ch, pcs=pcs: nc.vector.scalar_tensor_tensor(
                                out=tv, in0=v4(zin, cc), scalar=deff[:, o, ch:ch + 1], in1=pcs[:, 0:nb * ns_],
                                op0=ALU.mult, op1=ALU.add), R=[zin, deff, pcs], W=[t_])
                            if o == 0:
                                k.op(k.dve, lambda tv=tv, cc=cc: nc.vector.tensor_tensor(
                                    out=v4(z1, cc), in0=tv, in1=v4(x1, cc), op=ALU.mult), R=[t_, x1], W=[z1])
                            else:
                                k.op(k.pool, lambda tv=tv, cc=cc: nc.gpsimd.tensor_tensor(
                                    out=tv, in0=tv, in1=v4(x2, cc), op=ALU.mult), R=[t_, x2], W=[t_])
                                k.op(k.pool, lambda tv=tv, cc=cc: nc.gpsimd.tensor_tensor(
                                    out=v4(ho, cc), in0=tv, in1=v4(hg, cc), op=ALU.mult), R=[t_, hg], W=[ho])
                    for si_, sn in enumerate(sns):
                        hov = self.HYO[sn].t.rearrange("(b p) c -> p b c", p=128)
                        for b0 in range(0, nb, 16):
                            b1 = min(nb, b0 + 16)
                            o_ = (b0 * ns_ + si_) * CH
                            k.dma(k.pool, hov[:, b0:b1, c0:c0 + CH], sview(ho, o_, [(ns_ * CH, b1 - b0), (1, CH)]), [ho], [self.HYO[sn]])
            k.barrier()

    def phase_hy_transpose(self):
        k, nc = self.k, self.nc
        with contextlib.ExitStack() as es:
            ident = self.load_const(es, "ident", [128, 128], BF16)
            hi = [k.sb(es, "hi%d" % i, [128, 4, HC], BF16) for i in range(2)]
            mo = [k.sb(es, "mo%d" % i, [128, 4, 512], BF16) for i in range(2)]
            pst = [k.ps(es, "ptt%d" % i, [128, 8, 128], BF16) for i in range(4)]
            it = 0
            for (sn, L) in self.seqs:
                hv = self.HYO[sn].t.rearrange("(c j p) d -> c p j d", j=4, p=128)
                mv = self.MT[sn].t[0:512, :].rearrange("(b p) t -> p b t", p=128)
                for c in range(L // 512):
                    b = it % 2
                    it += 1
                    hi_, mo_ = hi[b], mo[b]
                    k.dma(k.sp, hi_[:], hv[c], [self.HYO[sn]], [hi_])
                    for j in range(4):
                        pt = pst[(it * 4 + j) % 4]
                        k.tr([(pt[:, blk, :], hi_[:, j, blk * 128:(blk + 1) * 128], ident[:]) for blk in range(4)], R=[hi_, ident], W=[pt])
                        if j % 2:
                            k.op(k.act, lambda pt=pt, j=j: nc.scalar.copy(out=mo_[:, :, j * 128:(j + 1) * 128], in_=pt[:, 0:4, :]), R=[pt], W=[mo_])
                        else:
                            k.op(k.dve, lambda pt=pt, j=j: nc.vector.tensor_copy(out=mo_[:, :, j * 128:(j + 1) * 128], in_=pt[:, 0:4, :]), R=[pt], W=[mo_])
                    k.dma(k.pool, mv[:, :, c * 512:(c + 1) * 512], mo_[:], [mo_], [self.MT[sn]])
            k.barrier()

    def phase_init_pads(self):
        k, nc = self.k, self.nc
        with contextlib.ExitStack() as es:
            zt = k.sb(es, "zt", [128, 8, 1024], BF16)
            k.op(k.dve, lambda: nc.vector.memset(zt[:], 0.0), W=[zt])
            for (sn, L) in self.seqs:
                for g, dl in enumerate((1, 4, 16)):
                    pad = 64 * dl
                    kt = self.KTc[sn, g]
                    kv = kt.t.rearrange("(h p) t -> p h t", p=128)
                    k.dma(k.pool, kv[:, :, 0:pad], zt[:, :, 0:pad], [zt], [kt])
                    k.dma(k.pool, kv[:, :, pad + L:pad + L + pad], zt[:, :, 0:pad], [zt], [kt])
                    vt = self.Vc[sn, g]
                    for r0 in (0, pad + L):
                        if pad >= 128:
                            k.dma(k.pool, vt.t[r0:r0 + pad, :].rearrange("(b p) c -> p b c", p=128), zt[:, 0:pad // 128, :], [zt], [vt])
                        else:
                            k.dma(k.pool, vt.t[r0:r0 + pad, :], zt[0:pad, 0, :], [zt], [vt])
            k.barrier()

    def phase_odd_proj(self, li, g):
        k, nc = self.k, self.nc
        W = self.w["c_w_in"]
        dl = (1, 4, 16)[g]
        pad = 64 * dl
        ncol = 3072 + (1024 if g == 0 else 0)
        with contextlib.ExitStack() as es:
            ident = self.load_const(es, "ident", [128, 128], BF16)
            Wsb = k.sb(es, "Wc", [128, 8, ncol], BF16)
            with contextlib.ExitStack() as es2:
                gsb = k.sb(es2, "gsb", [128, 8, 1], F32)
                k.dma(k.sp, gsb[:], dap(self.w["c_norm"], li * D, [[1, 128], [128, 8], [1, 1]]), [self.w["c_norm"]], [gsb], slow=True)
                stage = [k.sb(es2, "stage%d" % i, [128, 8, 256], F32) for i in range(2)]
                wv = W.t[li].rearrange("(k p) c -> p k c", p=128)
                for cb in range(ncol // 256):
                    c0 = cb * 256
                    src0 = g * 3072 + c0 if c0 < 3072 else 9216 + (c0 - 3072)
                    st = stage[cb % 2]
                    k.dma(k.sp, st[:], wv[:, :, src0:src0 + 256], [W], [st])
                    k.op(k.dve, lambda st=st, c0=c0: nc.vector.tensor_tensor(
                        out=Wsb[:, :, c0:c0 + 256], in0=st[:], in1=sview(gsb, 0, [(1, 8), (0, 256)]), op=ALU.mult), R=[st, gsb], W=[Wsb])
                k.barrier()
            xc = [k.sb(es, "xc%d" % i, [128, 8, 512], BF16) for i in range(2)]
            qbf = [k.sb(es, "qbf%d" % i, [128, 512], BF16) for i in range(4)]
            qTs = [k.sb(es, "qTs%d" % i, [128, 8, 512], BF16) for i in range(2)]
            kTs = [k.sb(es, "kTs%d" % i, [128, 8, 512], BF16) for i in range(2)]
            vo = [k.sb(es, "vo%d" % i, [128, 4, 1024], BF16) for i in range(2)]
            agT = [k.sb(es, "agT%d" % i, [128, 8, 512], BF16) for i in range(2)]
            cs = [k.sb(es, "cs%d" % i, [128, 4, 64], F32) for i in range(2)]
            tmpa = [k.sb(es, "tmpa%d" % i, [128, 128], F32) for i in range(4)]
            tmpb = [k.sb(es, "tmpb%d" % i, [128, 128], F32) for i in range(4)]
            psA = [k.ps(es, "psA%d" % i, [128, 512], F32) for i in range(5)]
            psT = [k.ps(es, "psT%d" % i, [128, 8, 128], BF16) for i in range(3)]
            pipe = Pipe(1)
            pa = [0]

            def nps():
                pa[0] += 1
                return psA[pa[0] % 5]
            it = 0
            tt = 0
            for (sn, L) in self.seqs:
                xt = self.xnT[sn]
                xt_v = xt.t.rearrange("k p t -> p k t")
                rope_v = self.c["rope16"].t.rearrange("(c j p) r -> c p j r", j=4, p=128)
                for c in range(L // 512):
                    b = it % 2
                    it += 1
                    x_, cs_ = xc[b], cs[b]
                    k.dma(k.sp, x_[:], xt_v[:, :, 2 + c * 512:2 + (c + 1) * 512], [xt], [x_])
                    k.dma(k.sp, cs_[:], rope_v[c], [self.c["rope16"]], [cs_])
                    qT_, kT_, vo_, ag_ = qTs[b], kTs[b], vo[b], agT[b]
                    for j in range(4):
                        for tq, dstT in ((0, qT_), (1, kT_)):
                            for hg_ in range(2):
                                tb_ = tt % 4
                                t3_ = tt % 3
                                tt += 1
                                p_ = nps()
                                c0 = tq * 1024 + hg_ * 512
                                k.mm([dict(out=p_[:], lhsT=x_[:, kk, j * 128:(j + 1) * 128], rhs=Wsb[:, kk, c0:c0 + 512],
                                           start=(kk == 0), stop=(kk == 7)) for kk in range(8)], R=[x_, Wsb], W=[p_])
                                q_ = qbf[tb_]
                                cc = sview(cs_, j * 64, [(0, 4), (1, 32)])
                                s1 = sview(cs_, j * 64 + 32, [(0, 4), (1, 16)])
                                s2 = sview(cs_, j * 64 + 48, [(0, 4), (1, 16)])
                                self.rope_tm(p_, 4, 128, 32, cc, (s1, s2), q_, 0, tmpa[tb_], tmpb[tb_], extra_R=[cs_])
                                pt = psT[t3_]

                                def stageB(pt=pt, q_=q_, j=j, hg_=hg_, dstT=dstT):
                                    k.tr([(pt[:, blk, :], q_[:, blk * 128:(blk + 1) * 128], ident[:]) for blk in range(4)], R=[q_, ident], W=[pt])
                                    k.op(k.act if hg_ else k.dve, lambda: (nc.scalar.copy if hg_ else nc.vector.tensor_copy)(
                                        out=dstT[:, hg_ * 4:(hg_ + 1) * 4, j * 128:(j + 1) * 128], in_=pt[:, 0:4, :]), R=[pt], W=[dstT])
                                pipe.push(stageB)
                        for vg in range(2):
                            p_ = nps()
                            c0 = 2048 + vg * 512
                            k.mm([dict(out=p_[:], lhsT=x_[:, kk, j * 128:(j + 1) * 128], rhs=Wsb[:, kk, c0:c0 + 512],
                                       start=(kk == 0), stop=(kk == 7)) for kk in range(8)], R=[x_, Wsb], W=[p_])
                            k.op(k.act, lambda p_=p_, j=j, vg=vg: nc.scalar.copy(out=vo_[:, j, vg * 512:(vg + 1) * 512], in_=p_[:]), R=[p_], W=[vo_])
                    pipe.flush()
                    if g == 0:
                        for cb in range(8):
                            p_ = nps()
                            k.mm([dict(out=p_[:], lhsT=Wsb[:, kk, 3072 + cb * 128:3072 + (cb + 1) * 128], rhs=x_[:, kk, :],
                                       start=(kk == 0), stop=(kk == 7)) for kk in range(8)], R=[x_, Wsb], W=[p_])
                            k.op(k.act, lambda p_=p_, cb=cb: nc.scalar.activation(out=ag_[:, cb, :], in_=p_[:], func=AF.Silu), R=[p_], W=[ag_])
                        k.dma(k.pool, self.CGT[sn].t.rearrange("(b p) t -> p b t", p=128)[:, :, c * 512:(c + 1) * 512], ag_[:], [ag_], [self.CGT[sn]])
                    sl = slice(c * 512, (c + 1) * 512)
                    k.dma(k.pool, self.QTc[sn, g].t.rearrange("(b p) t -> p b t", p=128)[:, :, sl], qT_[:], [qT_], [self.QTc[sn, g]])
                    k.dma(k.pool, self.KTc[sn, g].t.rearrange("(b p) t -> p b t", p=128)[:, :, pad + c * 512:pad + (c + 1) * 512], kT_[:], [kT_],
                          [self.KTc[sn, g]])
                    k.dma(k.pool, self.Vc[sn, g].t[pad + c * 512:pad + (c + 1) * 512, :].rearrange("(j p) d -> p j d", p=128), vo_[:], [vo_],
                          [self.Vc[sn, g]])
            k.barrier()

    def phase_dilated_attn(self):
        k, nc = self.k, self.nc
        CT = 2048
        SC = float(128 ** -0.5)
        with contextlib.ExitStack() as es:
            masks = k.sb(es, "masks", [128, 2, 128], BF16)
            k.dma(k.sp, masks[:, 0, :], self.c["maskA"].t[:, :], [self.c["maskA"]], [masks])
            k.dma(k.sp, masks[:, 1, :], self.c["maskB"].t[:, :], [self.c["maskB"]], [masks])
            ones = self.load_const(es, "ones", [128, 128], BF16)
            vlo = self.load_const(es, "vlo", [128, 128], BF16)
            vhi = self.load_const(es, "vhi", [128, 128], BF16)
            q_sb = [k.sb(es, "dq%d" % i, [128, CT], BF16) for i in range(2)]
            k_sb = [k.sb(es, "dk%d" % i, [128, CT + 2048], BF16) for i in range(2)]
            v_sb = [k.sb(es, "dv%d" % i, [128, 32, 128], BF16) for i in range(2)]
            g_sb = [k.sb(es, "dg%d" % i, [128, CT], BF16) for i in range(2)]
            accn = [k.sb(es, "accn%d" % i, [128, CT], F32) for i in range(2)]
            accd = [k.sb(es, "accd%d" % i, [128, CT], F32) for i in range(2)]
            mo = [k.sb(es, "dmo%d" % i, [128, CT], BF16) for i in range(2)]
            pT = [k.sb(es, "dpT%d" % i, [128, 512], BF16) for i in range(3)]
            psS = [k.ps(es, "dpsS%d" % i, [128, 512], F32) for i in range(3)]
            psO = [k.ps(es, "dpsO%d" % i, [128, 512], F32) for i in range(3)]
            hi = 0
            li_ = 0
            si = 0
            pipe = Pipe(1)
            for (sn, L) in self.seqs:
                for t0 in range(0, L, CT):
                    for h in range(8):
                        an, ad, g_, mo_ = accn[hi % 2], accd[hi % 2], g_sb[hi % 2], mo[hi % 2]
                        hi += 1
                        k.dma(k.sp, g_[:], self.CGT[sn].t[h * 128:(h + 1) * 128, t0:t0 + CT], [self.CGT[sn]], [g_])
                        for g, dl in enumerate((1, 4, 16)):
                            pad = 64 * dl
                            Ls_ = L // dl
                            q_, k_, v_ = q_sb[li_ % 2], k_sb[li_ % 2], v_sb[li_ % 2]
                            li_ += 1
                            k.dma(k.sp, q_[:], self.QTc[sn, g].t[h * 128:(h + 1) * 128, t0:t0 + CT], [self.QTc[sn, g]], [q_])
                            k.dma(k.sp, k_[:, 0:CT + 2 * pad], self.KTc[sn, g].t[h * 128:(h + 1) * 128, t0:t0 + CT + 2 * pad],
                                  [self.KTc[sn, g]], [k_])
                            nblk = CT // (128 * dl)
                            nm = nblk + 1
                            V = self.Vc[sn, g]
                            for r in range(dl):
                                k.dma(k.sp, v_[:, r * nm:(r + 1) * nm, :],
                                      dap(V, (t0 + r) * D + h * 128, [[dl * D, 128], [128 * dl * D, nm], [1, 128]]), [V], [v_])
                            if dl == 16:
                                pairs = [((r, 0), (r + 1, 0)) for r in range(0, 16, 2)]
                            else:
                                pairs = [((r, b), (r, b + 1)) for r in range(dl) for b in range(0, nblk, 2)]
                            for pr in pairs:
                                pS, pT_ = psS[si % 3], pT[si % 3]
                                pO = psO[si % 3]
                                si += 1
                                mms = []
                                for ti, (r, b) in enumerate(pr):
                                    qcol = 128 * b * dl + r
                                    for kt in range(2):
                                        kcol = 128 * (b + kt) * dl + r
                                        mms.append(dict(out=pS[:, (ti * 2 + kt) * 128:(ti * 2 + kt + 1) * 128],
                                                        lhsT=sview(k_, kcol, [(dl, 128)]), rhs=sview(q_, qcol, [(dl, 128)]),
                                                        start=True, stop=True))
                                k.mm(mms, R=[k_, q_], W=[pS])
                                k.op(k.act, lambda pS=pS, pT_=pT_: nc.scalar.activation(out=pT_[:], in_=pS[:], func=AF.Exp, scale=SC),
                                     R=[pS], W=[pT_])
                                k.op(k.dve, lambda pT_=pT_: nc.vector.tensor_tensor(
                                    out=sview(pT_, 0, [(256, 2), (1, 256)]), in0=sview(pT_, 0, [(256, 2), (1, 256)]),
                                    in1=sview(masks, 0, [(0, 2), (1, 256)]), op=ALU.mult), R=[pT_, masks], W=[pT_])
                                def stageB(pr=pr, pO=pO, pT_=pT_, v_=v_, nm=nm, dl=dl, g=g, t0=t0, Ls_=Ls_, an=an, ad=ad):
                                    mms = []
                                    for ti, (r, b) in enumerate(pr):
                                        for kt in range(2):
                                            mms.append(dict(out=pO[:, ti * 128:(ti + 1) * 128], lhsT=v_[:, r * nm + b + kt, :],
                                                            rhs=pT_[:, (ti * 2 + kt) * 128:(ti * 2 + kt + 1) * 128], start=(kt == 0), stop=(kt == 1)))
                                    for ti, (r, b) in enumerate(pr):
                                        for kt in range(2):
                                            s_lo = (t0 // dl) + 128 * (b + kt) - 64
                                            vl = vlo if s_lo < 0 else (vhi if s_lo + 128 > Ls_ else ones)
                                            mms.append(dict(out=pO[:, 256 + ti * 128:256 + (ti + 1) * 128], lhsT=vl[:],
                                                            rhs=pT_[:, (ti * 2 + kt) * 128:(ti * 2 + kt + 1) * 128], start=(kt == 0), stop=(kt == 1)))
                                    k.mm(mms, R=[v_, pT_, ones, vlo, vhi], W=[pO])
                                    (r0, b0), (r1, b1) = pr
                                    c0 = 128 * b0 * dl + r0
                                    step = (128 * b1 * dl + r1) - c0
                                    av_n = sview(an, c0, [(step, 2), (dl, 128)])
                                    av_d = sview(ad, c0, [(step, 2), (dl, 128)])
                                    pn = sview(pO, 0, [(128, 2), (1, 128)])
                                    pd = sview(pO, 256, [(128, 2), (1, 128)])
                                    if g == 0:
                                        k.op(k.dve, lambda av_n=av_n, pn=pn: nc.vector.tensor_copy(out=av_n, in_=pn), R=[pO], W=[an])
                                        k.op(k.act, lambda av_d=av_d, pd=pd: nc.scalar.copy(out=av_d, in_=pd), R=[pO], W=[ad])
                                    else:
                                        k.op(k.dve, lambda av_n=av_n, pn=pn: nc.vector.tensor_tensor(out=av_n, in0=av_n, in1=pn, op=ALU.add),
                                             R=[pO, an], W=[an])
                                        k.op(k.dve, lambda av_d=av_d, pd=pd: nc.vector.tensor_tensor(out=av_d, in0=av_d, in1=pd, op=ALU.add),
                                             R=[pO, ad], W=[ad])
                                pipe.push(stageB)
                        pipe.flush()
                        k.op(k.dve, lambda ad=ad: nc.vector.reciprocal(out=ad[:], in_=ad[:]), R=[ad], W=[ad])
                        k.op(k.pool, lambda an=an, ad=ad: nc.gpsimd.tensor_tensor(out=an[:], in0=an[:], in1=ad[:], op=ALU.mult), R=[an, ad], W=[an])
                        k.op(k.pool, lambda an=an, g_=g_, mo_=mo_: nc.gpsimd.tensor_tensor(out=mo_[:], in0=an[:], in1=g_[:], op=ALU.mult),
                             R=[an, g_], W=[mo_])
                        k.dma(k.pool, self.MT[sn].t[h * 128:(h + 1) * 128, t0:t0 + CT], mo_[:], [mo_], [self.MT[sn]])
            k.barrier()

    def build_all(self):
        self.phase_init_pads()
        cur = self.x_in
        bufs = [self.xa, self.xb]
        for layer in range(self.depth):
            nxt = bufs[layer % 2]
            li = layer // 2
            self.phase_norm(cur)
            if layer % 2 == 0:
                self.phase_even_proj(li)
                self.phase_filter(li)
                self.phase_hyena(li)
                self.phase_hy_transpose()
                self.phase_band_attn(li)
                self.phase_outproj("a_w_out", li, cur, nxt)
            else:
                for g in range(3):
                    self.phase_odd_proj(li, g)
                self.phase_dilated_attn()
                self.phase_outproj("c_w_out", li, cur, nxt)
            cur = nxt
        self.phase_norm(cur, final=True)
        self.finish()


_PROG_CACHE = {}


def kernel(**inputs):
    x_prompt = np.asarray(inputs["x_prompt"], dtype=np.float32)
    x_sample = np.asarray(inputs["x_sample"], dtype=np.float32)
    Lp, Ls = x_prompt.shape[1], x_sample.shape[1]
    nsamp = x_sample.shape[0]
    ns = nsamp // NCORES
    key = (Lp, Ls, ns)
    if key not in _PROG_CACHE:
        P = Prog(Lp, Ls, ns, depth=4)
        P.build_all()
        _PROG_CACHE[key] = P
    P = _PROG_CACHE[key]
    in_maps = [P.in_map(x_prompt[0], x_sample[c * ns:(c + 1) * ns], inputs) for c in range(NCORES)]
    res = run_bass_kernel_spmd(P.nc, in_maps, core_ids=list(range(NCORES)))
    y_prompt = np.asarray(res.results[0]["y_p"], dtype=np.float32)[None]
    y_sample = np.stack([np.asarray(res.results[c]["y_s%d" % i], dtype=np.float32) for c in range(NCORES) for i in range(ns)])
    return (y_prompt, y_sample)
```

```python
import contextlib
import math
import numpy as np
import ml_dtypes
import concourse.bass as bass
import concourse.mybir as mybir
from concourse.bass_utils import run_bass_kernel_spmd

F32, BF16 = mybir.dt.float32, mybir.dt.bfloat16
AF = mybir.ActivationFunctionType
ALU = mybir.AluOpType
AX = mybir.AxisListType

D = 1024
HC = 512
NCORES = 8
EVEN_IN = 3328
ODD_IN = 10240
EPS = 1e-6
SAFE_SAME_ENGINE = True


class Res:
    __slots__ = ("name", "t", "writers", "readers", "dsem", "is_dram", "is_psum")

    def __init__(self, name, t, is_dram=False):
        self.name, self.t = name, t
        self.writers = {}
        self.readers = {}
        self.dsem = {}
        self.is_dram = is_dram
        self.is_psum = False

    def __getitem__(self, idx):
        return self.t[idx]


class DSem:
    __slots__ = ("h", "cnt")

    def __init__(self, h):
        self.h, self.cnt = h, 0


class EngQ:
    def __init__(self, nc, eng, name, is_pe=False):
        self.eng, self.name, self.is_pe = eng, name, is_pe
        self.sem = nc.alloc_semaphore("sem_" + name)
        self.n = 0
        self.seen = {}


class K:
    def __init__(self, nc):
        self.nc = nc
        self.pe = EngQ(nc, nc.tensor, "pe", True)
        self.act = EngQ(nc, nc.scalar, "act")
        self.dve = EngQ(nc, nc.vector, "dve")
        self.pool = EngQ(nc, nc.gpsimd, "pool")
        self.sp = EngQ(nc, nc.sync, "sp")
        self.engs = [self.pe, self.act, self.dve, self.pool, self.sp]
        self.free_dsems = {}
        self.all_dsems = []
        self.dsem_of = {}
        self.live = []
        self.ninst = 0
        self.uid = 0

    def sb(self, es, name, shape, dt):
        self.uid += 1
        name = "%s_%d" % (name, self.uid)
        t = es.enter_context(self.nc.sbuf_tensor(name, list(shape), dt))
        r = Res(name, t)
        self.live.append(r)
        return r

    def ps(self, es, name, shape, dt=F32):
        self.uid += 1
        name = "%s_%d" % (name, self.uid)
        t = es.enter_context(self.nc.psum_tensor(name, list(shape), dt))
        r = Res(name, t)
        r.is_psum = True
        self.live.append(r)
        return r

    def dram(self, name, shape, dt, kind="Internal"):
        t = self.nc.dram_tensor(name, list(shape), dt, kind=kind)
        return Res(name, t.ap(), is_dram=True)

    def _get_dsem(self, r, q):
        d = r.dsem.get(q.name)
        if d is None:
            fl = self.free_dsems.setdefault(q.name, [])
            if fl:
                d = fl.pop()
            else:
                h = self.nc.alloc_semaphore("dsem%d" % len(self.all_dsems))
                d = DSem(h)
                self.all_dsems.append(d)
                self.dsem_of[h] = d
            r.dsem[q.name] = d
        return d

    def _wait(self, q, sem, val):
        if sem is q.sem and (q.is_pe or not SAFE_SAME_ENGINE):
            return
        ds = self.dsem_of.get(sem)
        if ds is not None:
            val = ds.cnt
        if q.seen.get(sem, 0) >= val:
            return
        q.eng.wait_ge(sem, val)
        q.seen[sem] = val
        self.ninst += 1

    @staticmethod
    def _rw(R, W):
        R2 = [r for r in R if not r.is_psum]
        W2 = list(W) + [r for r in R if r.is_psum and r not in W]
        return R2, W2

    def _deps(self, q, R, W, dma_sem=None):
        for r in R:
            for s, v in r.writers.items():
                self._wait(q, s, v)
        for w in W:
            for s, v in w.writers.items():
                if dma_sem is not None and s is dma_sem and not w.readers:
                    continue
                self._wait(q, s, v)
            for s, v in w.readers.items():
                self._wait(q, s, v)

    def _mark(self, tok, R, W):
        s, v = tok
        for r in R:
            if r.readers.get(s, 0) < v:
                r.readers[s] = v
        for w in W:
            if w.is_dram:
                w.writers[s] = v
            else:
                w.writers = {s: v}
                w.readers = {}

    def op(self, q, fn, R=(), W=()):
        R, W = self._rw(R, W)
        self._deps(q, R, W)
        ins = fn()
        q.n += 1
        ins.then_inc(q.sem, 1)
        self.ninst += 1
        self._mark((q.sem, q.n), R, W)

    def mm(self, mms, R=(), W=()):
        q = self.pe
        self._deps(q, R, W)
        ins = None
        for kw in mms:
            ins = self.nc.tensor.matmul(**kw)
        self.ninst += len(mms)
        q.n += 1
        ins.then_inc(q.sem, 1)
        self._mark((q.sem, q.n), R, W)

    def tr(self, trs, R=(), W=()):
        q = self.pe
        self._deps(q, R, W)
        ins = None
        for (o, i, ident) in trs:
            ins = self.nc.tensor.transpose(out=o, in_=i, identity=ident)
        self.ninst += len(trs)
        q.n += 1
        ins.then_inc(q.sem, 1)
        self._mark((q.sem, q.n), R, W)

    def dma(self, q, out, in_, R, W, slow=False):
        assert len(W) == 1
        owner = W[0] if not W[0].is_dram else R[0]
        assert not owner.is_dram
        ds = self._get_dsem(owner, q)
        self._deps(q, R, W, dma_sem=ds.h)
        ds.cnt += 16
        q.eng.dma_start(out=out, in_=in_, allow_slow_non_contiguous=slow).then_inc(ds.h, 16)
        self.ninst += 1
        self._mark((ds.h, ds.cnt), R, W)

    def barrier(self):
        toks = [(e.sem, e.n) for e in self.engs if e.n > 0]
        toks += [(d.h, d.cnt) for d in self.all_dsems if d.cnt > 0]
        for q in self.engs:
            for tok in toks:
                if tok[0] is q.sem:
                    continue
                if q.seen.get(tok[0], 0) >= tok[1]:
                    continue
                q.eng.wait_ge(tok[0], tok[1])
                q.seen[tok[0]] = tok[1]
        for r in self.live:
            for qn, d in r.dsem.items():
                self.free_dsems[qn].append(d)
            r.dsem = {}
        self.live = []


class Pipe:
    def __init__(self, depth=1):
        self.q, self.depth = [], depth

    def push(self, fn):
        self.q.append(fn)
        while len(self.q) > self.depth:
            self.q.pop(0)()

    def flush(self):
        while self.q:
            self.q.pop(0)()


def dap(res, offset, pairs):
    return bass.AP(res.t.tensor, offset, [list(p) for p in pairs])


def rope_table(L, rot):
    half = rot // 2
    inv = np.power(np.float32(500000.0), -2.0 * np.arange(half, dtype=np.float32) / np.float32(rot)).astype(np.float32)
    ang = (np.arange(L, dtype=np.float32)[:, None] * inv[None, :]).astype(np.float32)
    co, si = np.cos(ang), np.sin(ang)
    return np.concatenate([co, co, -si, si], axis=1).astype(np.float32)


def filter_feats(L):
    t = np.linspace(0.0, 1.0, L, dtype=np.float32)[:, None]
    wpos = (2.0 * math.pi * np.arange(L, dtype=np.float32)[:, None] / L).astype(np.float32)
    bands = np.linspace(1e-4, 15, 16, dtype=np.float32)[None, :]
    feats = np.concatenate([t, np.cos(bands * wpos), -np.sin(bands * wpos)], axis=-1).astype(np.float32)
    return feats


def decay_deltas():
    max_decay = math.log(1e-2) / 0.3
    min_decay = math.log(1e-2) / 1.5
    return np.abs(np.linspace(min_decay, max_decay, HC, dtype=np.float32)).astype(np.float32)


def const_tables(Ls_list):
    c = {}
    ident = np.eye(128, dtype=np.float32)
    c["ident"] = ident.astype(ml_dtypes.bfloat16)
    c["antiid"] = ident[::-1].copy().astype(ml_dtypes.bfloat16)
    j = np.arange(128)[:, None]
    i = np.arange(128)[None, :]
    c["maskA"] = (j >= i).astype(np.float32).astype(ml_dtypes.bfloat16)
    c["maskB"] = (j <= i).astype(np.float32).astype(ml_dtypes.bfloat16)
    lo = np.zeros((128, 128), np.float32)
    lo[64:, :] = 1.0
    hi = np.zeros((128, 128), np.float32)
    hi[:64, :] = 1.0
    c["ones"] = np.ones((128, 128), np.float32).astype(ml_dtypes.bfloat16)
    c["vlo"] = lo.astype(ml_dtypes.bfloat16)
    c["vhi"] = hi.astype(ml_dtypes.bfloat16)
    Lmax = max(Ls_list)
    c["rope8"] = rope_table(Lmax, 16)
    c["rope16"] = rope_table(Lmax, 32)
    for L in sorted(set(Ls_list)):
        f = filter_feats(L)
        c["featsT_%d" % L] = np.ascontiguousarray(f.T)
        c["featsTr_%d" % L] = np.ascontiguousarray(f[::-1].T)
        t = np.linspace(0.0, 1.0, L, dtype=np.float32)
        c["trow_%d" % L] = np.stack([t, t[::-1]]).astype(np.float32)
    c["negdelta"] = (-decay_deltas()).reshape(4, 128).T.copy()
    return c


def pstep(res):
    return res.t[:].ap[0][0]


def sview(res, off, dims, nparts=128, p0=0):
    ps_ = pstep(res)
    return bass.AP(res.t[:].tensor, p0 * ps_ + off, [[ps_, nparts]] + [list(d) for d in dims])


class Prog:
    GP = 8192

    def __init__(self, Lp, Ls, ns, depth=4, debug=()):
        self.Lp, self.Ls, self.ns, self.depth = Lp, Ls, ns, depth
        self.debug = set(debug)
        self.nc = bass.Bass("TRN2", target_bir_lowering=False)
        self.k = K(self.nc)
        self.seqs = [("p", Lp)] + [("s%d" % i, Ls) for i in range(ns)]
        self.inputs = {}
        self.outputs = {}
        self.consts = const_tables([Lp, Ls])
        self._declare()

    def inp(self, name, shape, dt):
        r = self.k.dram(name, shape, dt, kind="ExternalInput")
        self.inputs[name] = r
        return r

    def scratch(self, name, shape, dt):
        kind = "ExternalOutput" if name in self.debug else "Internal"
        r = self.k.dram(name, shape, dt, kind=kind)
        if kind == "ExternalOutput":
            self.outputs[name] = r
        return r

    def _declare(self):
        ne, no = (self.depth + 1) // 2, self.depth // 2
        self.ne, self.no = ne, no
        i = self.inp
        self.x_in = {"p": i("x_p", [self.Lp, D], F32)}
        xs = i("x_s", [self.ns, self.Ls, D], F32)
        for s in range(self.ns):
            r = Res("x_s%d" % s, xs.t[s], is_dram=True)
            self.x_in["s%d" % s] = r
        self.w = {}
        for name, shape in [
            ("a_norm", [ne, D]), ("a_w_in", [ne, D, EVEN_IN]), ("a_conv_w", [ne, 3, 3 * HC]),
            ("a_conv_b", [ne, 3 * HC]), ("a_filt_w1", [ne, 33, 64]), ("a_filt_b1", [ne, 64]),
            ("a_filt_f1", [ne, 64]), ("a_filt_w2", [ne, 64, 64]), ("a_filt_b2", [ne, 64]),
            ("a_filt_f2", [ne, 64]), ("a_filt_w3", [ne, 64, 4 * HC]), ("a_hyena_d", [ne, 2, HC]),
            ("a_sink", [ne, 8]), ("a_w_out", [ne, D, D]), ("c_norm", [max(no, 1), D]),
            ("c_w_in", [max(no, 1), D, ODD_IN]), ("c_w_out", [max(no, 1), D, D]), ("final_norm", [1, D]),
        ]:
            self.w[name] = i(name, shape, F32)
        self.c = {}
        for name, arr in self.consts.items():
            dt = BF16 if arr.dtype == ml_dtypes.bfloat16 else F32
            self.c[name] = i("c_" + name, list(arr.shape), dt)
        self.y = {}
        for (sn, L) in self.seqs:
            r = self.k.dram("y_" + sn, [L, D], F32, kind="ExternalOutput")
            self.outputs["y_" + sn] = r
            self.y[sn] = r
        sc = self.scratch
        self.xa, self.xb, self.xnT = {}, {}, {}
        self.HY, self.HG, self.QT, self.KT, self.VB, self.AGT, self.MT, self.HYO = {}, {}, {}, {}, {}, {}, {}, {}
        for (sn, L) in self.seqs:
            self.xa[sn] = sc("xa_" + sn, [L, D], F32)
            self.xb[sn] = sc("xb_" + sn, [L, D], F32)
            self.xnT[sn] = sc("xnT_" + sn, [8, 128, L + 4], BF16)
            self.HY[sn] = sc("HY_" + sn, [L, 3 * HC], F32)
            self.HG[sn] = sc("HG_" + sn, [L, HC], F32)
            self.QT[sn] = sc("QT_" + sn, [512, L], BF16)
            self.KT[sn] = sc("KT_" + sn, [128, L], BF16)
            self.VB[sn] = sc("VB_" + sn, [L, 128], BF16)
            self.AGT[sn] = sc("AGT_" + sn, [512, L], BF16)
            self.MT[sn] = sc("MT_" + sn, [D, L], BF16)
            self.HYO[sn] = sc("HYO_" + sn, [L, HC], BF16)
        self.QTc, self.KTc, self.Vc, self.CGT = {}, {}, {}, {}
        for (sn, L) in self.seqs:
            for g, dl in enumerate((1, 4, 16)):
                pad = 64 * dl
                self.QTc[sn, g] = sc("QTc%d_%s" % (g, sn), [D, L], BF16)
                self.KTc[sn, g] = sc("KTc%d_%s" % (g, sn), [D, L + 2 * pad], BF16)
                self.Vc[sn, g] = sc("Vc%d_%s" % (g, sn), [L + 2 * pad, D], BF16)
            self.CGT[sn] = sc("CGT_" + sn, [D, L], BF16)
        self.A, self.Dx = {}, {}
        for L in sorted(set([self.Lp, self.Ls])):
            self.A[L] = sc("A_%d" % L, [2, HC, 2 * L], BF16)
            self.Dx[L] = sc("Dx_%d" % L, [2, HC], F32)

    def phase_norm(self, src, gname=None, final=False):
        k, nc = self.k, self.nc
        with contextlib.ExitStack() as es:
            ident = k.sb(es, "ident", [128, 128], BF16)
            k.dma(k.sp, ident[:], self.c["ident"].t[:, :], [self.c["ident"]], [ident])
            zc = k.sb(es, "zc", [128, 8, 2], BF16)
            k.op(k.dve, lambda: nc.vector.memset(zc[:], 0.0), W=[zc])
            xin = [k.sb(es, "xin%d" % i, [128, 4, D], F32) for i in range(2)]
            sq = [k.sb(es, "sq%d" % i, [128, D], F32) for i in range(2)]
            ss = [k.sb(es, "ss%d" % i, [128, 4], F32) for i in range(2)]
            rs = [k.sb(es, "rs%d" % i, [128, 4], F32) for i in range(2)]
            if final:
                gb = k.sb(es, "gb", [128, D], F32)
                k.dma(k.sp, gb[:], dap(self.w["final_norm"], 0, [[0, 128], [1, D]]), [self.w["final_norm"]], [gb])
                yo = [k.sb(es, "yo%d" % i, [128, 4, D], F32) for i in range(2)]
            else:
                xs = [k.sb(es, "xs%d" % i, [128, 4, D], BF16) for i in range(2)]
                xo = [k.sb(es, "xo%d" % i, [128, 8, 512], BF16) for i in range(2)]
                pst = [k.ps(es, "pt%d" % i, [128, 8, 128], BF16) for i in range(4)]
            it = 0
            for (sn, L) in self.seqs:
                x = src[sn]
                if not final:
                    xt = self.xnT[sn]
                    xt_v = xt.t.rearrange("k p t -> p k t")
                    k.dma(k.pool, xt_v[:, :, 0:2], zc[:], [zc], [xt], slow=True)
                    k.dma(k.pool, xt_v[:, :, L + 2:L + 4], zc[:], [zc], [xt], slow=True)
                xv = x.t.rearrange("(c j p) d -> c p j d", j=4, p=128)
                for c in range(L // 512):
                    b = it % 2
                    it += 1
                    xi, s_, r_, q_ = xin[b], ss[b], rs[b], sq[b]
                    k.dma(k.sp, xi[:], xv[c], [x], [xi])
                    k.op(k.dve, lambda: nc.vector.memset(s_[:], 0.0), W=[s_])
                    for j in range(4):
                        k.op(k.act, lambda j=j: nc.scalar.activation(out=q_[:], in_=xi[:, j, :], func=AF.Square,
                                                                     accum_out=s_[:, j:j + 1]), R=[xi], W=[q_, s_])
                    k.op(k.act, lambda: nc.scalar.activation(out=r_[:], in_=s_[:], func=AF.Sqrt, bias=EPS, scale=1.0 / D),
                         R=[s_], W=[r_])
                    k.op(k.dve, lambda: nc.vector.reciprocal(out=r_[:], in_=r_[:]), R=[r_], W=[r_])
                    if final:
                        y_ = yo[b]
                        for j in range(4):
                            k.op(k.dve, lambda j=j: nc.vector.scalar_tensor_tensor(
                                out=y_[:, j, :], in0=xi[:, j, :], scalar=r_[:, j:j + 1], in1=gb[:],
                                op0=ALU.mult, op1=ALU.mult), R=[xi, r_, gb], W=[y_])
                        yv = self.y[sn].t.rearrange("(c j p) d -> c p j d", j=4, p=128)
                        k.dma(k.pool, yv[c], y_[:], [y_], [self.y[sn]])
                        continue
                    xs_, xo_ = xs[b], xo[b]
                    for j in range(4):
                        if j % 2 == 0:
                            k.op(k.act, lambda j=j: nc.scalar.activation(out=xs_[:, j, :], in_=xi[:, j, :], func=AF.Copy,
                                                                         scale=r_[:, j:j + 1]), R=[xi, r_], W=[xs_])
                        else:
                            k.op(k.dve, lambda j=j: nc.vector.tensor_scalar(out=xs_[:, j, :], in0=xi[:, j, :],
                                                                            scalar1=r_[:, j:j + 1], scalar2=None,
                                                                            op0=ALU.mult), R=[xi, r_], W=[xs_])
                    for j in range(4):
                        pt = pst[(it * 4 + j) % 4]
                        k.tr([(pt[:, kk, :], xs_[:, j, kk * 128:(kk + 1) * 128], ident[:]) for kk in range(8)],
                             R=[xs_, ident], W=[pt])
                        if j % 2 == 0:
                            k.op(k.dve, lambda j=j, pt=pt: nc.vector.tensor_copy(out=xo_[:, :, j * 128:(j + 1) * 128], in_=pt[:]),
                                 R=[pt], W=[xo_])
                        else:
                            k.op(k.act, lambda j=j, pt=pt: nc.scalar.copy(out=xo_[:, :, j * 128:(j + 1) * 128], in_=pt[:]),
                                 R=[pt], W=[xo_])
                    k.dma(k.pool, xt_v[:, :, 2 + c * 512:2 + (c + 1) * 512], xo_[:], [xo_], [xt])
            k.barrier()

    def rope_tm(self, ps, nh, hd, rot, cs_ap_cc, cs_ap_ss, dst, dst_off, tmp_a, tmp_b, extra_R=()):
        k, nc = self.k, self.nc
        half = rot // 2
        x_all = sview(ps, 0, [(1, nh * hd)])
        xr = sview(ps, 0, [(hd, nh), (1, rot)])
        x1 = sview(ps, 0, [(hd, nh), (1, half)])
        x2 = sview(ps, half, [(hd, nh), (1, half)])
        ta = sview(tmp_a, 0, [(rot, nh), (1, rot)])
        tb1 = sview(tmp_b, 0, [(rot, nh), (1, half)])
        tb2 = sview(tmp_b, half, [(rot, nh), (1, half)])
        tb = sview(tmp_b, 0, [(rot, nh), (1, rot)])
        ss1 = cs_ap_ss[0]
        ss2 = cs_ap_ss[1]
        R0 = [ps] + list(extra_R)
        k.op(k.act, lambda: nc.scalar.copy(out=sview(dst, dst_off, [(1, nh * hd)]), in_=x_all), R=[ps], W=[dst])
        k.op(k.dve, lambda: nc.vector.tensor_tensor(out=ta, in0=xr, in1=cs_ap_cc, op=ALU.mult), R=R0, W=[tmp_a])
        k.op(k.dve, lambda: nc.vector.tensor_tensor(out=tb1, in0=x2, in1=ss1, op=ALU.mult), R=R0, W=[tmp_b])
        k.op(k.dve, lambda: nc.vector.tensor_tensor(out=tb2, in0=x1, in1=ss2, op=ALU.mult), R=R0, W=[tmp_b])
        k.op(k.dve, lambda: nc.vector.tensor_tensor(out=sview(dst, dst_off, [(hd, nh), (1, rot)]), in0=ta, in1=tb,
                                                    op=ALU.add), R=[tmp_a, tmp_b], W=[dst])

    def phase_even_proj(self, li, parts=("hy", "hg", "q", "kv", "ag")):
        k, nc = self.k, self.nc
        W = self.w["a_w_in"]
        with contextlib.ExitStack() as es:
            ident = k.sb(es, "ident", [128, 128], BF16)
            k.dma(k.sp, ident[:], self.c["ident"].t[:, :], [self.c["ident"]], [ident])
            Wsb = k.sb(es, "Wsb", [128, 8, 1792], BF16)
            Wtap = [k.sb(es, "Wtap%d" % t, [128, 8, 1536], BF16) for t in range(3)]
            biasb = k.sb(es, "biasb", [128, 1536], F32)
            k.dma(k.sp, biasb[:], dap(self.w["a_conv_b"], li * 1536, [[0, 128], [1, 1536]]), [self.w["a_conv_b"]], [biasb])
            with contextlib.ExitStack() as es2:
                gsb = k.sb(es2, "gsb", [128, 8, 1], F32)
                k.dma(k.sp, gsb[:], dap(self.w["a_norm"], li * D, [[1, 128], [128, 8], [1, 1]]), [self.w["a_norm"]], [gsb], slow=True)
                tapb = [k.sb(es2, "tapb%d" % t, [128, 1536], F32) for t in range(3)]
                for t in range(3):
                    k.dma(k.sp, tapb[t][:], dap(self.w["a_conv_w"], (li * 3 + t) * 1536, [[0, 128], [1, 1536]]),
                          [self.w["a_conv_w"]], [tapb[t]])
                stage = [k.sb(es2, "stage%d" % i, [128, 8, 256], F32) for i in range(2)]
                wv = W.t[li].rearrange("(k p) c -> p k c", p=128)
                for cb in range(13):
                    c0 = cb * 256
                    st = stage[cb % 2]
                    k.dma(k.sp, st[:], wv[:, :, c0:c0 + 256], [W], [st])
                    k.op(k.dve, lambda st=st: nc.vector.tensor_tensor(
                        out=st[:], in0=st[:], in1=sview(gsb, 0, [(1, 8), (0, 256)]), op=ALU.mult), R=[st, gsb], W=[st])
                    if c0 < 1536:
                        for t in range(3):
                            k.op(k.dve, lambda st=st, t=t, c0=c0: nc.vector.tensor_tensor(
                                out=Wtap[t][:, :, c0:c0 + 256], in0=st[:],
                                in1=sview(tapb[t], c0, [(0, 8), (1, 256)]), op=ALU.mult), R=[st, tapb[t]], W=[Wtap[t]])
                    else:
                        k.op(k.act, lambda st=st, c0=c0: nc.scalar.copy(out=Wsb[:, :, c0 - 1536:c0 - 1536 + 256], in_=st[:]),
                             R=[st], W=[Wsb])
                k.barrier()
            xc = [k.sb(es, "xc%d" % i, [128, 8, 516], BF16) for i in range(2)]
            xcB = [k.sb(es, "xcB%d" % i, [128, 8, 516], BF16) for i in range(2)]
            hyo = [k.sb(es, "hyo%d" % i, [128, 1536], F32) for i in range(2)]
            hgo = [k.sb(es, "hgo%d" % i, [128, 512], F32) for i in range(2)]
            qbf = [k.sb(es, "qbf%d" % i, [128, 512], BF16) for i in range(4)]
            kvbf = [k.sb(es, "kvbf%d" % i, [128, 256], BF16) for i in range(4)]
            qTs = [k.sb(es, "qTs%d" % i, [128, 4, 512], BF16) for i in range(2)]
            kTs = [k.sb(es, "kTs%d" % i, [128, 512], BF16) for i in range(2)]
            vo = [k.sb(es, "vo%d" % i, [128, 4, 128], BF16) for i in range(2)]
            agT = [k.sb(es, "agT%d" % i, [128, 4, 512], BF16) for i in range(2)]
            cs = [k.sb(es, "cs%d" % i, [128, 4, 32], F32) for i in range(2)]
            tmpa = [k.sb(es, "tmpa%d" % i, [128, 128], F32) for i in range(4)]
            tmpb = [k.sb(es, "tmpb%d" % i, [128, 128], F32) for i in range(4)]
            psA = [k.ps(es, "psA%d" % i, [128, 512], F32) for i in range(5)]
            psT = [k.ps(es, "psT%d" % i, [128, 8, 128], BF16) for i in range(3)]
            pipe = Pipe(1)
            t3 = [0]
            pa = [0]

            def nps():
                pa[0] += 1
                return psA[pa[0] % 5]
            it = 0
            tt = 0
            for (sn, L) in self.seqs:
                xt = self.xnT[sn]
                xt_v = xt.t.rearrange("k p t -> p k t")
                rope_v = self.c["rope8"].t.rearrange("(c j p) r -> c p j r", j=4, p=128)
                for c in range(L // 512):
                    b = it % 2
                    it += 1
                    x_ = xc[b]
                    k.dma(k.sp, x_[:], xt_v[:, :, c * 512:c * 512 + 516], [xt], [x_])
                    xB_ = xcB[b]
                    k.dma(k.sp, xB_[:, :, 0:514], xt_v[:, :, c * 512 + 1:c * 512 + 515], [xt], [xB_])
                    cs_ = cs[b]
                    k.dma(k.sp, cs_[:], rope_v[c], [self.c["rope8"]], [cs_])
                    qT_, kT_, vo_, ag_ = qTs[b], kTs[b], vo[b], agT[b]
                    for j in range(4):
                        tb_ = tt % 2
                        tt += 1
                        t0 = 2 + j * 128
                        hy_ = hyo[tb_]
                        tok0 = c * 512 + j * 128
                        for g in range(3 if "hy" in parts else 0):
                            p_ = nps()
                            mms = []
                            for t in range(3):
                                for kk in range(8):
                                    src_ = x_ if t == 1 else xB_
                                    o_ = t0 if t == 1 else (j * 128 + t)
                                    mms.append(dict(out=p_[:], lhsT=src_[:, kk, o_:o_ + 128],
                                                    rhs=Wtap[t][:, kk, g * 512:(g + 1) * 512],
                                                    start=(t == 0 and kk == 0), stop=(t == 2 and kk == 7)))
                            k.mm(mms, R=[x_, xB_] + Wtap, W=[p_])
                            k.op(k.dve, lambda p_=p_, g=g, hy_=hy_: nc.vector.tensor_tensor(
                                out=hy_[:, g * 512:(g + 1) * 512], in0=p_[:], in1=biasb[:, g * 512:(g + 1) * 512], op=ALU.add),
                                R=[p_, biasb], W=[hy_])
                        if "hy" in parts:
                            k.dma(k.pool, self.HY[sn].t[tok0:tok0 + 128, :], hy_[:], [hy_], [self.HY[sn]])
                        if "hg" not in parts:
                            continue
                        p_ = nps()
                        k.mm([dict(out=p_[:], lhsT=x_[:, kk, t0:t0 + 128], rhs=Wsb[:, kk, 0:512], start=(kk == 0), stop=(kk == 7))
                              for kk in range(8)], R=[x_, Wsb], W=[p_])
                        hg_ = hgo[tb_]
                        k.op(k.act, lambda p_=p_, hg_=hg_: nc.scalar.activation(out=hg_[:], in_=p_[:], func=AF.Silu), R=[p_], W=[hg_])
                        k.dma(k.pool, self.HG[sn].t[tok0:tok0 + 128, :], hg_[:], [hg_], [self.HG[sn]])
                        if "q" not in parts:
                            continue
                        p_ = nps()
                        k.mm([dict(out=p_[:], lhsT=x_[:, kk, t0:t0 + 128], rhs=Wsb[:, kk, 512:1024], start=(kk == 0), stop=(kk == 7))
                              for kk in range(8)], R=[x_, Wsb], W=[p_])
                        q_ = qbf[tt % 4]
                        cc = sview(cs_, j * 32, [(0, 8), (1, 16)])
                        s1 = sview(cs_, j * 32 + 16, [(0, 8), (1, 8)])
                        s2 = sview(cs_, j * 32 + 24, [(0, 8), (1, 8)])
                        self.rope_tm(p_, 8, 64, 16, cc, (s1, s2), q_, 0, tmpa[tt % 4], tmpb[tt % 4], extra_R=[cs_])
                        t3[0] += 1
                        pt = psT[t3[0] % 3]

                        def stageBq(pt=pt, q_=q_, j=j, qT_=qT_):
                            k.tr([(pt[:, blk, :], q_[:, blk * 128:(blk + 1) * 128], ident[:]) for blk in range(4)], R=[q_, ident], W=[pt])
                            k.op(k.act, lambda: nc.scalar.copy(out=qT_[:, :, j * 128:(j + 1) * 128], in_=pt[:, 0:4, :]), R=[pt], W=[qT_])
                        pipe.push(stageBq)
                        if "kv" not in parts:
                            continue
                        p_ = nps()
                        k.mm([dict(out=p_[:, 0:256], lhsT=x_[:, kk, t0:t0 + 128], rhs=Wsb[:, kk, 1024:1280], start=(kk == 0), stop=(kk == 7))
                              for kk in range(8)], R=[x_, Wsb], W=[p_])
                        kv_ = kvbf[tt % 4]
                        cc2 = sview(cs_, j * 32, [(0, 2), (1, 16)])
                        s12 = sview(cs_, j * 32 + 16, [(0, 2), (1, 8)])
                        s22 = sview(cs_, j * 32 + 24, [(0, 2), (1, 8)])
                        self.rope_tm(p_, 2, 64, 16, cc2, (s12, s22), kv_, 0, tmpa[(tt + 2) % 4], tmpb[(tt + 2) % 4], extra_R=[cs_])
                        k.op(k.dve, lambda p_=p_, vo_=vo_, j=j: nc.vector.tensor_copy(out=vo_[:, j, :], in_=p_[:, 128:256]), R=[p_], W=[vo_])
                        t3[0] += 1
                        pt2 = psT[t3[0] % 3]

                        def stageBk(pt2=pt2, kv_=kv_, j=j, kT_=kT_):
                            k.tr([(pt2[:, 4, :], kv_[:, 0:128], ident[:])], R=[kv_, ident], W=[pt2])
                            k.op(k.dve, lambda: nc.vector.tensor_copy(out=kT_[:, j * 128:(j + 1) * 128], in_=pt2[:, 4, :]), R=[pt2], W=[kT_])
                        pipe.push(stageBk)
                    pipe.flush()
                    for cb in range(4 if "ag" in parts else 0):
                        p_ = nps()
                        k.mm([dict(out=p_[:], lhsT=Wsb[:, kk, 1280 + cb * 128:1280 + (cb + 1) * 128], rhs=x_[:, kk, 2:514],
                                   start=(kk == 0), stop=(kk == 7)) for kk in range(8)], R=[x_, Wsb], W=[p_])
                        k.op(k.act, lambda p_=p_, cb=cb, ag_=ag_: nc.scalar.activation(out=ag_[:, cb, :], in_=p_[:], func=AF.Silu),
                             R=[p_], W=[ag_])
                    sl = slice(c * 512, (c + 1) * 512)
                    if "q" in parts:
                        k.dma(k.pool, self.QT[sn].t.rearrange("(b p) t -> p b t", p=128)[:, :, sl], qT_[:], [qT_], [self.QT[sn]])
                    if "kv" in parts:
                        k.dma(k.pool, self.KT[sn].t[:, sl], kT_[:], [kT_], [self.KT[sn]])
                        k.dma(k.pool, self.VB[sn].t.rearrange("(c j p) d -> c p j d", j=4, p=128)[c], vo_[:], [vo_], [self.VB[sn]])
                    if "ag" in parts:
                        k.dma(k.pool, self.AGT[sn].t.rearrange("(b p) t -> p b t", p=128)[:, :, sl], ag_[:], [ag_], [self.AGT[sn]])
            k.barrier()

    def finish(self):
        k = self.k
        k.barrier()

    def in_map(self, x_p, x_s, weights):
        m = {"x_p": np.ascontiguousarray(x_p, dtype=np.float32), "x_s": np.ascontiguousarray(x_s, dtype=np.float32)}
        for name in self.w:
            a = np.asarray(weights[name], dtype=np.float32)
            if name == "final_norm":
                a = a.reshape(1, D)
            m[name] = np.ascontiguousarray(a)
        for name, arr in self.consts.items():
            m["c_" + name] = arr
        return m

    def phase_outproj(self, wname, li, src, dst):
        k, nc = self.k, self.nc
        Wd = self.w[wname]
        with contextlib.ExitStack() as es:
            Wo = k.sb(es, "Wo", [128, 8, D], BF16)
            with contextlib.ExitStack() as es2:
                stage = [k.sb(es2, "stg%d" % i, [128, 8, 256], F32) for i in range(2)]
                wv = Wd.t[li].rearrange("(k p) c -> p k c", p=128)
                for cb in range(4):
                    st = stage[cb % 2]
                    k.dma(k.sp, st[:], wv[:, :, cb * 256:(cb + 1) * 256], [Wd], [st])
                    if cb % 2:
                        k.op(k.act, lambda st=st, cb=cb: nc.scalar.copy(out=Wo[:, :, cb * 256:(cb + 1) * 256], in_=st[:]), R=[st], W=[Wo])
                    else:
                        k.op(k.dve, lambda st=st, cb=cb: nc.vector.tensor_copy(out=Wo[:, :, cb * 256:(cb + 1) * 256], in_=st[:]), R=[st], W=[Wo])
                k.barrier()
            mt = [k.sb(es, "mt%d" % i, [128, 8, 512], BF16) for i in range(2)]
            xi = [k.sb(es, "xi%d" % i, [128, 4, D], F32) for i in range(2)]
            xo = [k.sb(es, "xo%d" % i, [128, 4, D], F32) for i in range(2)]
            ps = [k.ps(es, "po%d" % i, [128, 512], F32) for i in range(4)]
            it = 0
            pi = 0
            for (sn, L) in self.seqs:
                mtv = self.MT[sn].t.rearrange("(k p) t -> p k t", p=128)
                xv = src[sn].t.rearrange("(c j p) d -> c p j d", j=4, p=128)
                dv = dst[sn].t.rearrange("(c j p) d -> c p j d", j=4, p=128)
                for c in range(L // 512):
                    b = it % 2
                    it += 1
                    mt_, xi_, xo_ = mt[b], xi[b], xo[b]
                    k.dma(k.sp, mt_[:], mtv[:, :, c * 512:(c + 1) * 512], [self.MT[sn]], [mt_])
                    k.dma(k.sp, xi_[:], xv[c], [src[sn]], [xi_])
                    for j in range(4):
                        for g in range(2):
                            p_ = ps[pi % 4]
                            pi += 1
                            k.mm([dict(out=p_[:], lhsT=mt_[:, kk, j * 128:(j + 1) * 128], rhs=Wo[:, kk, g * 512:(g + 1) * 512],
                                       start=(kk == 0), stop=(kk == 7)) for kk in range(8)], R=[mt_, Wo], W=[p_])
                            k.op(k.dve, lambda p_=p_, j=j, g=g: nc.vector.tensor_tensor(
                                out=xo_[:, j, g * 512:(g + 1) * 512], in0=p_[:], in1=xi_[:, j, g * 512:(g + 1) * 512], op=ALU.add),
                                R=[p_, xi_], W=[xo_])
                    k.dma(k.pool, dv[c], xo_[:], [xo_], [dst[sn]])
            k.barrier()

    def load_const(self, es, name, shape, dt):
        k = self.k
        r = k.sb(es, name, shape, dt)
        k.dma(k.sp, r[:], self.c[name].t[:, :], [self.c[name]], [r])
        return r

    def phase_band_attn(self, li):
        k, nc = self.k, self.nc
        with contextlib.ExitStack() as es:
            maskA = self.load_const(es, "maskA", [128, 128], BF16)
            maskB = self.load_const(es, "maskB", [128, 128], BF16)
            ones = self.load_const(es, "ones", [128, 128], BF16)
            snk = k.sb(es, "snk", [64, 8], F32)
            k.dma(k.sp, snk[:], dap(self.w["a_sink"], li * 8, [[0, 64], [1, 8]]), [self.w["a_sink"]], [snk])
            k.op(k.act, lambda: nc.scalar.activation(out=snk[:], in_=snk[:], func=AF.Exp), R=[snk], W=[snk])
            q_sb = [k.sb(es, "q_sb%d" % i, [64, 8, 512], BF16) for i in range(2)]
            ag_sb = [k.sb(es, "ag_sb%d" % i, [64, 8, 512], BF16) for i in range(2)]
            k_sb = [k.sb(es, "k_sb%d" % i, [64, 2, 768], BF16) for i in range(2)]
            v_sb = [k.sb(es, "v_sb%d" % i, [128, 6, 128], BF16) for i in range(2)]
            mt_sb = [k.sb(es, "mt_sb%d" % i, [64, 8, 512], BF16) for i in range(2)]
            pT = [k.sb(es, "pT%d" % i, [128, 512], BF16) for i in range(3)]
            den = [k.sb(es, "den%d" % i, [64, 512], F32) for i in range(2)]
            o_sb = [k.sb(es, "o_sb%d" % i, [64, 512], F32) for i in range(2)]
            psS = [k.ps(es, "psS%d" % i, [128, 512], F32) for i in range(3)]
            psO = [k.ps(es, "psO%d" % i, [128, 512], F32) for i in range(2)]
            psD = [k.ps(es, "psD%d" % i, [128, 512], F32) for i in range(2)]
            it = 0
            u = 0
            si = 0
            pipe = Pipe(1)
            for (sn, L) in self.seqs:
                nq = L // 128
                qv = self.QT[sn].t.rearrange("(h d) t -> d h t", d=64)
                agv = self.AGT[sn].t.rearrange("(h d) t -> d h t", d=64)
                kv = self.KT[sn].t.rearrange("(h d) t -> d h t", d=64)
                vv = self.VB[sn].t.rearrange("(b p) d -> p b d", p=128)
                mv = self.MT[sn].t[512:1024, :].rearrange("(h d) t -> d h t", d=64)
                for c in range(L // 512):
                    b = it % 2
                    it += 1
                    t0 = c * 512
                    q_, ag_, k_, v_, mt_ = q_sb[b], ag_sb[b], k_sb[b], v_sb[b], mt_sb[b]
                    k.dma(k.sp, q_[:], qv[:, :, t0:t0 + 512], [self.QT[sn]], [q_])
                    k.dma(k.sp, ag_[:], agv[:, :, t0:t0 + 512], [self.AGT[sn]], [ag_])
                    ks, ke = max(0, t0 - 128), min(L, t0 + 640)
                    k.dma(k.sp, k_[:, :, ks - (t0 - 128):ke - (t0 - 128)], kv[:, :, ks:ke], [self.KT[sn]], [k_])
                    kb0, kb1 = max(0, c * 4 - 1), min(nq, c * 4 + 5)
                    k.dma(k.sp, v_[:, kb0 - (c * 4 - 1):kb1 - (c * 4 - 1), :], vv[:, kb0:kb1, :], [self.VB[sn]], [v_])
                    for jq in range(4):
                        qb = c * 4 + jq
                        for kvh in range(2):
                            kbs = [kb for kb in (qb - 1, qb, qb + 1) if 0 <= kb < nq]
                            pO, pD = psO[u % 2], psD[u % 2]
                            den_, o_ = den[u % 2], o_sb[u % 2]
                            u += 1
                            for i, kb in enumerate(kbs):
                                bl = kb - (c * 4 - 1)
                                pS, pT_ = psS[si % 3], pT[si % 3]
                                si += 1
                                k.mm([dict(out=pS[:], lhsT=k_[:, kvh, bl * 128:(bl + 1) * 128],
                                           rhs=q_[:, kvh * 4:(kvh + 1) * 4, jq * 128:(jq + 1) * 128], start=True, stop=True)],
                                     R=[k_, q_], W=[pS])
                                k.op(k.act, lambda pS=pS, pT_=pT_: nc.scalar.activation(out=pT_[:], in_=pS[:], func=AF.Exp, scale=0.125),
                                     R=[pS], W=[pT_])
                                if kb != qb:
                                    mk = maskA if kb < qb else maskB
                                    k.op(k.dve, lambda pT_=pT_, mk=mk: nc.vector.tensor_tensor(
                                        out=sview(pT_, 0, [(128, 4), (1, 128)]), in0=sview(pT_, 0, [(128, 4), (1, 128)]),
                                        in1=sview(mk, 0, [(0, 4), (1, 128)]), op=ALU.mult), R=[pT_, mk], W=[pT_])
                                st, sp_ = (i == 0), (i == len(kbs) - 1)

                                def stageB(pO=pO, pD=pD, v_=v_, bl=bl, kvh=kvh, pT_=pT_, st=st, sp_=sp_):
                                    k.mm([dict(out=pO[0:64, :], lhsT=v_[:, bl, kvh * 64:(kvh + 1) * 64], rhs=pT_[:], start=st, stop=sp_)],
                                         R=[v_, pT_], W=[pO])
                                    k.mm([dict(out=pD[0:64, :], lhsT=ones[:, 0:64], rhs=pT_[:], start=st, stop=sp_)], R=[ones, pT_], W=[pD])
                                pipe.push(stageB)

                            def stageN(pO=pO, pD=pD, den_=den_, o_=o_, kvh=kvh, jq=jq, mt_=mt_, ag_=ag_):
                                k.op(k.dve, lambda: nc.vector.tensor_tensor(
                                    out=sview(den_, 0, [(128, 4), (1, 128)], nparts=64), in0=sview(pD, 0, [(128, 4), (1, 128)], nparts=64),
                                    in1=sview(snk, kvh * 4, [(1, 4), (0, 128)], nparts=64), op=ALU.add), R=[pD, snk], W=[den_])
                                k.op(k.dve, lambda: nc.vector.reciprocal(out=den_[:], in_=den_[:]), R=[den_], W=[den_])
                                k.op(k.dve, lambda: nc.vector.tensor_tensor(
                                    out=o_[:], in0=pO[0:64, :], in1=den_[:], op=ALU.mult), R=[pO, den_], W=[o_])
                                k.op(k.pool, lambda: nc.gpsimd.tensor_tensor(
                                    out=mt_[:, kvh * 4:(kvh + 1) * 4, jq * 128:(jq + 1) * 128], in0=sview(o_, 0, [(128, 4), (1, 128)], nparts=64),
                                    in1=ag_[:, kvh * 4:(kvh + 1) * 4, jq * 128:(jq + 1) * 128], op=ALU.mult), R=[o_, ag_], W=[mt_])
                            pipe.push(stageN)
                    pipe.flush()
                    k.dma(k.pool, mv[:, :, t0:t0 + 512], mt_[:], [mt_], [self.MT[sn]])
            k.barrier()

    def phase_filter(self, li):
        k, nc = self.k, self.nc
        PI = float(np.pi)
        with contextlib.ExitStack() as es:
            def ldw(name, shape, src_ap):
                st = k.sb(es, name + "_f", shape, F32)
                k.dma(k.sp, st[:], src_ap, [self.w[name]], [st], slow=True)
                return st
            w1f = ldw("a_filt_w1", [33, 64], self.w["a_filt_w1"].t[li])
            w2f = ldw("a_filt_w2", [64, 64], self.w["a_filt_w2"].t[li])
            w3f = ldw("a_filt_w3", [64, 4 * HC], self.w["a_filt_w3"].t[li])
            w1 = k.sb(es, "w1b", [33, 64], BF16)
            w2 = k.sb(es, "w2b", [64, 64], BF16)
            w3 = k.sb(es, "w3b", [64, 4 * HC], BF16)
            k.op(k.dve, lambda: nc.vector.tensor_copy(out=w1[:], in_=w1f[:]), R=[w1f], W=[w1])
            k.op(k.dve, lambda: nc.vector.tensor_copy(out=w2[:], in_=w2f[:]), R=[w2f], W=[w2])
            k.op(k.act, lambda: nc.scalar.copy(out=w3[:], in_=w3f[:]), R=[w3f], W=[w3])
            fb = []
            for (fn, bn) in (("a_filt_f1", "a_filt_b1"), ("a_filt_f2", "a_filt_b2")):
                f_ = k.sb(es, fn, [64, 1], F32)
                b_ = k.sb(es, bn, [64, 1], F32)
                k.dma(k.sp, f_[:], dap(self.w[fn], li * 64, [[1, 64], [1, 1]]), [self.w[fn]], [f_], slow=True)
                k.dma(k.sp, b_[:], dap(self.w[bn], li * 64, [[1, 64], [1, 1]]), [self.w[bn]], [b_], slow=True)
                k.op(k.dve, lambda f_=f_, b_=b_: nc.vector.tensor_tensor(out=b_[:], in0=b_[:], in1=f_[:], op=ALU.mult), R=[f_, b_], W=[b_])
                fb.append((f_, b_))
            ndl = self.load_const(es, "negdelta", [128, 4], F32)
            fs = [k.sb(es, "fs%d" % i, [33, 512], F32) for i in range(2)]
            fsb = [k.sb(es, "fsb%d" % i, [33, 512], BF16) for i in range(2)]
            tr_ = [k.sb(es, "trw%d" % i, [128, 512], F32) for i in range(2)]
            pre = k.sb(es, "pre", [64, 512], F32)
            m1 = k.sb(es, "m1", [64, 512], F32)
            hb = [k.sb(es, "hb%d" % i, [64, 512], BF16) for i in range(2)]
            win = [k.sb(es, "win%d" % i, [128, 512], F32) for i in range(2)]
            ao = [k.sb(es, "ao%d" % i, [128, 512], BF16) for i in range(4)]
            dx = k.sb(es, "dxs", [128, 1], F32)
            psh = [k.ps(es, "psh%d" % i, [128, 512], F32) for i in range(2)]
            ps3 = [k.ps(es, "ps3%d" % i, [128, 512], F32) for i in range(4)]
            it = 0
            a3 = 0

            def sin_layer(p_, f_, b_, out_bf):
                k.op(k.dve, lambda: nc.vector.tensor_scalar(out=pre[:], in0=p_[0:64, :], scalar1=f_[:, 0:1], scalar2=b_[:, 0:1],
                                                            op0=ALU.mult, op1=ALU.add), R=[p_, f_, b_], W=[pre])
                k.op(k.dve, lambda: nc.vector.tensor_scalar(out=m1[:], in0=pre[:], scalar1=PI, scalar2=-2 * PI, op0=ALU.is_gt, op1=ALU.mult),
                     R=[pre], W=[m1])
                k.op(k.dve, lambda: nc.vector.tensor_tensor(out=pre[:], in0=pre[:], in1=m1[:], op=ALU.add), R=[pre, m1], W=[pre])
                k.op(k.dve, lambda: nc.vector.tensor_scalar(out=m1[:], in0=pre[:], scalar1=-PI, scalar2=2 * PI, op0=ALU.is_lt, op1=ALU.mult),
                     R=[pre], W=[m1])
                k.op(k.dve, lambda: nc.vector.tensor_tensor(out=pre[:], in0=pre[:], in1=m1[:], op=ALU.add), R=[pre, m1], W=[pre])
                k.op(k.act, lambda: nc.scalar.activation(out=out_bf[:], in_=pre[:], func=AF.Sin), R=[pre], W=[out_bf])

            for L in sorted(set([self.Lp, self.Ls])):
                A, Dx = self.A[L], self.Dx[L]
                for dr in range(2):
                    fT = self.c["featsT_%d" % L] if dr == 0 else self.c["featsTr_%d" % L]
                    for c in range(L // 512):
                        b = it % 2
                        it += 1
                        f_s, f_b, t_ = fs[b], fsb[b], tr_[b]
                        k.dma(k.sp, f_s[:], fT.t[:, c * 512:(c + 1) * 512], [fT], [f_s])
                        k.dma(k.sp, t_[:], dap(self.c["trow_%d" % L], dr * L + c * 512, [[0, 128], [1, 512]]), [self.c["trow_%d" % L]], [t_])
                        k.op(k.act, lambda: nc.scalar.copy(out=f_b[:], in_=f_s[:]), R=[f_s], W=[f_b])
                        p_ = psh[0]
                        k.mm([dict(out=p_[0:64, :], lhsT=w1[:], rhs=f_b[:], start=True, stop=True)], R=[w1, f_b], W=[p_])
                        sin_layer(p_, fb[0][0], fb[0][1], hb[0])
                        p_ = psh[1]
                        k.mm([dict(out=p_[0:64, :], lhsT=w2[:], rhs=hb[0][:], start=True, stop=True)], R=[w2, hb[0]], W=[p_])
                        sin_layer(p_, fb[1][0], fb[1][1], hb[1])
                        for cb in range(4):
                            w_ = win[cb % 2]
                            k.op(k.act, lambda w_=w_, cb=cb: nc.scalar.activation(out=w_[:], in_=t_[:], func=AF.Exp, scale=ndl[:, cb:cb + 1]),
                                 R=[t_, ndl], W=[w_])
                            for o in range(2):
                                p3 = ps3[a3 % 4]
                                a_ = ao[a3 % 4]
                                a3 += 1
                                col = o * 1024 + dr * 512 + cb * 128
                                k.mm([dict(out=p3[:], lhsT=w3[:, col:col + 128], rhs=hb[1][:], start=True, stop=True)], R=[w3, hb[1]], W=[p3])
                                k.op(k.dve, lambda p3=p3, a_=a_, w_=w_: nc.vector.scalar_tensor_tensor(
                                    out=a_[:], in0=w_[:], scalar=0.05, in1=p3[:], op0=ALU.add, op1=ALU.mult), R=[p3, w_], W=[a_])
                                row0 = o * HC + cb * 128
                                if dr == 0:
                                    k.dma(k.pool, dap(A, row0 * 2 * L + (L - 1) + c * 512, [[2 * L, 128], [1, 512]]), a_[:], [a_], [A])
                                else:
                                    last = (c == L // 512 - 1)
                                    n = 511 if last else 512
                                    k.dma(k.pool, dap(A, row0 * 2 * L + c * 512, [[2 * L, 128], [1, n]]), a_[:, 0:n], [a_], [A])
                                    if last:
                                        k.op(k.dve, lambda p3=p3, w_=w_: nc.vector.scalar_tensor_tensor(
                                            out=dx[:], in0=w_[:, 511:512], scalar=0.05, in1=p3[:, 511:512], op0=ALU.add, op1=ALU.mult),
                                            R=[p3, w_], W=[dx])
                                        k.dma(k.pool, dap(Dx, row0, [[1, 128], [1, 1]]), dx[:], [dx], [Dx], slow=True)
            k.barrier()

    def phase_hyena(self, li):
        k, nc = self.k, self.nc
        CH = 32
        groups = {}
        for (sn, L) in self.seqs:
            groups.setdefault(L, []).append(sn)
        NBM = max(len(v) * (L // 128) for L, v in groups.items())
        GP = self.GP
        with contextlib.ExitStack() as es:
            anti = self.load_const(es, "antiid", [128, 128], BF16)
            zero = k.sb(es, "zero", [128, 128], BF16)
            k.op(k.dve, lambda: nc.vector.memset(zero[:], 0.0), W=[zero])
            x1 = k.sb(es, "x1s", [128, NBM * CH], F32)
            x2 = k.sb(es, "x2s", [128, NBM * CH], F32)
            z0 = k.sb(es, "z0s", [128, NBM * CH], F32)
            z1 = k.sb(es, "z1s", [128, NBM * CH], F32)
            hg = k.sb(es, "hgs", [128, NBM * CH], F32)
            zb = k.sb(es, "zbs", [128, NBM * CH], BF16)
            zr = k.sb(es, "zrs", [128, NBM * CH], BF16)
            ho = k.sb(es, "hos", [128, NBM * CH], BF16)
            tmp = [k.sb(es, "tmpc%d" % i, [128, NBM], F32) for i in range(2)]
            G = [k.sb(es, "G%d" % i, [128, GP], BF16) for i in range(3)]
            deff = k.sb(es, "deff", [128, 2, HC], F32)
            dxb = k.sb(es, "dxb", [128, 2, HC], F32)
            psr = [k.ps(es, "psr%d" % i, [128, 512], F32) for i in range(2)]
            psc = [k.ps(es, "psc%d" % i, [128, 512], F32) for i in range(4)]
            gi = 0
            pc_i = 0
            for L, sns in groups.items():
                nb = L // 128
                ns_ = len(sns)
                tot = ns_ * nb * CH
                A, Dx = self.A[L], self.Dx[L]
                ncol = 2 * L - 128
                npieces = (ncol + GP - 1) // GP

                def v4(res, cc, j0=0, j1=nb):
                    return sview(res, j0 * ns_ * CH + cc, [(CH, (j1 - j0) * ns_)])
                k.dma(k.sp, deff[:], dap(self.w["a_hyena_d"], li * 2 * HC, [[0, 128], [1, 2 * HC]]), [self.w["a_hyena_d"]], [deff])
                k.dma(k.sp, dxb[:], dap(Dx, 0, [[0, 128], [1, 2 * HC]]), [Dx], [dxb])
                k.op(k.dve, lambda: nc.vector.tensor_tensor(out=deff[:], in0=deff[:], in1=dxb[:], op=ALU.add), R=[deff, dxb], W=[deff])
                for c0 in range(0, HC, CH):
                    for si_, sn in enumerate(sns):
                        hyv = self.HY[sn].t.rearrange("(b p) c -> p b c", p=128)
                        hgv = self.HG[sn].t.rearrange("(b p) c -> p b c", p=128)
                        for b0 in range(0, nb, 16):
                            b1 = min(nb, b0 + 16)
                            o_ = (b0 * ns_ + si_) * CH
                            dst = lambda r: sview(r, o_, [(ns_ * CH, b1 - b0), (1, CH)])
                            k.dma(k.sp, dst(x1), hyv[:, b0:b1, c0:c0 + CH], [self.HY[sn]], [x1])
                            k.dma(k.sp, dst(x2), hyv[:, b0:b1, HC + c0:HC + c0 + CH], [self.HY[sn]], [x2])
                            k.dma(k.sp, dst(z0), hyv[:, b0:b1, 2 * HC + c0:2 * HC + c0 + CH], [self.HY[sn]], [z0])
                            k.dma(k.sp, dst(hg), hgv[:, b0:b1, c0:c0 + CH], [self.HG[sn]], [hg])
                    for o in range(2):
                        zin = z0 if o == 0 else z1
                        k.op(k.act, lambda zin=zin: nc.scalar.copy(out=zb[:, 0:tot], in_=zin[:, 0:tot]), R=[zin], W=[zb])
                        for f0 in range(0, tot, 512):
                            n = min(512, tot - f0)
                            pr = psr[(f0 // 512) % 2]
                            k.mm([dict(out=pr[:, 0:n], lhsT=anti[:], rhs=zb[:, f0:f0 + n], start=True, stop=True)], R=[anti, zb], W=[pr])
                            k.op(k.dve, lambda pr=pr, f0=f0, n=n: nc.vector.tensor_copy(out=zr[:, f0:f0 + n], in_=pr[:, 0:n]),
                                 R=[pr], W=[zr])
                        for cc in range(CH):
                            ch = c0 + cc
                            pcs = psc[pc_i % 4]
                            pc_i += 1

                            def pv(i0=0, n=nb):
                                return pcs[:, i0 * ns_:(i0 + n) * ns_]
                            k.mm([dict(out=pv(), lhsT=zero[:], rhs=v4(zr, cc), start=True, stop=False)], R=[zero, zr], W=[pcs])
                            for pc in range(npieces):
                                g_ = G[gi % 3]
                                gi += 1
                                w_ = min(GP, ncol - pc * GP)
                                base = (o * HC + ch) * 2 * L + pc * GP
                                k.dma(k.sp, g_[:, 0:w_], dap(A, base, [[1, 128], [1, w_]]), [A], [g_])
                                mms = []
                                for col in range(pc * GP, pc * GP + w_, 128):
                                    Dd = (col - (L - 128)) // 128
                                    if Dd >= 0:
                                        j0, j1, i0 = 0, nb - Dd, Dd
                                    else:
                                        j0, j1, i0 = -Dd, nb, 0
                                    mms.append(dict(out=pv(i0, j1 - j0), lhsT=g_[:, col - pc * GP:col - pc * GP + 128],
                                                    rhs=v4(zr, cc, j0, j1), start=False, stop=(col + 128 >= ncol)))
                                k.mm(mms, R=[g_, zr], W=[pcs])
                            t_ = tmp[cc % 2]
                            tv = t_[:, 0:nb * ns_]
                            k.op(k.dve, lambda tv=tv, zin=zin, cc=cc, o=o, ch=ch, pcs=pcs: nc.vector.scalar_tensor_tensor(
                                out=tv, in0=v4(zin, cc), scalar=deff[:, o, ch:ch + 1], in1=pcs[:, 0:nb * ns_],
                                op0=ALU.mult, op1=ALU.add), R=[zin, deff, pcs], W=[t_])
                            if o == 0:
                                k.op(k.dve, lambda tv=tv, cc=cc: nc.vector.tensor_tensor(
                                    out=v4(z1, cc), in0=tv, in1=v4(x1, cc), op=ALU.mult), R=[t_, x1], W=[z1])
                            else:
                                k.op(k.pool, lambda tv=tv, cc=cc: nc.gpsimd.tensor_tensor(
                                    out=tv, in0=tv, in1=v4(x2, cc), op=ALU.mult), R=[t_, x2], W=[t_])
                                k.op(k.pool, lambda tv=tv, cc=cc: nc.gpsimd.tensor_tensor(
                                    out=v4(ho, cc), in0=tv, in1=v4(hg, cc), op=ALU.mult), R=[t_, hg], W=[ho])
                    for si_, sn in enumerate(sns):
                        hov = self.HYO[sn].t.rearrange("(b p) c -> p b c", p=128)
                        for b0 in range(0, nb, 16):
                            b1 = min(nb, b0 + 16)
                            o_ = (b0 * ns_ + si_) * CH
                            k.dma(k.pool, hov[:, b0:b1, c0:c0 + CH], sview(ho, o_, [(ns_ * CH, b1 - b0), (1, CH)]), [ho], [self.HYO[sn]])
            k.barrier()

    def phase_hy_transpose(self):
        k, nc = self.k, self.nc
        with contextlib.ExitStack() as es:
            ident = self.load_const(es, "ident", [128, 128], BF16)
            hi = [k.sb(es, "hi%d" % i, [128, 4, HC], BF16) for i in range(2)]
            mo = [k.sb(es, "mo%d" % i, [128, 4, 512], BF16) for i in range(2)]
            pst = [k.ps(es, "ptt%d" % i, [128, 8, 128], BF16) for i in range(4)]
            it = 0
            for (sn, L) in self.seqs:
                hv = self.HYO[sn].t.rearrange("(c j p) d -> c p j d", j=4, p=128)
                mv = self.MT[sn].t[0:512, :].rearrange("(b p) t -> p b t", p=128)
                for c in range(L // 512):
                    b = it % 2
                    it += 1
                    hi_, mo_ = hi[b], mo[b]
                    k.dma(k.sp, hi_[:], hv[c], [self.HYO[sn]], [hi_])
                    for j in range(4):
                        pt = pst[(it * 4 + j) % 4]
                        k.tr([(pt[:, blk, :], hi_[:, j, blk * 128:(blk + 1) * 128], ident[:]) for blk in range(4)], R=[hi_, ident], W=[pt])
                        if j % 2:
                            k.op(k.act, lambda pt=pt, j=j: nc.scalar.copy(out=mo_[:, :, j * 128:(j + 1) * 128], in_=pt[:, 0:4, :]), R=[pt], W=[mo_])
                        else:
                            k.op(k.dve, lambda pt=pt, j=j: nc.vector.tensor_copy(out=mo_[:, :, j * 128:(j + 1) * 128], in_=pt[:, 0:4, :]), R=[pt], W=[mo_])
                    k.dma(k.pool, mv[:, :, c * 512:(c + 1) * 512], mo_[:], [mo_], [self.MT[sn]])
            k.barrier()

    def phase_init_pads(self):
        k, nc = self.k, self.nc
        with contextlib.ExitStack() as es:
            zt = k.sb(es, "zt", [128, 8, 1024], BF16)
            k.op(k.dve, lambda: nc.vector.memset(zt[:], 0.0), W=[zt])
            for (sn, L) in self.seqs:
                for g, dl in enumerate((1, 4, 16)):
                    pad = 64 * dl
                    kt = self.KTc[sn, g]
                    kv = kt.t.rearrange("(h p) t -> p h t", p=128)
                    k.dma(k.pool, kv[:, :, 0:pad], zt[:, :, 0:pad], [zt], [kt])
                    k.dma(k.pool, kv[:, :, pad + L:pad + L + pad], zt[:, :, 0:pad], [zt], [kt])
                    vt = self.Vc[sn, g]
                    for r0 in (0, pad + L):
                        if pad >= 128:
                            k.dma(k.pool, vt.t[r0:r0 + pad, :].rearrange("(b p) c -> p b c", p=128), zt[:, 0:pad // 128, :], [zt], [vt])
                        else:
                            k.dma(k.pool, vt.t[r0:r0 + pad, :], zt[0:pad, 0, :], [zt], [vt])
            k.barrier()

    def phase_odd_proj(self, li, g):
        k, nc = self.k, self.nc
        W = self.w["c_w_in"]
        dl = (1, 4, 16)[g]
        pad = 64 * dl
        ncol = 3072 + (1024 if g == 0 else 0)
        with contextlib.ExitStack() as es:
            ident = self.load_const(es, "ident", [128, 128], BF16)
            Wsb = k.sb(es, "Wc", [128, 8, ncol], BF16)
            with contextlib.ExitStack() as es2:
                gsb = k.sb(es2, "gsb", [128, 8, 1], F32)
                k.dma(k.sp, gsb[:], dap(self.w["c_norm"], li * D, [[1, 128], [128, 8], [1, 1]]), [self.w["c_norm"]], [gsb], slow=True)
                stage = [k.sb(es2, "stage%d" % i, [128, 8, 256], F32) for i in range(2)]
                wv = W.t[li].rearrange("(k p) c -> p k c", p=128)
                for cb in range(ncol // 256):
                    c0 = cb * 256
                    src0 = g * 3072 + c0 if c0 < 3072 else 9216 + (c0 - 3072)
                    st = stage[cb % 2]
                    k.dma(k.sp, st[:], wv[:, :, src0:src0 + 256], [W], [st])
                    k.op(k.dve, lambda st=st, c0=c0: nc.vector.tensor_tensor(
                        out=Wsb[:, :, c0:c0 + 256], in0=st[:], in1=sview(gsb, 0, [(1, 8), (0, 256)]), op=ALU.mult), R=[st, gsb], W=[Wsb])
                k.barrier()
            xc = [k.sb(es, "xc%d" % i, [128, 8, 512], BF16) for i in range(2)]
            qbf = [k.sb(es, "qbf%d" % i, [128, 512], BF16) for i in range(4)]
            qTs = [k.sb(es, "qTs%d" % i, [128, 8, 512], BF16) for i in range(2)]
            kTs = [k.sb(es, "kTs%d" % i, [128, 8, 512], BF16) for i in range(2)]
            vo = [k.sb(es, "vo%d" % i, [128, 4, 1024], BF16) for i in range(2)]
            agT = [k.sb(es, "agT%d" % i, [128, 8, 512], BF16) for i in range(2)]
            cs = [k.sb(es, "cs%d" % i, [128, 4, 64], F32) for i in range(2)]
            tmpa = [k.sb(es, "tmpa%d" % i, [128, 128], F32) for i in range(4)]
            tmpb = [k.sb(es, "tmpb%d" % i, [128, 128], F32) for i in range(4)]
            psA = [k.ps(es, "psA%d" % i, [128, 512], F32) for i in range(5)]
            psT = [k.ps(es, "psT%d" % i, [128, 8, 128], BF16) for i in range(3)]
            pipe = Pipe(2)
            pa = [0]

            def nps():
                pa[0] += 1
                return psA[pa[0] % 5]
            it = 0
            tt = 0
            for (sn, L) in self.seqs:
                xt = self.xnT[sn]
                xt_v = xt.t.rearrange("k p t -> p k t")
                rope_v = self.c["rope16"].t.rearrange("(c j p) r -> c p j r", j=4, p=128)
                for c in range(L // 512):
                    b = it % 2
                    it += 1
                    x_, cs_ = xc[b], cs[b]
                    k.dma(k.sp, x_[:], xt_v[:, :, 2 + c * 512:2 + (c + 1) * 512], [xt], [x_])
                    k.dma(k.sp, cs_[:], rope_v[c], [self.c["rope16"]], [cs_])
                    qT_, kT_, vo_, ag_ = qTs[b], kTs[b], vo[b], agT[b]
                    for j in range(4):
                        for tq, dstT in ((0, qT_), (1, kT_)):
                            for hg_ in range(2):
                                tb_ = tt % 4
                                t3_ = tt % 3
                                tt += 1
                                p_ = nps()
                                c0 = tq * 1024 + hg_ * 512
                                k.mm([dict(out=p_[:], lhsT=x_[:, kk, j * 128:(j + 1) * 128], rhs=Wsb[:, kk, c0:c0 + 512],
                                           start=(kk == 0), stop=(kk == 7)) for kk in range(8)], R=[x_, Wsb], W=[p_])
                                q_ = qbf[tb_]
                                cc = sview(cs_, j * 64, [(0, 4), (1, 32)])
                                s1 = sview(cs_, j * 64 + 32, [(0, 4), (1, 16)])
                                s2 = sview(cs_, j * 64 + 48, [(0, 4), (1, 16)])
                                self.rope_tm(p_, 4, 128, 32, cc, (s1, s2), q_, 0, tmpa[tb_], tmpb[tb_], extra_R=[cs_])
                                pt = psT[t3_]

                                def stageB(pt=pt, q_=q_, j=j, hg_=hg_, dstT=dstT):
                                    k.tr([(pt[:, blk, :], q_[:, blk * 128:(blk + 1) * 128], ident[:]) for blk in range(4)], R=[q_, ident], W=[pt])
                                    k.op(k.act if hg_ else k.dve, lambda: (nc.scalar.copy if hg_ else nc.vector.tensor_copy)(
                                        out=dstT[:, hg_ * 4:(hg_ + 1) * 4, j * 128:(j + 1) * 128], in_=pt[:, 0:4, :]), R=[pt], W=[dstT])
                                pipe.push(stageB)
                        for vg in range(2):
                            p_ = nps()
                            c0 = 2048 + vg * 512
                            k.mm([dict(out=p_[:], lhsT=x_[:, kk, j * 128:(j + 1) * 128], rhs=Wsb[:, kk, c0:c0 + 512],
                                       start=(kk == 0), stop=(kk == 7)) for kk in range(8)], R=[x_, Wsb], W=[p_])
                            k.op(k.act, lambda p_=p_, j=j, vg=vg: nc.scalar.copy(out=vo_[:, j, vg * 512:(vg + 1) * 512], in_=p_[:]), R=[p_], W=[vo_])
                    pipe.flush()
                    if g == 0:
                        for cb in range(8):
                            p_ = nps()
                            k.mm([dict(out=p_[:], lhsT=Wsb[:, kk, 3072 + cb * 128:3072 + (cb + 1) * 128], rhs=x_[:, kk, :],
                                       start=(kk == 0), stop=(kk == 7)) for kk in range(8)], R=[x_, Wsb], W=[p_])
                            k.op(k.act, lambda p_=p_, cb=cb: nc.scalar.activation(out=ag_[:, cb, :], in_=p_[:], func=AF.Silu), R=[p_], W=[ag_])
                        k.dma(k.pool, self.CGT[sn].t.rearrange("(b p) t -> p b t", p=128)[:, :, c * 512:(c + 1) * 512], ag_[:], [ag_], [self.CGT[sn]])
                    sl = slice(c * 512, (c + 1) * 512)
                    k.dma(k.pool, self.QTc[sn, g].t.rearrange("(b p) t -> p b t", p=128)[:, :, sl], qT_[:], [qT_], [self.QTc[sn, g]])
                    k.dma(k.pool, self.KTc[sn, g].t.rearrange("(b p) t -> p b t", p=128)[:, :, pad + c * 512:pad + (c + 1) * 512], kT_[:], [kT_],
                          [self.KTc[sn, g]])
                    k.dma(k.pool, self.Vc[sn, g].t[pad + c * 512:pad + (c + 1) * 512, :].rearrange("(j p) d -> p j d", p=128), vo_[:], [vo_],
                          [self.Vc[sn, g]])
            k.barrier()

    def phase_dilated_attn(self):
        k, nc = self.k, self.nc
        CT = 2048
        SC = float(128 ** -0.5)
        with contextlib.ExitStack() as es:
            masks = k.sb(es, "masks", [128, 2, 128], BF16)
            k.dma(k.sp, masks[:, 0, :], self.c["maskA"].t[:, :], [self.c["maskA"]], [masks])
            k.dma(k.sp, masks[:, 1, :], self.c["maskB"].t[:, :], [self.c["maskB"]], [masks])
            ones = self.load_const(es, "ones", [128, 128], BF16)
            vlo = self.load_const(es, "vlo", [128, 128], BF16)
            vhi = self.load_const(es, "vhi", [128, 128], BF16)
            q_sb = [k.sb(es, "dq%d" % i, [128, CT], BF16) for i in range(2)]
            k_sb = [k.sb(es, "dk%d" % i, [128, CT + 2048], BF16) for i in range(2)]
            v_sb = [k.sb(es, "dv%d" % i, [128, 32, 128], BF16) for i in range(2)]
            g_sb = [k.sb(es, "dg%d" % i, [128, CT], BF16) for i in range(2)]
            accn = [k.sb(es, "accn%d" % i, [128, CT], F32) for i in range(2)]
            accd = [k.sb(es, "accd%d" % i, [128, CT], F32) for i in range(2)]
            mo = [k.sb(es, "dmo%d" % i, [128, CT], BF16) for i in range(2)]
            pT = [k.sb(es, "dpT%d" % i, [128, 512], BF16) for i in range(4)]
            psS = [k.ps(es, "dpsS%d" % i, [128, 512], F32) for i in range(3)]
            psO = [k.ps(es, "dpsO%d" % i, [128, 512], F32) for i in range(3)]
            hi = 0
            li_ = 0
            si = 0
            pipe = Pipe(2)
            for (sn, L) in self.seqs:
                for t0 in range(0, L, CT):
                    for h in range(8):
                        an, ad, g_, mo_ = accn[hi % 2], accd[hi % 2], g_sb[hi % 2], mo[hi % 2]
                        hi += 1
                        k.dma(k.sp, g_[:], self.CGT[sn].t[h * 128:(h + 1) * 128, t0:t0 + CT], [self.CGT[sn]], [g_])
                        for g, dl in enumerate((1, 4, 16)):
                            pad = 64 * dl
                            Ls_ = L // dl
                            q_, k_, v_ = q_sb[li_ % 2], k_sb[li_ % 2], v_sb[li_ % 2]
                            li_ += 1
                            k.dma(k.sp, q_[:], self.QTc[sn, g].t[h * 128:(h + 1) * 128, t0:t0 + CT], [self.QTc[sn, g]], [q_])
                            k.dma(k.sp, k_[:, 0:CT + 2 * pad], self.KTc[sn, g].t[h * 128:(h + 1) * 128, t0:t0 + CT + 2 * pad],
                                  [self.KTc[sn, g]], [k_])
                            nblk = CT // (128 * dl)
                            nm = nblk + 1
                            V = self.Vc[sn, g]
                            for r in range(dl):
                                k.dma(k.sp, v_[:, r * nm:(r + 1) * nm, :],
                                      dap(V, (t0 + r) * D + h * 128, [[dl * D, 128], [128 * dl * D, nm], [1, 128]]), [V], [v_])
                            if dl == 16:
                                pairs = [((r, 0), (r + 1, 0)) for r in range(0, 16, 2)]
                            else:
                                pairs = [((r, b), (r, b + 1)) for r in range(dl) for b in range(0, nblk, 2)]
                            for pr in pairs:
                                pS, pT_ = psS[si % 3], pT[si % 4]
                                pO = psO[si % 3]
                                si += 1
                                mms = []
                                for ti, (r, b) in enumerate(pr):
                                    qcol = 128 * b * dl + r
                                    for kt in range(2):
                                        kcol = 128 * (b + kt) * dl + r
                                        mms.append(dict(out=pS[:, (ti * 2 + kt) * 128:(ti * 2 + kt + 1) * 128],
                                                        lhsT=sview(k_, kcol, [(dl, 128)]), rhs=sview(q_, qcol, [(dl, 128)]),
                                                        start=True, stop=True))
                                k.mm(mms, R=[k_, q_], W=[pS])
                                k.op(k.act, lambda pS=pS, pT_=pT_: nc.scalar.activation(out=pT_[:], in_=pS[:], func=AF.Exp, scale=SC),
                                     R=[pS], W=[pT_])
                                k.op(k.dve, lambda pT_=pT_: nc.vector.tensor_tensor(
                                    out=sview(pT_, 0, [(256, 2), (1, 256)]), in0=sview(pT_, 0, [(256, 2), (1, 256)]),
                                    in1=sview(masks, 0, [(0, 2), (1, 256)]), op=ALU.mult), R=[pT_, masks], W=[pT_])
                                def stageB(pr=pr, pO=pO, pT_=pT_, v_=v_, nm=nm, dl=dl, g=g, t0=t0, Ls_=Ls_, an=an, ad=ad):
                                    mms = []
                                    for ti, (r, b) in enumerate(pr):
                                        for kt in range(2):
                                            mms.append(dict(out=pO[:, ti * 128:(ti + 1) * 128], lhsT=v_[:, r * nm + b + kt, :],
                                                            rhs=pT_[:, (ti * 2 + kt) * 128:(ti * 2 + kt + 1) * 128], start=(kt == 0), stop=(kt == 1)))
                                    for ti, (r, b) in enumerate(pr):
                                        for kt in range(2):
                                            s_lo = (t0 // dl) + 128 * (b + kt) - 64
                                            vl = vlo if s_lo < 0 else (vhi if s_lo + 128 > Ls_ else ones)
                                            mms.append(dict(out=pO[:, 256 + ti * 128:256 + (ti + 1) * 128], lhsT=vl[:],
                                                            rhs=pT_[:, (ti * 2 + kt) * 128:(ti * 2 + kt + 1) * 128], start=(kt == 0), stop=(kt == 1)))
                                    k.mm(mms, R=[v_, pT_, ones, vlo, vhi], W=[pO])
                                    (r0, b0), (r1, b1) = pr
                                    c0 = 128 * b0 * dl + r0
                                    step = (128 * b1 * dl + r1) - c0
                                    av_n = sview(an, c0, [(step, 2), (dl, 128)])
                                    av_d = sview(ad, c0, [(step, 2), (dl, 128)])
                                    pn = sview(pO, 0, [(128, 2), (1, 128)])
                                    pd = sview(pO, 256, [(128, 2), (1, 128)])
                                    if g == 0:
                                        k.op(k.dve, lambda av_n=av_n, pn=pn: nc.vector.tensor_copy(out=av_n, in_=pn), R=[pO], W=[an])
                                        k.op(k.act, lambda av_d=av_d, pd=pd: nc.scalar.copy(out=av_d, in_=pd), R=[pO], W=[ad])
                                    else:
                                        k.op(k.dve, lambda av_n=av_n, pn=pn: nc.vector.tensor_tensor(out=av_n, in0=av_n, in1=pn, op=ALU.add),
                                             R=[pO, an], W=[an])
                                        k.op(k.dve, lambda av_d=av_d, pd=pd: nc.vector.tensor_tensor(out=av_d, in0=av_d, in1=pd, op=ALU.add),
                                             R=[pO, ad], W=[ad])
                                pipe.push(stageB)
                        pipe.flush()
                        k.op(k.dve, lambda ad=ad: nc.vector.reciprocal(out=ad[:], in_=ad[:]), R=[ad], W=[ad])
                        k.op(k.pool, lambda an=an, ad=ad: nc.gpsimd.tensor_tensor(out=an[:], in0=an[:], in1=ad[:], op=ALU.mult), R=[an, ad], W=[an])
                        k.op(k.pool, lambda an=an, g_=g_, mo_=mo_: nc.gpsimd.tensor_tensor(out=mo_[:], in0=an[:], in1=g_[:], op=ALU.mult),
                             R=[an, g_], W=[mo_])
                        k.dma(k.pool, self.MT[sn].t[h * 128:(h + 1) * 128, t0:t0 + CT], mo_[:], [mo_], [self.MT[sn]])
            k.barrier()

    def build_all(self):
        self.phase_init_pads()
        cur = self.x_in
        bufs = [self.xa, self.xb]
        for layer in range(self.depth):
            nxt = bufs[layer % 2]
            li = layer // 2
            self.phase_norm(cur)
            if layer % 2 == 0:
                self.phase_even_proj(li)
                self.phase_filter(li)
                self.phase_hyena(li)
                self.phase_hy_transpose()
                self.phase_band_attn(li)
                self.phase_outproj("a_w_out", li, cur, nxt)
            else:
                for g in range(3):
                    self.phase_odd_proj(li, g)
                self.phase_dilated_attn()
                self.phase_outproj("c_w_out", li, cur, nxt)
            cur = nxt
        self.phase_norm(cur, final=True)
        self.finish()


_PROG_CACHE = {}


def kernel(**inputs):
    x_prompt = np.asarray(inputs["x_prompt"], dtype=np.float32)
    x_sample = np.asarray(inputs["x_sample"], dtype=np.float32)
    Lp, Ls = x_prompt.shape[1], x_sample.shape[1]
    nsamp = x_sample.shape[0]
    ns = nsamp // NCORES
    key = (Lp, Ls, ns)
    if key not in _PROG_CACHE:
        P = Prog(Lp, Ls, ns, depth=4)
        P.build_all()
        _PROG_CACHE[key] = P
    P = _PROG_CACHE[key]
    in_maps = [P.in_map(x_prompt[0], x_sample[c * ns:(c + 1) * ns], inputs) for c in range(NCORES)]
    res = run_bass_kernel_spmd(P.nc, in_maps, core_ids=list(range(NCORES)))
    y_prompt = np.asarray(res.results[0]["y_p"], dtype=np.float32)[None]
    y_sample = np.stack([np.asarray(res.results[c]["y_s%d" % i], dtype=np.float32) for c in range(NCORES) for i in range(ns)])
    return (y_prompt, y_sample)
```
